# Optimizing a Trainium2 kernel written in Bass

```python
import jax, jax.numpy as jnp
from jax import lax
import numpy as np

D_MODEL = 2048
BATCH = 8
SEQ = 2048
DEPTH = 2

GRID_W = 64
CTX_LEN = 256
NA_HEADS = 8
HEAD_DIM = 128
NA_WIDTH = NA_HEADS * HEAD_DIM
CONV_CH = D_MODEL - NA_WIDTH
CONV_K = 3
WIN_ROWS = 8
WIN_COLS = 16
EVEN_IN = 3 * NA_WIDTH + 3 * CONV_CH
EVEN_SPLITS = (NA_WIDTH, 2 * NA_WIDTH, 3 * NA_WIDTH, 3 * NA_WIDTH + CONV_CH, 3 * NA_WIDTH + 2 * CONV_CH)
CHUNK = 128
SG_WIDTH = D_MODEL
SG_GROUPS = 16
SG_GROUP_DIM = SG_WIDTH // SG_GROUPS
FFN_HIDDEN = ((8 * D_MODEL // 3 + 255) // 256) * 256
EPS = 1e-6
NEG_INF = -1e30

kernel_name = "hybrid_natten_shortconv_gmlp_dit"


def rmsnorm(x, g):
    xf = x.astype(jnp.float32)
    y = xf * lax.rsqrt(jnp.mean(xf * xf, axis=-1, keepdims=True) + EPS)
    return (y * g.astype(jnp.float32)).astype(x.dtype)


def layernorm(x, g, b):
    xf = x.astype(jnp.float32)
    mu = jnp.mean(xf, axis=-1, keepdims=True)
    var = jnp.mean(jnp.square(xf - mu), axis=-1, keepdims=True)
    y = (xf - mu) * lax.rsqrt(var + EPS)
    return (y * g.astype(jnp.float32) + b.astype(jnp.float32)).astype(x.dtype)


def ada_mods(cond, w, b):
    m = jax.nn.silu(cond) @ w + b
    return jnp.split(m[:, None, :], 6, axis=-1)


def modulate(h, shift, scale):
    return h * (1.0 + scale) + shift


def heads(t):
    b, l, _ = t.shape
    return t.reshape(b, l, NA_HEADS, HEAD_DIM).transpose(0, 2, 1, 3)


def merge_heads(t):
    b, h, l, d = t.shape
    return t.transpose(0, 2, 1, 3).reshape(b, l, h * d)


def neighbourhood_attention(q, k, v, k_ctx, v_ctx, rpb, rows):
    b, h, s, d = q.shape
    kr = min(WIN_ROWS, rows)
    r = jnp.arange(rows)
    row_start = jnp.clip(r - kr // 2, 0, rows - kr)
    row_idx = row_start[:, None] + jnp.arange(kr)[None, :]
    col = jnp.arange(GRID_W)
    col_start = jnp.clip(col - WIN_COLS // 2, 0, GRID_W - WIN_COLS)
    col_mask = (col[None, :] >= col_start[:, None]) & (col[None, :] < col_start[:, None] + WIN_COLS)
    mask = jnp.tile(col_mask, (1, kr))

    qg = q.reshape(b, h, rows, GRID_W, d)
    kg = k.reshape(b, h, rows, GRID_W, d)[:, :, row_idx].reshape(b, h, rows, kr * GRID_W, d)
    vg = v.reshape(b, h, rows, GRID_W, d)[:, :, row_idx].reshape(b, h, rows, kr * GRID_W, d)

    dr = row_idx - r[:, None] + (WIN_ROWS - 1)
    dc = jnp.clip(col[None, :] - col[:, None], -(WIN_COLS - 1), WIN_COLS - 1) + (WIN_COLS - 1)
    bias = rpb[:, dr[:, None, :, None], dc[None, :, None, :]]
    bias = bias.reshape(h, rows, GRID_W, kr * GRID_W).astype(jnp.float32)

    scale = d ** -0.5
    s_loc = jnp.einsum('bhrqd,bhrkd->bhrqk', qg, kg).astype(jnp.float32) * scale + bias
    s_loc = jnp.where(mask, s_loc, NEG_INF)
    s_ctx = jnp.einsum('bhrqd,bhld->bhrql', qg, k_ctx).astype(jnp.float32) * scale
    p = jax.nn.softmax(jnp.concatenate([s_loc, s_ctx], axis=-1), axis=-1).astype(v.dtype)
    p_loc, p_ctx = p[..., :kr * GRID_W], p[..., kr * GRID_W:]
    out = (jnp.einsum('bhrqk,bhrkd->bhrqd', p_loc, vg)
           + jnp.einsum('bhrql,bhld->bhrqd', p_ctx, v_ctx))
    return out.reshape(b, h, s, d)


def context_attention(q, k, v):
    s = jnp.einsum('bhqd,bhkd->bhqk', q, k).astype(jnp.float32) * (q.shape[-1] ** -0.5)
    p = jax.nn.softmax(s, axis=-1).astype(v.dtype)
    return jnp.einsum('bhqk,bhkd->bhqd', p, v)


def short_gated_conv(u, gate_b, gate_c, w_conv):
    z = gate_c * u
    zp = jnp.pad(z, ((0, 0), (1, 1), (0, 0)))
    y = zp[:, :-2] * w_conv[0] + zp[:, 1:-1] * w_conv[1] + zp[:, 2:] * w_conv[2]
    return gate_b * y


def even_mixer(hx, hc, w_in, w_out, rpb, w_conv, rows, ctx_out):
    qx, kx, vx, bx, cx, ux = jnp.split(hx @ w_in, EVEN_SPLITS, axis=-1)
    if ctx_out:
        qc, kc, vc, bc, cc, uc = jnp.split(hc @ w_in, EVEN_SPLITS, axis=-1)
    else:
        kc, vc = jnp.split(hc @ w_in[:, NA_WIDTH:3 * NA_WIDTH], 2, axis=-1)
    kc_h, vc_h = heads(kc), heads(vc)
    attn_x = neighbourhood_attention(heads(qx), heads(kx), heads(vx), kc_h, vc_h, rpb, rows)
    mix_x = jnp.concatenate([merge_heads(attn_x), short_gated_conv(ux, bx, cx, w_conv)], axis=-1) @ w_out
    if not ctx_out:
        return mix_x, None
    attn_c = context_attention(heads(qc), kc_h, vc_h)
    mix_c = jnp.concatenate([merge_heads(attn_c), short_gated_conv(uc, bc, cc, w_conv)], axis=-1) @ w_out
    return mix_x, mix_c


def odd_mixer(h, w_in, w_out, ln_g, ln_b, w_s, b_s):
    z = jax.nn.gelu(h @ w_in)
    u, v = jnp.split(z, 2, axis=-1)
    v = layernorm(v, ln_g, ln_b)
    b, l, _ = v.shape
    n = l // CHUNK
    vg = v.reshape(b, n, CHUNK, SG_GROUPS, SG_GROUP_DIM)
    mixed = jnp.einsum('gts,bnsgc->bntgc', w_s, vg) + b_s.T[None, None, :, :, None]
    return (u * mixed.reshape(b, l, SG_WIDTH)) @ w_out


def swiglu(h, w_gate, w_up, w_down):
    return (jax.nn.silu(h @ w_gate) * (h @ w_up)) @ w_down


def setup_inputs(seed: int = 0) -> dict:
    key = jax.random.key(seed)
    ks = jax.random.split(key, 24)
    n_even = (DEPTH + 1) // 2
    n_odd = DEPTH // 2
    f32 = jnp.float32

    def nrm(k, shape, scale):
        return jax.random.normal(k, shape, f32) * scale

    return {
        "x": nrm(ks[0], (BATCH, SEQ, D_MODEL), 1.0),
        "c": nrm(ks[1], (BATCH, D_MODEL), 1.0),
        "ctx": nrm(ks[2], (BATCH, CTX_LEN, D_MODEL), 1.0),
        "c_ctx": nrm(ks[3], (D_MODEL,), 1.0),
        "ada_w": nrm(ks[4], (DEPTH, D_MODEL, 6 * D_MODEL), 0.5 * D_MODEL ** -0.5),
        "ada_b": nrm(ks[5], (DEPTH, 6 * D_MODEL), 0.02),
        "norm_mix_g": 1.0 + nrm(ks[6], (DEPTH, D_MODEL), 0.02),
        "norm_ffn_g": 1.0 + nrm(ks[7], (DEPTH, D_MODEL), 0.02),
        "ffn_w_gate": nrm(ks[8], (DEPTH, D_MODEL, FFN_HIDDEN), D_MODEL ** -0.5),
        "ffn_w_up": nrm(ks[9], (DEPTH, D_MODEL, FFN_HIDDEN), D_MODEL ** -0.5),
        "ffn_w_down": nrm(ks[10], (DEPTH, FFN_HIDDEN, D_MODEL), FFN_HIDDEN ** -0.5),
        "even_w_in": nrm(ks[11], (n_even, D_MODEL, EVEN_IN), D_MODEL ** -0.5),
        "even_w_out": nrm(ks[12], (n_even, NA_WIDTH + CONV_CH, D_MODEL), (NA_WIDTH + CONV_CH) ** -0.5),
        "rpb": nrm(ks[13], (n_even, NA_HEADS, 2 * WIN_ROWS - 1, 2 * WIN_COLS - 1), 0.1),
        "conv_w": nrm(ks[14], (n_even, CONV_K, CONV_CH), CONV_K ** -0.5),
        "odd_w_in": nrm(ks[15], (n_odd, D_MODEL, 2 * SG_WIDTH), D_MODEL ** -0.5),
        "odd_w_out": nrm(ks[16], (n_odd, SG_WIDTH, D_MODEL), SG_WIDTH ** -0.5),
        "sg_ln_g": 1.0 + nrm(ks[17], (n_odd, SG_WIDTH), 0.02),
        "sg_ln_b": nrm(ks[18], (n_odd, SG_WIDTH), 0.02),
        "sg_w": nrm(ks[19], (n_odd, SG_GROUPS, CHUNK, CHUNK), CHUNK ** -0.5),
        "sg_b": 1.0 + nrm(ks[20], (n_odd, SG_GROUPS, CHUNK), 0.02),
        "final_norm_g": 1.0 + nrm(ks[21], (D_MODEL,), 0.02),
    }


def reference(x, c, ctx, c_ctx, ada_w, ada_b, norm_mix_g, norm_ffn_g, ffn_w_gate, ffn_w_up,
              ffn_w_down, even_w_in, even_w_out, rpb, conv_w, odd_w_in, odd_w_out,
              sg_ln_g, sg_ln_b, sg_w, sg_b, final_norm_g):
    rows = x.shape[1] // GRID_W
    for i in range(DEPTH):
        j = i // 2
        ctx_needed = any(l % 2 == 0 for l in range(i + 1, DEPTH))
        sh_x, sc_x, g_x, shf_x, scf_x, gf_x = ada_mods(c, ada_w[i], ada_b[i])
        sh_c, sc_c, g_c, shf_c, scf_c, gf_c = ada_mods(c_ctx[None], ada_w[i], ada_b[i])

        hx = modulate(rmsnorm(x, norm_mix_g[i]), sh_x, sc_x)
        need_hc = (i % 2 == 0) or ctx_needed
        hc = modulate(rmsnorm(ctx, norm_mix_g[i]), sh_c, sc_c) if need_hc else None

        if i % 2 == 0:
            mix_x, mix_c = even_mixer(hx, hc, even_w_in[j], even_w_out[j], rpb[j], conv_w[j],
                                      rows, ctx_needed)
        else:
            mix_x = odd_mixer(hx, odd_w_in[j], odd_w_out[j], sg_ln_g[j], sg_ln_b[j], sg_w[j], sg_b[j])
            mix_c = (odd_mixer(hc, odd_w_in[j], odd_w_out[j], sg_ln_g[j], sg_ln_b[j], sg_w[j], sg_b[j])
                     if ctx_needed else None)

        x = x + g_x * mix_x
        hx = modulate(rmsnorm(x, norm_ffn_g[i]), shf_x, scf_x)
        x = x + gf_x * swiglu(hx, ffn_w_gate[i], ffn_w_up[i], ffn_w_down[i])

        if ctx_needed:
            ctx = ctx + g_c * mix_c
            hc = modulate(rmsnorm(ctx, norm_ffn_g[i]), shf_c, scf_c)
            ctx = ctx + gf_c * swiglu(hc, ffn_w_gate[i], ffn_w_up[i], ffn_w_down[i])

    return rmsnorm(x, final_norm_g)
```

```python
import math
import os
from contextlib import ExitStack
import numpy as np
import concourse.bass as bass
import concourse.mybir as mybir
from concourse.bass_utils import run_bass_kernel_spmd

F32 = mybir.dt.float32
BF16 = mybir.dt.bfloat16
AF = mybir.ActivationFunctionType
ALU = mybir.AluOpType
KB = 1024


class _Op:
    __slots__ = ("eng", "fn", "deps", "dma", "needed", "sem", "val", "idx")

    def __init__(self, eng, fn, dma):
        self.eng = eng
        self.fn = fn
        self.dma = dma
        self.deps = set()
        self.needed = False
        self.sem = None
        self.val = None


class Sched:
    def __init__(self, nc, stack, n_dma_sems=20):
        self.nc = nc
        self.engs = {"pe": nc.tensor, "act": nc.scalar, "dve": nc.vector,
                     "pool": nc.gpsimd, "sp": nc.sync}
        self.ops = []
        self.lw = {}
        self.rd = {}
        self.stack = stack
        self.n_dma_sems = n_dma_sems
        self.last_eng_op = {}
        self.dmas_since_barrier = []
        self.pending_barrier = {}

    def op(self, eng, fn, reads=(), writes=(), dma=False, exempt=False):
        o = _Op(eng, fn, dma)
        o.idx = len(self.ops)
        for k in reads:
            w = self.lw.get(k)
            if w is not None:
                o.deps.add(w)
        for k in writes:
            w = self.lw.get(k)
            if w is not None:
                o.deps.add(w)
            for r in self.rd.get(k, ()):
                o.deps.add(r)
        if not exempt and eng in self.pending_barrier:
            for d in self.pending_barrier.pop(eng):
                o.deps.add(d)
        o.deps.discard(o)
        for k in reads:
            self.rd.setdefault(k, []).append(o)
        for k in writes:
            self.lw[k] = o
            self.rd[k] = []
        self.ops.append(o)
        if dma:
            self.dmas_since_barrier.append(o)
        else:
            self.last_eng_op[eng] = o
        return o

    def barrier(self):
        tails = list(self.last_eng_op.values()) + list(self.dmas_since_barrier)
        self.dmas_since_barrier = []
        for e in self.engs:
            prev = self.pending_barrier.get(e, [])
            self.pending_barrier[e] = prev + tails

    def emit(self):
        nc = self.nc
        for o in self.ops:
            for d in o.deps:
                if (not d.dma) and d.eng == "pe" and o.eng == "pe" and not o.dma:
                    continue
                d.needed = True
        esem = {e: self.stack.enter_context(nc.semaphore("s_" + e)) for e in self.engs}
        dsem = {e: [self.stack.enter_context(nc.semaphore("d_%s%d" % (e, i)))
                    for i in range(self.n_dma_sems)] for e in ("sp", "pool", "act")}
        dcount = {e: [0] * self.n_dma_sems for e in dsem}
        drr = {e: 0 for e in dsem}
        ecount = {e: 0 for e in self.engs}
        seen = {e: {} for e in self.engs}

        def need(e, sem, val):
            key = id(sem)
            if seen[e].get(key, 0) >= val:
                return
            seen[e][key] = val
            self.engs[e].wait_ge(sem, val)

        for o in self.ops:
            e = o.eng
            if o.dma:
                i = drr[e]
                drr[e] = (i + 1) % self.n_dma_sems
                o.sem = dsem[e][i]
                dcount[e][i] += 16
                o.val = dcount[e][i]
                if o.val > 16:
                    need(e, o.sem, o.val - 16)
            for d in sorted(o.deps, key=lambda x: x.idx):
                if (not d.dma) and d.eng == "pe" and e == "pe" and not o.dma:
                    continue
                need(e, d.sem, d.val)
            ins = o.fn(self.engs[e])
            if ins is None:
                assert not o.needed and not o.dma
                continue
            if o.dma:
                ins.then_inc(o.sem, 16)
            elif o.needed:
                ecount[e] += 1
                o.sem = esem[e]
                o.val = ecount[e]
                ins.then_inc(o.sem, 1)
        self.ecount = ecount


D = 2048
S = 2048
CTX = 256
FF = 5632
NF = FF // 128
SCALE = 128 ** -0.5
EPS = 1e-6

OFF_MISC = 0
OFF_XRES = 8 * KB
OFF_W = 16 * KB
OFF_HX = 40 * KB
OFF_HXC = 104 * KB
OFF_AT = 112 * KB
OFF_SCR = 176 * KB
ARENA_BYTES = 207 * KB
WG_BYTES = 12 * KB


def pairs_of(m):
    if m <= 1:
        return [0, 1, 2, 3]
    if m >= 14:
        return [12, 13, 14, 15]
    return list(range(m - 2, m + 3))


def tile_base(m):
    if m == 0:
        return 5
    if m == 1:
        return 9
    if m == 14:
        return 13
    if m == 15:
        return 17
    return 0


def build(stop=None):
    nc = bass.Bass("TRN2", target_bir_lowering=False)

    def din(name, shape, dt=F32):
        return nc.dram_tensor(name, list(shape), dt, kind="ExternalInput").ap()

    x_d = din("x", [S, D])
    ctx_d = din("ctx", [CTX, D])
    cT_d = din("cT", [128, 16, 2])
    ada_w = din("ada_w", [2, D, 6 * D])
    adabT_d = din("adabT", [128, 2, 96])
    vecs_d = din("vecs", [128, 7, 16])
    convw_d = din("convw", [128, 8, 3])
    w_gate = din("ffn_w_gate", [2, D, FF])
    w_up = din("ffn_w_up", [2, D, FF])
    w_down = din("ffn_w_down", [2, FF, D])
    e_win = din("even_w_in", [D, 6144])
    e_wout = din("even_w_out", [D, D])
    bias_d = din("biastab", [128, 8, 21, 128])
    mask_d = din("masktab", [128, 21, 128])
    o_win = din("odd_w_in", [D, 4096])
    o_wout = din("odd_w_out", [D, D])
    sgwT_d = din("sgwT", [128, 16, 128])
    sgb_d = din("sgb", [16, 128])
    ident_d = din("ident", [128, 128])
    out_d = nc.dram_tensor("out", [S, D], F32, kind="ExternalOutput").ap()
    dbg_d = None
    if stop is not None:
        dbg_d = nc.dram_tensor("dbg", [16, 128, 2304], F32, kind="ExternalOutput").ap()

    xT_d = nc.dram_tensor("xT_scr", [16, 128, S], F32).ap()
    modrow_d = nc.dram_tensor("modrow_scr", [2, 2, 6 * D], F32).ap()
    etab_d = nc.dram_tensor("etab_scr", [8, 128, 21 * 128], BF16).ap()

    with ExitStack() as st:
        sch = Sched(nc, st)
        arena = st.enter_context(nc.sbuf_tensor("arena", [128, ARENA_BYTES // 2], BF16))
        PS = [st.enter_context(nc.psum_tensor("ps%d" % i, [128, 1024], F32)) for i in range(4)]

        def bank(i):
            return PS[i // 2][:, (i % 2) * 512:(i % 2) * 512 + 512]

        def T(off, dt, *shape):
            n = 1
            for s_ in shape:
                n *= s_
            esz = 2 if dt == BF16 else 4
            a = arena[:, off // 2: off // 2 + n * esz // 2]
            if dt != BF16:
                a = a.bitcast(dt)
            if len(shape) == 2:
                a = a.rearrange("p (a b) -> p a b", a=shape[0])
            elif len(shape) == 3:
                a = a.rearrange("p (a b c) -> p a b c", a=shape[0], b=shape[1])
            return a

        class Bump:
            def __init__(self, base, size):
                self.base = base
                self.size = size
                self.cur = base

            def reset(self):
                self.cur = self.base

            def alloc(self, dt, *shape):
                n = 1
                for s_ in shape:
                    n *= s_
                nb = n * (2 if dt == BF16 else 4)
                nb = (nb + 31) // 32 * 32
                off = self.cur
                self.cur += nb
                assert self.cur <= self.base + self.size, ("bump overflow", self.cur - self.base, self.size)
                return T(off, dt, *shape)

        misc = Bump(OFF_MISC, 8 * KB)
        scr = Bump(OFF_SCR, 31 * KB)
        atr = Bump(OFF_AT, 64 * KB)

        ident = misc.alloc(F32, 128)
        ones = misc.alloc(BF16, 128)
        modsT = misc.alloc(F32, 2, 96)
        modsC = misc.alloc(F32, 32)
        adabT = misc.alloc(F32, 2, 96)
        vecs = misc.alloc(F32, 7, 16)
        Aco = misc.alloc(F32, 4, 16)
        AcoC = misc.alloc(F32, 16)
        convw = misc.alloc(F32, 8, 3)
        epsT = misc.alloc(F32, 1)
        cT = misc.alloc(F32, 16, 2)
        s2 = misc.alloc(BF16, 16, 2)
        st1 = misc.alloc(F32, 8, 8)
        st2 = misc.alloc(F32, 8, 8)
        lnmu = misc.alloc(F32, 8)
        lnex = misc.alloc(F32, 8)
        lnrs = misc.alloc(F32, 8)
        lnnm = misc.alloc(F32, 8)

        XRES = [T(OFF_XRES + i * 2 * KB, F32, 512) for i in range(4)]
        HX = T(OFF_HX, BF16, 16, 2048)
        HXC = T(OFF_HXC, BF16, 16, 256)
        AT = T(OFF_AT, BF16, 16, 2048)

        def hx_rhs(k, lo, n):
            if lo >= 2048:
                return HXC[:, k, lo - 2048: lo - 2048 + n]
            return HX[:, k, lo: lo + n]

        def hx_keys(lo, n):
            return [("HX", k, t) for k in range(16) for t in range(lo // 128, (lo + n) // 128)]

        wstate = {"i": 0}

        def wslot():
            g = wstate["i"] % 2
            wstate["i"] += 1
            return g

        def WT(g, off, kc, n):
            return T(OFF_W + g * WG_BYTES + off, BF16, kc, n)

        def load_w(g, off, src2d, kc, n, key_extra=0):
            t = WT(g, off, kc, n)
            src = src2d.rearrange("(kc p) n -> p kc n", p=128)
            nbytes = kc * n * 2
            keys = [("W", g, s_) for s_ in range(off // (4 * KB), (off + nbytes + 4 * KB - 1) // (4 * KB))]
            sch.op("pool", lambda e: e.dma_start(out=t, in_=src), writes=keys, dma=True, exempt=True)
            wkeys[id(t)] = keys
            return t

        wkeys = {}

        dve_copy = lambda out, in_: (lambda e: e.tensor_copy(out=out, in_=in_))

        sch.op("sp", lambda e: e.dma_start(out=ident, in_=ident_d), writes=["ident"], dma=True)
        sch.op("sp", lambda e: e.dma_start(out=adabT, in_=adabT_d), writes=["adabT"], dma=True)
        sch.op("sp", lambda e: e.dma_start(out=vecs, in_=vecs_d), writes=["vecs"], dma=True)
        sch.op("sp", lambda e: e.dma_start(out=convw, in_=convw_d), writes=["convw"], dma=True)
        sch.op("sp", lambda e: e.dma_start(out=cT, in_=cT_d), writes=["cT"], dma=True)
        sch.op("dve", lambda e: e.memset(ones, 1.0), writes=["ones"])
        sch.op("dve", lambda e: e.memset(epsT, EPS), writes=["eps"])
        sch.op("act", lambda e: e.activation(out=s2, in_=cT, func=AF.Silu), reads=["cT"], writes=["s2"])

        def ada(layer, jlist, with_ctx):
            rowst = [scr_rows[i] for i in range(4)]
            cnt = 0
            for j in jlist:
                for n in range(8):
                    col0 = j * 2048 + n * 256
                    g = wslot()
                    wt = load_w(g, 0, ada_w[layer][:, col0:col0 + 256], 16, 256)
                    b = cnt % 2
                    psb = bank(b)

                    def mm(e, wt=wt, psb=psb):
                        for k in range(16):
                            ins = e.matmul(psb[0:2, 0:256], lhsT=s2[:, k, :], rhs=wt[:, k, :],
                                           start=(k == 0), stop=(k == 15))
                        return ins
                    sch.op("pe", mm, reads=wkeys[id(wt)] + ["s2"], writes=[("ps", b)])
                    rt = rowst[cnt % 4]
                    sch.op("dve", dve_copy(rt[0:2, :], psb[0:2, 0:256]), reads=[("ps", b)],
                           writes=[("row", cnt % 4)])
                    sch.op("sp", lambda e, rt=rt, col0=col0: e.dma_start(
                        out=modrow_d[layer, :, col0:col0 + 256], in_=rt[0:2, :]),
                        reads=[("row", cnt % 4)], writes=[("modrow", layer, j)], dma=True)
                    cnt += 1
            for j in jlist:
                src = modrow_d[layer, 0, j * 2048:(j + 1) * 2048].rearrange("(c p) -> p c", p=128)
                sch.op("sp", lambda e, src=src, j=j: e.dma_start(
                    out=modsT[:, layer, j * 16:(j + 1) * 16], in_=src, allow_slow_non_contiguous=True),
                    reads=[("modrow", layer, j)], writes=[("modsT", layer, j)], dma=True)
                sch.op("dve", lambda e, j=j: e.tensor_tensor(
                    out=modsT[:, layer, j * 16:(j + 1) * 16], in0=modsT[:, layer, j * 16:(j + 1) * 16],
                    in1=adabT[:, layer, j * 16:(j + 1) * 16], op=ALU.add),
                    reads=[("modsT", layer, j), "adabT"], writes=[("modsT", layer, j)])
                if with_ctx and j < 2:
                    srcc = modrow_d[layer, 1, j * 2048:(j + 1) * 2048].rearrange("(c p) -> p c", p=128)
                    sch.op("sp", lambda e, srcc=srcc, j=j: e.dma_start(
                        out=modsC[:, j * 16:(j + 1) * 16], in_=srcc, allow_slow_non_contiguous=True),
                        reads=[("modrow", layer, j)], writes=[("modsC", j)], dma=True)
                    sch.op("dve", lambda e, j=j: e.tensor_tensor(
                        out=modsC[:, j * 16:(j + 1) * 16], in0=modsC[:, j * 16:(j + 1) * 16],
                        in1=adabT[:, layer, j * 16:(j + 1) * 16], op=ALU.add),
                        reads=[("modsC", j), "adabT"], writes=[("modsC", j)])

        def mk_aco(layer, which):
            gi = (0 if which == 0 else 2) + layer
            j = 1 if which == 0 else 4
            sch.op("dve", lambda e: e.scalar_tensor_tensor(
                out=Aco[:, layer * 2 + which, :], in0=modsT[:, layer, j * 16:(j + 1) * 16], scalar=1.0,
                in1=vecs[:, gi, :], op0=ALU.add, op1=ALU.mult),
                reads=[("modsT", layer, j), "vecs"], writes=[("Aco", layer, which)])

        scr.reset()
        scr_rows = [scr.alloc(F32, 256) for _ in range(4)]
        ada(0, [0, 1], True)
        mk_aco(0, 0)
        sch.op("dve", lambda e: e.scalar_tensor_tensor(
            out=AcoC, in0=modsC[:, 16:32], scalar=1.0, in1=vecs[:, 0, :], op0=ALU.add, op1=ALU.mult),
            reads=[("modsC", 1), "vecs"], writes=["AcoC"])
        ada(0, [2, 3, 4, 5], False)
        mk_aco(0, 1)
        ada(1, [0, 1, 2, 3, 4, 5], False)
        mk_aco(1, 0)
        mk_aco(1, 1)

        mk = atr.alloc(F32, 21 * 128)
        bst = [atr.alloc(F32, 21 * 128) for _ in range(1)]
        eo = [atr.alloc(BF16, 21 * 128) for _ in range(2)]
        sch.op("sp", lambda e: e.dma_start(out=mk, in_=mask_d.rearrange("p a b -> p (a b)")), writes=["mk"],
               dma=True)
        for h in range(8):
            sch.op("sp", lambda e, h=h: e.dma_start(out=bst[0], in_=bias_d[:, h].rearrange("p a b -> p (a b)")),
                   writes=["bst"], dma=True)
            sch.op("act", lambda e: e.activation(out=bst[0], in_=bst[0], func=AF.Exp), reads=["bst"],
                   writes=["bst"])
            sch.op("dve", lambda e, h=h: e.tensor_tensor(out=eo[h % 2], in0=bst[0], in1=mk, op=ALU.mult),
                   reads=["bst", "mk"], writes=[("eo", h % 2)])
            sch.op("sp", lambda e, h=h: e.dma_start(out=etab_d[h], in_=eo[h % 2]), reads=[("eo", h % 2)],
                   writes=[("etab", h)], dma=True)
        sch.barrier()

        def norm_tile(XT, xkeys, ntok, aco, bvec, out_fn, okeys_fn, nb, final=False, fin_out=None):
            sq, rstd, tmp = nb
            pss = bank(6)
            for k in range(16):
                sb = sq[k % 2]
                sch.op("act", lambda e, k=k, sb=sb: e.activation(out=sb[:, 0:ntok], in_=XT[:, k, 0:ntok],
                                                                func=AF.Square),
                       reads=[xkeys[k]], writes=[("sq", k % 2)])
                sch.op("pe", lambda e, k=k, sb=sb: e.matmul(pss[:, 0:ntok], lhsT=ones, rhs=sb[:, 0:ntok],
                                                            start=(k == 0), stop=(k == 15)),
                       reads=[("sq", k % 2), "ones"], writes=[("ps", 6)])
            sch.op("act", lambda e: e.activation(out=rstd[:, 0:ntok], in_=pss[:, 0:ntok], func=AF.Sqrt,
                                                 scale=1.0 / D, bias=epsT),
                   reads=[("ps", 6), "eps"], writes=["rstd"])
            sch.op("dve", lambda e: e.reciprocal(out=rstd[:, 0:ntok], in_=rstd[:, 0:ntok]),
                   reads=["rstd"], writes=["rstd"])
            for k in range(16):
                if final:
                    sch.op("dve", lambda e, k=k: e.scalar_tensor_tensor(
                        out=fin_out[:, k, 0:ntok], in0=XT[:, k, 0:ntok], scalar=aco[:, k:k + 1],
                        in1=rstd[:, 0:ntok], op0=ALU.mult, op1=ALU.mult),
                        reads=[xkeys[k], "rstd"] + aco_keys[0], writes=[("fin", k)])
                    continue
                tb = tmp[k % 2]
                sch.op("dve", lambda e, k=k, tb=tb: e.scalar_tensor_tensor(
                    out=tb[:, 0:ntok], in0=XT[:, k, 0:ntok], scalar=aco[:, k:k + 1], in1=rstd[:, 0:ntok],
                    op0=ALU.mult, op1=ALU.mult),
                    reads=[xkeys[k], "rstd"] + aco_keys[0], writes=[("ntmp", k % 2)])
                sch.op("act", lambda e, k=k, tb=tb: e.activation(out=out_fn(k), in_=tb[:, 0:ntok],
                                                                func=AF.Identity, bias=bvec[:, k:k + 1]),
                       reads=[("ntmp", k % 2)] + aco_keys[0], writes=okeys_fn(k))

        aco_keys = [[]]

        def norm_bufs():
            sq = [scr.alloc(BF16, 512) for _ in range(2)]
            rstd = scr.alloc(F32, 512)
            tmp = [scr.alloc(F32, 512) for _ in range(2)]
            return sq, rstd, tmp

        def norm_phase(layer, which):
            scr.reset()
            nb = norm_bufs()
            XTs = [T(OFF_AT + i * 32 * KB, F32, 16, 512) for i in range(2)]
            j_shift = 0 if which == 0 else 3
            aco = Aco[:, layer * 2 + which, :]
            bvec = modsT[:, layer, j_shift * 16:(j_shift + 1) * 16]
            aco_keys[0] = [("Aco", layer, which), ("modsT", layer, j_shift)]
            for tt in range(4):
                XT = XTs[tt % 2]
                for k in range(16):
                    sch.op("sp", lambda e, k=k, tt=tt, XT=XT: e.dma_start(
                        out=XT[:, k, :], in_=xT_d[k, :, tt * 512:(tt + 1) * 512]),
                        reads=[("xT", k, tt)], writes=[("XT", tt % 2, k)], dma=True)
                norm_tile(XT, [("XT", tt % 2, k) for k in range(16)], 512, aco, bvec,
                          lambda k, tt=tt: HX[:, k, tt * 512:(tt + 1) * 512],
                          lambda k, tt=tt: [("HX", k, t) for t in range(tt * 4, tt * 4 + 4)], nb)
            sch.barrier()

        scr.reset()
        nbA = norm_bufs()
        xin = T(OFF_AT, F32, 4, 2048)
        XTa = T(OFF_AT + 32 * KB, F32, 16, 512)
        aco_keys[0] = [("Aco", 0, 0), ("modsT", 0, 0)]
        for tt in range(5):
            nblk = 4 if tt < 4 else 2
            ntok = nblk * 128
            srcd = x_d if tt < 4 else ctx_d
            for t in range(nblk):
                r0 = (tt * 512 if tt < 4 else 0) + t * 128
                sch.op("sp", lambda e, t=t, r0=r0, srcd=srcd: e.dma_start(out=xin[:, t, :],
                                                                          in_=srcd[r0:r0 + 128, :]),
                       writes=[("xin", t)], dma=True)
            for k in range(16):
                b = k % 2
                psb = bank(b)

                def tr(e, k=k, psb=psb, nblk=nblk):
                    for t in range(nblk):
                        ins = e.transpose(psb[:, t * 128:(t + 1) * 128], xin[:, t, k * 128:(k + 1) * 128], ident)
                    return ins
                sch.op("pe", tr, reads=["ident"] + [("xin", t) for t in range(nblk)], writes=[("ps", b)])
                sch.op("dve" if k % 2 == 0 else "act",
                       (lambda e, k=k, psb=psb, ntok=ntok: e.tensor_copy(out=XTa[:, k, 0:ntok], in_=psb[:, 0:ntok]))
                       if k % 2 == 0 else
                       (lambda e, k=k, psb=psb, ntok=ntok: e.copy(out=XTa[:, k, 0:ntok], in_=psb[:, 0:ntok])),
                       reads=[("ps", b)], writes=[("XTa", k)])
                if tt < 4:
                    sch.op("sp", lambda e, k=k, tt=tt: e.dma_start(out=xT_d[k, :, tt * 512:(tt + 1) * 512],
                                                                   in_=XTa[:, k, :]),
                           reads=[("XTa", k)], writes=[("xT", k, tt)], dma=True)
            if tt < 4:
                norm_tile(XTa, [("XTa", k) for k in range(16)], 512, Aco[:, 0, :], modsT[:, 0, 0:16],
                          lambda k, tt=tt: HX[:, k, tt * 512:(tt + 1) * 512],
                          lambda k, tt=tt: [("HX", k, t) for t in range(tt * 4, tt * 4 + 4)], nbA)
            else:
                aco_keys[0] = ["AcoC", ("modsC", 0)]
                norm_tile(XTa, [("XTa", k) for k in range(16)], 256, AcoC, modsC[:, 0:16],
                          lambda k: HXC[:, k, :],
                          lambda k: [("HX", k, 16), ("HX", k, 17)], nbA)
        sch.barrier()

        def dump(kind):
            if kind == "HX":
                st_ = scr.alloc(F32, 2304) if False else None
                for k in range(16):
                    for tt in range(5):
                        n = 512 if tt < 4 else 256
                        xb = XRES[(k * 5 + tt) % 4]
                        srcv = HX[:, k, tt * 512:(tt + 1) * 512] if tt < 4 else HXC[:, k, :]
                        sch.op("dve", lambda e, xb=xb, srcv=srcv, n=n: e.tensor_copy(out=xb[:, 0:n], in_=srcv),
                               reads=hx_keys(tt * 512, n), writes=[("xres", (k * 5 + tt) % 4)])
                        sch.op("sp", lambda e, xb=xb, k=k, tt=tt, n=n: e.dma_start(
                            out=dbg_d[k, :, tt * 512:tt * 512 + n], in_=xb[:, 0:n]),
                            reads=[("xres", (k * 5 + tt) % 4)], writes=[("dbg", k, tt)], dma=True)
            elif kind == "AT":
                for k in range(16):
                    for tt in range(4):
                        xb = XRES[(k * 4 + tt) % 4]
                        sch.op("dve", lambda e, xb=xb, k=k, tt=tt: e.tensor_copy(
                            out=xb, in_=AT[:, k, tt * 512:(tt + 1) * 512]),
                            reads=[("AT", k, t) for t in range(16)], writes=[("xres", (k * 4 + tt) % 4)])
                        sch.op("sp", lambda e, xb=xb, k=k, tt=tt: e.dma_start(
                            out=dbg_d[k, :, tt * 512:(tt + 1) * 512], in_=xb),
                            reads=[("xres", (k * 4 + tt) % 4)], writes=[("dbg", k, tt)], dma=True)
            elif kind == "xT":
                for k in range(16):
                    for tt in range(4):
                        xb = XRES[(k * 4 + tt) % 4]
                        sch.op("sp", lambda e, xb=xb, k=k, tt=tt: e.dma_start(
                            out=xb, in_=xT_d[k, :, tt * 512:(tt + 1) * 512]),
                            reads=[("xT", k, tt)], writes=[("xres", (k * 4 + tt) % 4)], dma=True)
                        sch.op("sp", lambda e, xb=xb, k=k, tt=tt: e.dma_start(
                            out=dbg_d[k, :, tt * 512:(tt + 1) * 512], in_=xb),
                            reads=[("xres", (k * 4 + tt) % 4)], writes=[("dbg", k, tt)], dma=True)
            sch.barrier()
            sch.op("sp", lambda e: None)
            sch.emit()

        if stop == "A":
            dump("HX")
            return nc

        def gemm(psb, pskey, wt, wkey, K, rhs_fn, rkeys):
            def f(e):
                for k in range(K):
                    ins = e.matmul(psb, lhsT=wt[:, k, :], rhs=rhs_fn(k), start=(k == 0), stop=(k == K - 1))
                return ins
            sch.op("pe", f, reads=wkeys[id(wt)] + rkeys, writes=[pskey])

        scr.reset()
        qT = scr.alloc(BF16, 2048)
        kT = scr.alloc(BF16, 2304)
        Vh = scr.alloc(BF16, 18, 128)
        Et = scr.alloc(BF16, 21, 128)
        PTf = [scr.alloc(F32, 640) for _ in range(2)]
        PTb = [scr.alloc(BF16, 896) for _ in range(2)]
        rZ = [scr.alloc(F32, 128) for _ in range(2)]
        for h in range(8):
            if os.environ.get("DBGBAR", "") == "1":
                sch.barrier()
            g = wslot()
            wq = load_w(g, 0, e_win[:, h * 128:(h + 1) * 128], 16, 128, 0)
            wk = load_w(g, 4 * KB, e_win[:, 1024 + h * 128:1024 + (h + 1) * 128], 16, 128, 1)
            wv = load_w(g, 8 * KB, e_win[:, 2048 + h * 128:2048 + (h + 1) * 128], 16, 128, 2)
            sch.op("sp", lambda e, h=h: e.dma_start(out=Et.rearrange("p a b -> p (a b)"), in_=etab_d[h]),
                   reads=[("etab", h)], writes=["Et"], dma=True)
            gi = 0
            for tt in range(4):
                b = gi % 2
                gi += 1
                gemm(bank(b), ("ps", b), wq, ("W", g, 0), 16, lambda k, tt=tt: hx_rhs(k, tt * 512, 512),
                     hx_keys(tt * 512, 512))
                sch.op("act", lambda e, b=b, tt=tt: e.activation(out=qT[:, tt * 512:(tt + 1) * 512], in_=bank(b),
                                                                func=AF.Identity, scale=SCALE),
                       reads=[("ps", b)], writes=[("qT", tt)])
            for tt in range(5):
                n = 512 if tt < 4 else 256
                b = gi % 2
                gi += 1
                gemm(bank(b)[:, 0:n], ("ps", b), wk, ("W", g, 1), 16, lambda k, tt=tt, n=n: hx_rhs(k, tt * 512, n),
                     hx_keys(tt * 512, n))
                sch.op("dve", lambda e, b=b, tt=tt, n=n: e.tensor_copy(out=kT[:, tt * 512:tt * 512 + n],
                                                                       in_=bank(b)[:, 0:n]),
                       reads=[("ps", b)], writes=[("kT", tt)])
            for vb in range(5):
                nblk = 4 if vb < 4 else 2
                b = gi % 2
                gi += 1

                def vmm(e, vb=vb, nblk=nblk, b=b, wv=wv):
                    for t in range(nblk):
                        blk = vb * 4 + t
                        for k in range(16):
                            ins = e.matmul(bank(b)[:, t * 128:(t + 1) * 128],
                                           lhsT=hx_rhs(k, blk * 128, 128), rhs=wv[:, k, :],
                                           start=(k == 0), stop=(k == 15))
                    return ins
                sch.op("pe", vmm, reads=wkeys[id(wv)] + hx_keys(vb * 512, nblk * 128), writes=[("ps", b)])
                sch.op("act" if vb % 2 == 0 else "dve",
                       (lambda e, vb=vb, nblk=nblk, b=b: e.copy(
                           out=Vh[:, vb * 4:vb * 4 + nblk, :],
                           in_=bank(b)[:, 0:nblk * 128].rearrange("p (a b) -> p a b", a=nblk)))
                       if vb % 2 == 0 else
                       (lambda e, vb=vb, nblk=nblk, b=b: e.tensor_copy(
                           out=Vh[:, vb * 4:vb * 4 + nblk, :],
                           in_=bank(b)[:, 0:nblk * 128].rearrange("p (a b) -> p a b", a=nblk))),
                       reads=[("ps", b)], writes=[("Vh", vb)])
            if stop == "Q" and h == 0:
                for idx_, (srcT, n_) in enumerate([(qT, 2048), (kT, 2304), (Vh.rearrange("p a b -> p (a b)"), 2304)]):
                    for c0_ in range(0, n_, 512):
                        w_ = min(512, n_ - c0_)
                        xb = XRES[(idx_ * 5 + c0_ // 512) % 4]
                        kk_ = (idx_ * 5 + c0_ // 512) % 4
                        sch.op("dve", lambda e, xb=xb, srcT=srcT, c0_=c0_, w_=w_: e.tensor_copy(
                            out=xb[:, 0:w_], in_=srcT[:, c0_:c0_ + w_]),
                            reads=[("qT", t) for t in range(4)] + [("kT", t) for t in range(5)] + [("Vh", t) for t in range(5)],
                            writes=[("xres", kk_)])
                        sch.op("sp", lambda e, xb=xb, idx_=idx_, c0_=c0_, w_=w_: e.dma_start(
                            out=dbg_d[idx_, :, c0_:c0_ + w_], in_=xb[:, 0:w_]),
                            reads=[("xres", kk_)], writes=[("dbg", idx_, c0_)], dma=True)
                sch.barrier()
                sch.op("sp", lambda e: None)
                sch.emit()
                return nc
            def stage1(m, h=h):
                pl = pairs_of(m)
                nl = len(pl)
                tb = tile_base(m)
                pb = m % 2
                psS = PS[1 + pb]
                qs = qT[:, m * 128:(m + 1) * 128]

                def qk(e):
                    for c in range(2):
                        ins = e.matmul(psS[:, c * 128:(c + 1) * 128], lhsT=kT[:, 2048 + c * 128:2048 + (c + 1) * 128],
                                       rhs=qs, start=True, stop=True)
                    for i, p in enumerate(pl):
                        ins = e.matmul(psS[:, 256 + i * 128:256 + (i + 1) * 128],
                                       lhsT=kT[:, p * 128:(p + 1) * 128], rhs=qs, start=True, stop=True)
                    return ins
                sch.op("pe", qk, reads=[("qT", m // 4), ("kT", 4)] + [("kT", p // 4) for p in pl],
                       writes=[("psS", pb)])
                sch.op("act", lambda e: e.activation(out=PTb[pb][:, 0:256], in_=psS[:, 0:256], func=AF.Exp),
                       reads=[("psS", pb)], writes=[("PTb", pb, 0)])
                sch.op("act", lambda e: e.activation(out=PTf[pb][:, 0:nl * 128], in_=psS[:, 256:256 + nl * 128],
                                                     func=AF.Exp),
                       reads=[("psS", pb)], writes=[("PTf", pb)])
                sch.op("dve", lambda e: e.tensor_tensor(
                    out=PTb[pb][:, 256:256 + nl * 128], in0=PTf[pb][:, 0:nl * 128],
                    in1=Et[:, tb:tb + nl, :].rearrange("p a b -> p (a b)"), op=ALU.mult),
                    reads=[("PTf", pb), "Et"], writes=[("PTb", pb, 1)])

            def stage2(m, h=h):
                pl = pairs_of(m)
                pb = m % 2
                psO = bank(6 + pb)

                def pv(e):
                    tiles = [(16 + c, c) for c in range(2)] + [(p, 2 + i) for i, p in enumerate(pl)]
                    for i, (vblk, ti) in enumerate(tiles):
                        ins = e.matmul(psO[:, 0:128], lhsT=Vh[:, vblk, :], rhs=PTb[pb][:, ti * 128:(ti + 1) * 128],
                                       start=(i == 0), stop=(i == len(tiles) - 1))
                    for i, (vblk, ti) in enumerate(tiles):
                        ins = e.matmul(psO[:, 128:256], lhsT=ones, rhs=PTb[pb][:, ti * 128:(ti + 1) * 128],
                                       start=(i == 0), stop=(i == len(tiles) - 1))
                    return ins
                sch.op("pe", pv, reads=[("PTb", pb, 0), ("PTb", pb, 1), "ones", ("Vh", 4)] +
                       [("Vh", p // 4) for p in pl], writes=[("ps", 6 + pb)])
                sch.op("dve", lambda e: e.reciprocal(out=rZ[pb], in_=psO[:, 128:256]),
                       reads=[("ps", 6 + pb)], writes=[("rZ", pb)])
                sch.op("dve", lambda e: e.tensor_tensor(
                    out=AT[:, h, m * 128:(m + 1) * 128], in0=psO[:, 0:128], in1=rZ[pb], op=ALU.mult),
                    reads=[("ps", 6 + pb), ("rZ", pb)], writes=[("AT", h, m)])

            skip = os.environ.get("DBGSKIP", "")
            if not (skip == "attn" and h == 1):
                stage1(0)
                for m in range(16):
                    if m + 1 < 16:
                        stage1(m + 1)
                    if not (skip == "s2" and h == 1):
                        stage2(m)
            if stop in ("C0", "C1") and h == int(stop[1]):
                sch.barrier()
                dump("AT")
                return nc
        sch.barrier()

        scr.reset()
        zb = [scr.alloc(F32, 2050) for _ in range(2)]
        ctmp = [scr.alloc(F32, 512) for _ in range(2)]
        ytmp = [scr.alloc(F32, 512) for _ in range(2)]
        for i in range(2):
            sch.op("dve", lambda e, i=i: e.memset(zb[i][:, 0:1], 0.0), writes=[("z", i, 0)])
            sch.op("dve", lambda e, i=i: e.memset(zb[i][:, 2049:2050], 0.0), writes=[("z", i, 3)])
        gi = 0
        for j in range(8):
            g = wslot()
            wb = load_w(g, 0, e_win[:, 3072 + j * 128:3072 + (j + 1) * 128], 16, 128, 0)
            wc = load_w(g, 4 * KB, e_win[:, 4096 + j * 128:4096 + (j + 1) * 128], 16, 128, 1)
            wu = load_w(g, 8 * KB, e_win[:, 5120 + j * 128:5120 + (j + 1) * 128], 16, 128, 2)
            z = zb[j % 2]
            for tt in range(4):
                b = gi % 4
                gi += 1
                gemm(bank(b), ("ps", b), wc, ("W", g, 1), 16, lambda k, tt=tt: hx_rhs(k, tt * 512, 512),
                     hx_keys(tt * 512, 512))
                ct = ctmp[tt % 2]
                sch.op("act", lambda e, b=b, ct=ct: e.copy(out=ct, in_=bank(b)), reads=[("ps", b)],
                       writes=[("ctmp", tt % 2)])
                b2 = gi % 4
                gi += 1
                gemm(bank(b2), ("ps", b2), wu, ("W", g, 2), 16, lambda k, tt=tt: hx_rhs(k, tt * 512, 512),
                     hx_keys(tt * 512, 512))
                sch.op("dve", lambda e, b2=b2, ct=ct, z=z, tt=tt: e.tensor_tensor(
                    out=z[:, 1 + tt * 512:1 + (tt + 1) * 512], in0=bank(b2), in1=ct, op=ALU.mult),
                    reads=[("ps", b2), ("ctmp", tt % 2)], writes=[("z", j % 2, 1, tt)])
            zkeys = [("z", j % 2, 0), ("z", j % 2, 3)] + [("z", j % 2, 1, t) for t in range(4)]
            for tt in range(4):
                b = gi % 4
                gi += 1
                gemm(bank(b), ("ps", b), wb, ("W", g, 0), 16, lambda k, tt=tt: hx_rhs(k, tt * 512, 512),
                     hx_keys(tt * 512, 512))
                yt = ytmp[tt % 2]
                c0 = 1 + tt * 512
                sch.op("dve", lambda e, z=z, yt=yt, c0=c0, j=j: e.tensor_scalar(
                    out=yt, in0=z[:, c0:c0 + 512], scalar1=convw[:, j, 1:2], scalar2=None, op0=ALU.mult),
                    reads=zkeys + ["convw"], writes=[("ytmp", tt % 2)])
                sch.op("dve", lambda e, z=z, yt=yt, c0=c0, j=j: e.scalar_tensor_tensor(
                    out=yt, in0=z[:, c0 - 1:c0 + 511], scalar=convw[:, j, 0:1], in1=yt, op0=ALU.mult, op1=ALU.add),
                    reads=zkeys + ["convw", ("ytmp", tt % 2)], writes=[("ytmp", tt % 2)])
                sch.op("dve", lambda e, z=z, yt=yt, c0=c0, j=j: e.scalar_tensor_tensor(
                    out=yt, in0=z[:, c0 + 1:c0 + 513], scalar=convw[:, j, 2:3], in1=yt, op0=ALU.mult, op1=ALU.add),
                    reads=zkeys + ["convw", ("ytmp", tt % 2)], writes=[("ytmp", tt % 2)])
                sch.op("dve", lambda e, b=b, yt=yt, j=j, tt=tt: e.tensor_tensor(
                    out=AT[:, 8 + j, tt * 512:(tt + 1) * 512], in0=bank(b), in1=yt, op=ALU.mult),
                    reads=[("ps", b), ("ytmp", tt % 2)], writes=[("AT", 8 + j, 4 * tt + t) for t in range(4)])
        sch.barrier()
        if stop == "C":
            dump("AT")
            return nc

        def outproj_phase(w2d, layer, jgate, tok0, ntt, at_fn, at_keys_fn):
            gi = 0
            for d in range(16):
                g = wslot()
                wt = load_w(g, 0, w2d[:, d * 128:(d + 1) * 128], 16, 128, 0)
                for sub in range(ntt):
                    tt = tok0 // 512 + sub
                    b = gi % 4
                    xb = XRES[gi % 4]
                    gi += 1
                    sch.op("sp", lambda e, xb=xb, d=d, tt=tt: e.dma_start(
                        out=xb, in_=xT_d[d, :, tt * 512:(tt + 1) * 512]),
                        reads=[("xT", d, tt)], writes=[("xres", (gi - 1) % 4)], dma=True)
                    gemm(bank(b), ("ps", b), wt, ("W", g, 0), 16, lambda k, sub=sub: at_fn(k, sub),
                         at_keys_fn(sub))
                    sch.op("dve", lambda e, xb=xb, b=b, d=d: e.scalar_tensor_tensor(
                        out=xb, in0=bank(b), scalar=modsT[:, layer, jgate * 16 + d:jgate * 16 + d + 1], in1=xb,
                        op0=ALU.mult, op1=ALU.add),
                        reads=[("ps", b), ("xres", (gi - 1) % 4), ("modsT", layer, jgate)],
                        writes=[("xres", (gi - 1) % 4)])
                    sch.op("sp", lambda e, xb=xb, d=d, tt=tt: e.dma_start(
                        out=xT_d[d, :, tt * 512:(tt + 1) * 512], in_=xb),
                        reads=[("xres", (gi - 1) % 4)], writes=[("xT", d, tt)], dma=True)

        outproj_phase(e_wout, 0, 2, 0, 4, lambda k, sub: AT[:, k, sub * 512:(sub + 1) * 512],
                      lambda sub: [("AT", k, 4 * sub + t) for k in range(16) for t in range(4)])
        sch.barrier()
        if stop == "D":
            dump("xT")
            return nc

        def ffn_phase(layer):
            norm_phase(layer, 1)
            HT = T(OFF_AT, BF16, NF, 1024)
            sg = [T(OFF_HXC + i * 2 * KB, F32, 512) for i in range(2)]
            for half in range(2):
                gi = 0
                for f in range(NF):
                    g = wslot()
                    wg_ = load_w(g, 0, w_gate[layer][:, f * 128:(f + 1) * 128], 16, 128, 0)
                    wu_ = load_w(g, 4 * KB, w_up[layer][:, f * 128:(f + 1) * 128], 16, 128, 1)
                    for sub in range(2):
                        lo = half * 1024 + sub * 512
                        b = gi % 4
                        b2 = (gi + 1) % 4
                        gi += 2
                        gemm(bank(b), ("ps", b), wg_, ("W", g, 0), 16, lambda k, lo=lo: hx_rhs(k, lo, 512),
                             hx_keys(lo, 512))
                        gemm(bank(b2), ("ps", b2), wu_, ("W", g, 1), 16, lambda k, lo=lo: hx_rhs(k, lo, 512),
                             hx_keys(lo, 512))
                        sgt = sg[(gi // 2) % 2]
                        sch.op("act", lambda e, b=b, sgt=sgt: e.activation(out=sgt, in_=bank(b), func=AF.Silu),
                               reads=[("ps", b)], writes=[("sg", (gi // 2) % 2)])
                        sch.op("dve", lambda e, b2=b2, sgt=sgt, f=f, sub=sub: e.tensor_tensor(
                            out=HT[:, f, sub * 512:(sub + 1) * 512], in0=bank(b2), in1=sgt, op=ALU.mult),
                            reads=[("ps", b2), ("sg", (gi // 2) % 2)], writes=[("HT", f, sub)])
                gi = 0
                for d in range(16):
                    g = wslot()
                    wd_ = load_w(g, 0, w_down[layer][:, d * 128:(d + 1) * 128], NF, 128, 0)
                    for sub in range(2):
                        tt = half * 2 + sub
                        b = 4 + gi % 2
                        xi = gi % 4
                        xb = XRES[xi]
                        gi += 1
                        sch.op("sp", lambda e, xb=xb, d=d, tt=tt: e.dma_start(
                            out=xb, in_=xT_d[d, :, tt * 512:(tt + 1) * 512]),
                            reads=[("xT", d, tt)], writes=[("xres", xi)], dma=True)
                        gemm(bank(b), ("ps", b), wd_, ("W", g, 0), NF,
                             lambda k, sub=sub: HT[:, k, sub * 512:(sub + 1) * 512],
                             [("HT", f, sub) for f in range(NF)])
                        sch.op("dve", lambda e, xb=xb, b=b, d=d: e.scalar_tensor_tensor(
                            out=xb, in0=bank(b), scalar=modsT[:, layer, 5 * 16 + d:5 * 16 + d + 1], in1=xb,
                            op0=ALU.mult, op1=ALU.add),
                            reads=[("ps", b), ("xres", xi), ("modsT", layer, 5)], writes=[("xres", xi)])
                        sch.op("sp", lambda e, xb=xb, d=d, tt=tt: e.dma_start(
                            out=xT_d[d, :, tt * 512:(tt + 1) * 512], in_=xb),
                            reads=[("xres", xi)], writes=[("xT", d, tt)], dma=True)
            sch.barrier()

        ffn_phase(0)
        if stop == "F0":
            dump("xT")
            return nc

        norm_phase(1, 0)
        scr.reset()
        utmp = [scr.alloc(F32, 512) for _ in range(2)]
        mtmp = [scr.alloc(F32, 512) for _ in range(2)]
        Bt = scr.alloc(F32, 16, 128)
        bsbc = scr.alloc(F32, 16, 128)
        junk = scr.alloc(BF16, 256)
        sgwT = scr.alloc(BF16, 16, 128)
        sch.op("pool", lambda e: e.dma_start(out=sgwT, in_=sgwT_d), writes=["sgwT"], dma=True)
        sch.op("sp", lambda e: e.dma_start(out=bsbc.rearrange("p a b -> p (a b)"),
                                           in_=sgb_d.rearrange("a b -> (a b)").partition_broadcast(128)),
               writes=["bsbc"], dma=True)
        for q4 in range(4):
            def rmm(e, q4=q4):
                return e.matmul(bank(q4), lhsT=ones, rhs=sgwT[:, q4 * 4:(q4 + 1) * 4, :].rearrange("p a b -> p (a b)"),
                                start=True, stop=True)
            sch.op("pe", rmm, reads=["ones", "sgwT"], writes=[("ps", q4)])
            for gg in range(4):
                gidx = q4 * 4 + gg
                sch.op("dve", lambda e, q4=q4, gg=gg, gidx=gidx: e.scalar_tensor_tensor(
                    out=Bt[:, gidx, :], in0=bank(q4)[:, gg * 128:(gg + 1) * 128], scalar=vecs[:, 6, gidx:gidx + 1],
                    in1=bsbc[:, gidx, :], op0=ALU.mult, op1=ALU.add),
                    reads=[("ps", q4), "vecs", "bsbc"], writes=[("Bt", gidx)])
        VH = T(OFF_AT, BF16, 8, 2048)
        ATh = T(OFF_AT + 32 * KB, BF16, 16, 1024)
        for half in range(2):
            gi = 0
            for cg in range(8):
                g = wslot()
                wv_ = load_w(g, 0, o_win[:, 2048 + cg * 256:2048 + (cg + 1) * 256], 16, 256, 0)
                for blk in range(8):
                    tok = half * 1024 + blk * 128
                    b = gi % 4
                    gi += 1

                    def vmm(e, tok=tok, b=b, wv_=wv_):
                        for k in range(16):
                            ins = e.matmul(bank(b)[:, 0:256], lhsT=hx_rhs(k, tok, 128), rhs=wv_[:, k, :],
                                           start=(k == 0), stop=(k == 15))
                        return ins
                    sch.op("pe", vmm, reads=wkeys[id(wv_)] + hx_keys(tok, 128), writes=[("ps", b)])
                    sch.op("act", lambda e, b=b, blk=blk, cg=cg: e.activation(
                        out=VH[:, blk, cg * 256:(cg + 1) * 256], in_=bank(b)[:, 0:256], func=AF.Gelu_apprx_tanh,
                        accum_out=st1[:, blk, cg:cg + 1]),
                        reads=[("ps", b)], writes=[("VH", blk, cg), ("st1", blk, cg)])
                    sch.op("act", lambda e, blk=blk, cg=cg: e.activation(
                        out=junk, in_=VH[:, blk, cg * 256:(cg + 1) * 256], func=AF.Square,
                        accum_out=st2[:, blk, cg:cg + 1]),
                        reads=[("VH", blk, cg)], writes=["junk", ("st2", blk, cg)])
            allst = [("st1", b_, c_) for b_ in range(8) for c_ in range(8)] + \
                    [("st2", b_, c_) for b_ in range(8) for c_ in range(8)]
            sch.op("dve", lambda e: e.tensor_reduce(out=lnmu, in_=st1, axis=mybir.AxisListType.X, op=ALU.add),
                   reads=allst, writes=["lnmu"])
            sch.op("dve", lambda e: e.tensor_reduce(out=lnex, in_=st2, axis=mybir.AxisListType.X, op=ALU.add),
                   reads=allst, writes=["lnex"])
            sch.op("dve", lambda e: e.tensor_scalar(out=lnmu, in0=lnmu, scalar1=1.0 / 2048, scalar2=None,
                                                    op0=ALU.mult), reads=["lnmu"], writes=["lnmu"])
            sch.op("dve", lambda e: e.tensor_tensor(out=lnnm, in0=lnmu, in1=lnmu, op=ALU.mult),
                   reads=["lnmu"], writes=["lnnm"])
            sch.op("dve", lambda e: e.scalar_tensor_tensor(out=lnrs, in0=lnex, scalar=1.0 / 2048, in1=lnnm,
                                                           op0=ALU.mult, op1=ALU.subtract),
                   reads=["lnex", "lnnm"], writes=["lnrs"])
            sch.op("act", lambda e: e.activation(out=lnrs, in_=lnrs, func=AF.Sqrt, scale=1.0, bias=epsT),
                   reads=["lnrs", "eps"], writes=["lnrs"])
            sch.op("dve", lambda e: e.reciprocal(out=lnrs, in_=lnrs), reads=["lnrs"], writes=["lnrs"])
            sch.op("dve", lambda e: e.scalar_tensor_tensor(out=lnnm, in0=lnmu, scalar=-1.0, in1=lnrs,
                                                           op0=ALU.mult, op1=ALU.mult),
                   reads=["lnmu", "lnrs"], writes=["lnnm"])
            for blk in range(8):
                sch.op("dve", lambda e, blk=blk: e.tensor_scalar(
                    out=VH[:, blk, :], in0=VH[:, blk, :], scalar1=lnrs[:, blk:blk + 1], scalar2=lnnm[:, blk:blk + 1],
                    op0=ALU.mult, op1=ALU.add),
                    reads=[("VH", blk, c_) for c_ in range(8)] + ["lnrs", "lnnm"],
                    writes=[("VH", blk, c_) for c_ in range(8)])
            gi = 0
            for gch in range(16):
                g = wslot()
                wu_ = load_w(g, 0, o_win[:, gch * 128:(gch + 1) * 128], 16, 128, 0)
                for sub in range(2):
                    lo = half * 1024 + sub * 512
                    b = gi % 2
                    bm = 2 + gi % 2
                    gi += 1
                    gemm(bank(b), ("ps", b), wu_, ("W", g, 0), 16, lambda k, lo=lo: hx_rhs(k, lo, 512),
                         hx_keys(lo, 512))
                    ut = utmp[gi % 2]
                    sch.op("act", lambda e, b=b, ut=ut: e.activation(out=ut, in_=bank(b), func=AF.Gelu_apprx_tanh),
                           reads=[("ps", b)], writes=[("utmp", gi % 2)])

                    def smm(e, sub=sub, gch=gch, bm=bm):
                        for n4 in range(4):
                            blk = sub * 4 + n4
                            ins = e.matmul(bank(bm)[:, n4 * 128:(n4 + 1) * 128],
                                           lhsT=VH[:, blk, gch * 128:(gch + 1) * 128], rhs=sgwT[:, gch, :],
                                           start=True, stop=True)
                        return ins
                    sch.op("pe", smm, reads=["sgwT"] + [("VH", sub * 4 + n4, gch // 2) for n4 in range(4)],
                           writes=[("ps", bm)])
                    mt = mtmp[gi % 2]
                    sch.op("dve", lambda e, bm=bm, mt=mt, gch=gch: e.scalar_tensor_tensor(
                        out=mt.rearrange("p (a b) -> p a b", a=4),
                        in0=bank(bm).rearrange("p (a b) -> p a b", a=4), scalar=vecs[:, 5, gch:gch + 1],
                        in1=Bt[:, gch:gch + 1, :].broadcast_to([128, 4, 128]), op0=ALU.mult, op1=ALU.add),
                        reads=[("ps", bm), "vecs", ("Bt", gch)], writes=[("mtmp", gi % 2)])
                    sch.op("dve", lambda e, mt=mt, ut=ut, gch=gch, sub=sub: e.tensor_tensor(
                        out=ATh[:, gch, sub * 512:(sub + 1) * 512], in0=mt, in1=ut, op=ALU.mult),
                        reads=[("mtmp", gi % 2), ("utmp", gi % 2)], writes=[("ATh", gch, sub)])
            if stop == "G" and half == 0:
                for k in range(16):
                    for sub in range(2):
                        xb = XRES[(k * 2 + sub) % 4]
                        sch.op("dve", lambda e, xb=xb, k=k, sub=sub: e.tensor_copy(
                            out=xb, in_=ATh[:, k, sub * 512:(sub + 1) * 512]),
                            reads=[("ATh", k, sub)], writes=[("xres", (k * 2 + sub) % 4)])
                        sch.op("sp", lambda e, xb=xb, k=k, sub=sub: e.dma_start(
                            out=dbg_d[k, :, sub * 512:(sub + 1) * 512], in_=xb),
                            reads=[("xres", (k * 2 + sub) % 4)], writes=[("dbg", k, sub)], dma=True)
                sch.barrier()
                sch.op("sp", lambda e: None)
                sch.emit()
                return nc
            outproj_phase(o_wout, 1, 2, half * 1024, 2, lambda k, sub: ATh[:, k, sub * 512:(sub + 1) * 512],
                          lambda sub: [("ATh", k, sub) for k in range(16)])
            sch.barrier()
        if stop == "H":
            dump("xT")
            return nc

        ffn_phase(1)

        scr.reset()
        nbF = norm_bufs()
        otile = [scr.alloc(F32, 2048) for _ in range(2)]
        XTf = T(OFF_AT, F32, 16, 512)
        FIN = T(OFF_AT + 32 * KB, F32, 16, 512)
        aco_keys[0] = ["vecs"]
        oi = 0
        for tt in range(4):
            for k in range(16):
                sch.op("sp", lambda e, k=k, tt=tt: e.dma_start(out=XTf[:, k, :], in_=xT_d[k, :, tt * 512:(tt + 1) * 512]),
                       reads=[("xT", k, tt)], writes=[("XT", 0, k)], dma=True)
            norm_tile(XTf, [("XT", 0, k) for k in range(16)], 512, vecs[:, 4, :], None, None, None, nbF,
                      final=True, fin_out=FIN)
            for t in range(4):
                ot = otile[oi % 2]
                for q4 in range(4):
                    b = (oi * 4 + q4) % 4

                    def trb(e, t=t, q4=q4, b=b):
                        for kk in range(4):
                            k = q4 * 4 + kk
                            ins = e.transpose(bank(b)[:, kk * 128:(kk + 1) * 128], FIN[:, k, t * 128:(t + 1) * 128],
                                              ident)
                        return ins
                    sch.op("pe", trb, reads=["ident"] + [("fin", q4 * 4 + kk) for kk in range(4)],
                           writes=[("ps", b)])
                    sch.op("act" if q4 % 2 == 0 else "dve",
                           (lambda e, ot=ot, q4=q4, b=b: e.copy(out=ot[:, q4 * 512:(q4 + 1) * 512], in_=bank(b)))
                           if q4 % 2 == 0 else
                           (lambda e, ot=ot, q4=q4, b=b: e.tensor_copy(out=ot[:, q4 * 512:(q4 + 1) * 512], in_=bank(b))),
                           reads=[("ps", b)], writes=[("ot", oi % 2, q4)])
                r0 = tt * 512 + t * 128
                sch.op("sp", lambda e, ot=ot, r0=r0: e.dma_start(out=out_d[r0:r0 + 128, :], in_=ot),
                       reads=[("ot", oi % 2, q4) for q4 in range(4)], writes=[("out", r0)], dma=True)
                oi += 1
        sch.barrier()
        sch.op("sp", lambda e: None)
        sch.emit()
    return nc


def _vec_pc(v):
    return np.ascontiguousarray(np.asarray(v, np.float32).reshape(16, 128).T)


def _bias_tables(rpb):
    tiles = [(6, 6 + d) for d in range(-2, 3)]
    for m in (0, 1):
        tiles += [(m, p) for p in range(4)]
    for m in (14, 15):
        tiles += [(m, p) for p in range(12, 16)]
    kk = np.arange(128)
    ki, kc = kk // 64, kk % 64
    qi, qc = kk // 64, kk % 64
    bias = np.zeros((128, 8, 21, 128), np.float32)
    mask = np.zeros((128, 21, 128), np.float32)
    for t, (m, p) in enumerate(tiles):
        kr = (2 * p + ki)[:, None]
        r = (2 * m + qi)[None, :]
        rs = np.clip(r - 4, 0, 24)
        okr = (kr >= rs) & (kr < rs + 8)
        cs = np.clip(qc - 8, 0, 48)[None, :]
        okc = (kc[:, None] >= cs) & (kc[:, None] < cs + 16)
        dr = np.clip(kr - r + 7, 0, 14)
        dc = np.clip(kc[:, None] - qc[None, :], -15, 15) + 15
        bias[:, :, t, :] = rpb[:, dr, dc].transpose(1, 0, 2)
        mask[:, t, :] = (okr & okc)
    return bias, mask


def prep_inputs(inputs):
    f = lambda a: np.ascontiguousarray(np.asarray(a, np.float32))
    x = f(inputs["x"])
    c = f(inputs["c"])
    ctx = f(inputs["ctx"])
    c_ctx = f(inputs["c_ctx"])
    ada_b = f(inputs["ada_b"])
    adabT = np.ascontiguousarray(ada_b.reshape(2, 96, 128).transpose(2, 0, 1))
    vecs = np.stack([_vec_pc(inputs["norm_mix_g"][0]), _vec_pc(inputs["norm_mix_g"][1]),
                     _vec_pc(inputs["norm_ffn_g"][0]), _vec_pc(inputs["norm_ffn_g"][1]),
                     _vec_pc(inputs["final_norm_g"]), _vec_pc(inputs["sg_ln_g"][0]),
                     _vec_pc(inputs["sg_ln_b"][0])], axis=1)
    convw = np.ascontiguousarray(f(inputs["conv_w"])[0].reshape(3, 8, 128).transpose(2, 1, 0))
    bias, mask = _bias_tables(f(inputs["rpb"])[0])
    sgwT = np.ascontiguousarray(f(inputs["sg_w"])[0].transpose(2, 0, 1))
    shared = {
        "ada_w": f(inputs["ada_w"]), "adabT": adabT, "vecs": np.ascontiguousarray(vecs), "convw": convw,
        "ffn_w_gate": f(inputs["ffn_w_gate"]), "ffn_w_up": f(inputs["ffn_w_up"]),
        "ffn_w_down": f(inputs["ffn_w_down"]), "even_w_in": f(inputs["even_w_in"])[0],
        "even_w_out": f(inputs["even_w_out"])[0], "biastab": bias, "masktab": mask,
        "odd_w_in": f(inputs["odd_w_in"])[0], "odd_w_out": f(inputs["odd_w_out"])[0],
        "sgwT": sgwT, "sgb": f(inputs["sg_b"])[0], "ident": np.eye(128, dtype=np.float32),
    }
    in_maps = []
    for b in range(x.shape[0]):
        cT = np.stack([_vec_pc(c[b]), _vec_pc(c_ctx)], axis=2)
        m = dict(shared)
        m["x"] = x[b]
        m["ctx"] = ctx[b]
        m["cT"] = np.ascontiguousarray(cT)
        in_maps.append(m)
    return in_maps


_NC_CACHE = {}


def kernel(**inputs):
    in_maps = prep_inputs(inputs)
    if "nc" not in _NC_CACHE:
        _NC_CACHE["nc"] = build()
    nc = _NC_CACHE["nc"]
    res = run_bass_kernel_spmd(nc, in_maps, core_ids=list(range(8)))
    return np.stack([r["out"] for r in res.results], axis=0).astype(np.float32)
```

```python
import math
import os
from contextlib import ExitStack
import numpy as np
import concourse.bass as bass
import concourse.mybir as mybir
from concourse.bass_utils import run_bass_kernel_spmd

F32 = mybir.dt.float32
BF16 = mybir.dt.bfloat16
AF = mybir.ActivationFunctionType
ALU = mybir.AluOpType
KB = 1024


class _Op:
    __slots__ = ("eng", "fn", "deps", "dma", "needed", "sem", "val", "idx")

    def __init__(self, eng, fn, dma):
        self.eng = eng
        self.fn = fn
        self.dma = dma
        self.deps = set()
        self.needed = False
        self.sem = None
        self.val = None


class Sched:
    def __init__(self, nc, stack, n_dma_sems=20):
        self.nc = nc
        self.engs = {"pe": nc.tensor, "act": nc.scalar, "dve": nc.vector,
                     "pool": nc.gpsimd, "sp": nc.sync}
        self.ops = []
        self.lw = {}
        self.rd = {}
        self.stack = stack
        self.n_dma_sems = n_dma_sems
        self.last_eng_op = {}
        self.dmas_since_barrier = []
        self.pending_barrier = {}

    def op(self, eng, fn, reads=(), writes=(), dma=False, exempt=False):
        o = _Op(eng, fn, dma)
        o.idx = len(self.ops)
        for k in reads:
            w = self.lw.get(k)
            if w is not None:
                o.deps.add(w)
        for k in writes:
            w = self.lw.get(k)
            if w is not None:
                o.deps.add(w)
            for r in self.rd.get(k, ()):
                o.deps.add(r)
        if not exempt and eng in self.pending_barrier:
            for d in self.pending_barrier.pop(eng):
                o.deps.add(d)
        o.deps.discard(o)
        for k in reads:
            self.rd.setdefault(k, []).append(o)
        for k in writes:
            self.lw[k] = o
            self.rd[k] = []
        self.ops.append(o)
        if dma:
            self.dmas_since_barrier.append(o)
        else:
            self.last_eng_op[eng] = o
        return o

    def barrier(self):
        tails = list(self.last_eng_op.values()) + list(self.dmas_since_barrier)
        self.dmas_since_barrier = []
        for e in self.engs:
            prev = self.pending_barrier.get(e, [])
            self.pending_barrier[e] = prev + tails

    def emit(self):
        nc = self.nc
        for o in self.ops:
            for d in o.deps:
                if (not d.dma) and d.eng == "pe" and o.eng == "pe" and not o.dma:
                    continue
                d.needed = True
        esem = {e: self.stack.enter_context(nc.semaphore("s_" + e)) for e in self.engs}
        dsem = {e: [self.stack.enter_context(nc.semaphore("d_%s%d" % (e, i)))
                    for i in range(self.n_dma_sems)] for e in ("sp", "pool", "act")}
        dcount = {e: [0] * self.n_dma_sems for e in dsem}
        drr = {e: 0 for e in dsem}
        ecount = {e: 0 for e in self.engs}
        seen = {e: {} for e in self.engs}

        def need(e, sem, val):
            key = id(sem)
            if seen[e].get(key, 0) >= val:
                return
            seen[e][key] = val
            self.engs[e].wait_ge(sem, val)

        for o in self.ops:
            e = o.eng
            if o.dma:
                i = drr[e]
                drr[e] = (i + 1) % self.n_dma_sems
                o.sem = dsem[e][i]
                dcount[e][i] += 16
                o.val = dcount[e][i]
                if o.val > 16:
                    need(e, o.sem, o.val - 16)
            for d in sorted(o.deps, key=lambda x: x.idx):
                if (not d.dma) and d.eng == "pe" and e == "pe" and not o.dma:
                    continue
                need(e, d.sem, d.val)
            ins = o.fn(self.engs[e])
            if ins is None:
                assert not o.needed and not o.dma
                continue
            if o.dma:
                ins.then_inc(o.sem, 16)
            elif o.needed:
                ecount[e] += 1
                o.sem = esem[e]
                o.val = ecount[e]
                ins.then_inc(o.sem, 1)
        self.ecount = ecount


D = 2048
S = 2048
CTX = 256
FF = 5632
NF = FF // 128
SCALE = 128 ** -0.5
EPS = 1e-6

OFF_MISC = 0
OFF_XRES = 8 * KB
OFF_W = 16 * KB
OFF_HX = 40 * KB
OFF_HXC = 104 * KB
OFF_AT = 112 * KB
OFF_SCR = 176 * KB
ARENA_BYTES = 207 * KB
WG_BYTES = 12 * KB


def pairs_of(m):
    if m <= 1:
        return [0, 1, 2, 3]
    if m >= 14:
        return [12, 13, 14, 15]
    return list(range(m - 2, m + 3))


def tile_base(m):
    if m == 0:
        return 5
    if m == 1:
        return 9
    if m == 14:
        return 13
    if m == 15:
        return 17
    return 0


def build(stop=None):
    nc = bass.Bass("TRN2", target_bir_lowering=False)

    def din(name, shape, dt=F32):
        return nc.dram_tensor(name, list(shape), dt, kind="ExternalInput").ap()

    x_d = din("x", [S, D])
    ctx_d = din("ctx", [CTX, D])
    cT_d = din("cT", [128, 16, 2])
    ada_w = din("ada_w", [2, D, 6 * D])
    adabT_d = din("adabT", [128, 2, 96])
    vecs_d = din("vecs", [128, 7, 16])
    convw_d = din("convw", [128, 8, 3])
    w_gate = din("ffn_w_gate", [2, D, FF])
    w_up = din("ffn_w_up", [2, D, FF])
    w_down = din("ffn_w_down", [2, FF, D])
    e_win = din("even_w_in", [D, 6144])
    e_wout = din("even_w_out", [D, D])
    bias_d = din("biastab", [128, 8, 21, 128])
    mask_d = din("masktab", [128, 21, 128])
    o_win = din("odd_w_in", [D, 4096])
    o_wout = din("odd_w_out", [D, D])
    sgwT_d = din("sgwT", [128, 16, 128])
    sgb_d = din("sgb", [16, 128])
    ident_d = din("ident", [128, 128])
    out_d = nc.dram_tensor("out", [S, D], F32, kind="ExternalOutput").ap()
    dbg_d = None
    if stop is not None:
        dbg_d = nc.dram_tensor("dbg", [16, 128, 2304], F32, kind="ExternalOutput").ap()

    xT_d = nc.dram_tensor("xT_scr", [16, 128, S], F32).ap()
    modrow_d = nc.dram_tensor("modrow_scr", [2, 2, 6 * D], F32).ap()
    etab_d = nc.dram_tensor("etab_scr", [8, 128, 21 * 128], BF16).ap()

    with ExitStack() as st:
        sch = Sched(nc, st)
        arena = st.enter_context(nc.sbuf_tensor("arena", [128, ARENA_BYTES // 2], BF16))
        PS = [st.enter_context(nc.psum_tensor("ps%d" % i, [128, 1024], F32)) for i in range(4)]

        def bank(i):
            return PS[i // 2][:, (i % 2) * 512:(i % 2) * 512 + 512]

        def T(off, dt, *shape):
            n = 1
            for s_ in shape:
                n *= s_
            esz = 2 if dt == BF16 else 4
            a = arena[:, off // 2: off // 2 + n * esz // 2]
            if dt != BF16:
                a = a.bitcast(dt)
            if len(shape) == 2:
                a = a.rearrange("p (a b) -> p a b", a=shape[0])
            elif len(shape) == 3:
                a = a.rearrange("p (a b c) -> p a b c", a=shape[0], b=shape[1])
            return a

        class Bump:
            def __init__(self, base, size):
                self.base = base
                self.size = size
                self.cur = base

            def reset(self):
                self.cur = self.base

            def alloc(self, dt, *shape):
                n = 1
                for s_ in shape:
                    n *= s_
                nb = n * (2 if dt == BF16 else 4)
                nb = (nb + 31) // 32 * 32
                off = self.cur
                self.cur += nb
                assert self.cur <= self.base + self.size, ("bump overflow", self.cur - self.base, self.size)
                return T(off, dt, *shape)

        misc = Bump(OFF_MISC, 8 * KB)
        scr = Bump(OFF_SCR, 31 * KB)
        atr = Bump(OFF_AT, 64 * KB)

        ident = misc.alloc(F32, 128)
        ones = misc.alloc(BF16, 128)
        modsT = misc.alloc(F32, 2, 96)
        modsC = misc.alloc(F32, 32)
        adabT = misc.alloc(F32, 2, 96)
        vecs = misc.alloc(F32, 7, 16)
        Aco = misc.alloc(F32, 4, 16)
        AcoC = misc.alloc(F32, 16)
        convw = misc.alloc(F32, 8, 3)
        epsT = misc.alloc(F32, 1)
        cT = misc.alloc(F32, 16, 2)
        s2 = misc.alloc(BF16, 16, 2)
        st1 = misc.alloc(F32, 8, 8)
        st2 = misc.alloc(F32, 8, 8)
        lnmu = misc.alloc(F32, 8)
        lnex = misc.alloc(F32, 8)
        lnrs = misc.alloc(F32, 8)
        lnnm = misc.alloc(F32, 8)

        XRES = [T(OFF_XRES + i * 2 * KB, F32, 512) for i in range(4)]
        HX = T(OFF_HX, BF16, 16, 2048)
        HXC = T(OFF_HXC, BF16, 16, 256)
        AT = T(OFF_AT, BF16, 16, 2048)

        def hx_rhs(k, lo, n):
            if lo >= 2048:
                return HXC[:, k, lo - 2048: lo - 2048 + n]
            return HX[:, k, lo: lo + n]

        def hx_keys(lo, n):
            return [("HX", k, t) for k in range(16) for t in range(lo // 128, (lo + n) // 128)]

        wstate = {"i": 0}

        def wslot():
            g = wstate["i"] % 2
            wstate["i"] += 1
            return g

        def WT(g, off, kc, n):
            return T(OFF_W + g * WG_BYTES + off, BF16, kc, n)

        def load_w(g, off, src2d, kc, n, key_extra=0):
            t = WT(g, off, kc, n)
            src = src2d.rearrange("(kc p) n -> p kc n", p=128)
            nbytes = kc * n * 2
            keys = [("W", g, s_) for s_ in range(off // (4 * KB), (off + nbytes + 4 * KB - 1) // (4 * KB))]
            sch.op("pool", lambda e: e.dma_start(out=t, in_=src), writes=keys, dma=True, exempt=True)
            wkeys[id(t)] = keys
            return t

        wkeys = {}

        dve_copy = lambda out, in_: (lambda e: e.tensor_copy(out=out, in_=in_))

        sch.op("sp", lambda e: e.dma_start(out=ident, in_=ident_d), writes=["ident"], dma=True)
        sch.op("sp", lambda e: e.dma_start(out=adabT, in_=adabT_d), writes=["adabT"], dma=True)
        sch.op("sp", lambda e: e.dma_start(out=vecs, in_=vecs_d), writes=["vecs"], dma=True)
        sch.op("sp", lambda e: e.dma_start(out=convw, in_=convw_d), writes=["convw"], dma=True)
        sch.op("sp", lambda e: e.dma_start(out=cT, in_=cT_d), writes=["cT"], dma=True)
        sch.op("dve", lambda e: e.memset(ones, 1.0), writes=["ones"])
        sch.op("dve", lambda e: e.memset(epsT, EPS), writes=["eps"])
        sch.op("act", lambda e: e.activation(out=s2, in_=cT, func=AF.Silu), reads=["cT"], writes=["s2"])

        def ada(layer, jlist, with_ctx, only=None, fin=True):
            rowst = [scr_rows[i] for i in range(4)]
            cnt = 0
            for j in jlist:
                for n in range(8):
                    if only is not None and (j * 8 + n) not in only:
                        cnt += 1
                        continue
                    col0 = j * 2048 + n * 256
                    g = wslot()
                    wt = load_w(g, 0, ada_w[layer][:, col0:col0 + 256], 16, 256)
                    b = cnt % 2
                    psb = bank(b)

                    def mm(e, wt=wt, psb=psb):
                        for k in range(16):
                            ins = e.matmul(psb[0:2, 0:256], lhsT=s2[:, k, :], rhs=wt[:, k, :],
                                           start=(k == 0), stop=(k == 15))
                        return ins
                    sch.op("pe", mm, reads=wkeys[id(wt)] + ["s2"], writes=[("ps", b)])
                    rt = rowst[cnt % 4]
                    sch.op("dve", dve_copy(rt[0:2, :], psb[0:2, 0:256]), reads=[("ps", b)],
                           writes=[("row", cnt % 4)])
                    sch.op("sp", lambda e, rt=rt, col0=col0: e.dma_start(
                        out=modrow_d[layer, :, col0:col0 + 256], in_=rt[0:2, :]),
                        reads=[("row", cnt % 4)], writes=[("modrow", layer, j)], dma=True)
                    cnt += 1
            if not fin:
                return
            for j in jlist:
                src = modrow_d[layer, 0, j * 2048:(j + 1) * 2048].rearrange("(c p) -> p c", p=128)
                sch.op("sp", lambda e, src=src, j=j: e.dma_start(
                    out=modsT[:, layer, j * 16:(j + 1) * 16], in_=src, allow_slow_non_contiguous=True),
                    reads=[("modrow", layer, j)], writes=[("modsT", layer, j)], dma=True)
                sch.op("dve", lambda e, j=j: e.tensor_tensor(
                    out=modsT[:, layer, j * 16:(j + 1) * 16], in0=modsT[:, layer, j * 16:(j + 1) * 16],
                    in1=adabT[:, layer, j * 16:(j + 1) * 16], op=ALU.add),
                    reads=[("modsT", layer, j), "adabT"], writes=[("modsT", layer, j)])
                if with_ctx and j < 2:
                    srcc = modrow_d[layer, 1, j * 2048:(j + 1) * 2048].rearrange("(c p) -> p c", p=128)
                    sch.op("sp", lambda e, srcc=srcc, j=j: e.dma_start(
                        out=modsC[:, j * 16:(j + 1) * 16], in_=srcc, allow_slow_non_contiguous=True),
                        reads=[("modrow", layer, j)], writes=[("modsC", j)], dma=True)
                    sch.op("dve", lambda e, j=j: e.tensor_tensor(
                        out=modsC[:, j * 16:(j + 1) * 16], in0=modsC[:, j * 16:(j + 1) * 16],
                        in1=adabT[:, layer, j * 16:(j + 1) * 16], op=ALU.add),
                        reads=[("modsC", j), "adabT"], writes=[("modsC", j)])

        def mk_aco(layer, which):
            gi = (0 if which == 0 else 2) + layer
            j = 1 if which == 0 else 4
            sch.op("dve", lambda e: e.scalar_tensor_tensor(
                out=Aco[:, layer * 2 + which, :], in0=modsT[:, layer, j * 16:(j + 1) * 16], scalar=1.0,
                in1=vecs[:, gi, :], op0=ALU.add, op1=ALU.mult),
                reads=[("modsT", layer, j), "vecs"], writes=[("Aco", layer, which)])

        scr.reset()
        scr_rows = [scr.alloc(F32, 256) for _ in range(4)]
        mk = atr.alloc(F32, 21 * 128)
        bst = [atr.alloc(F32, 21 * 128) for _ in range(1)]
        eo = [atr.alloc(BF16, 21 * 128) for _ in range(2)]
        sch.op("sp", lambda e: e.dma_start(out=mk, in_=mask_d.rearrange("p a b -> p (a b)")), writes=["mk"],
               dma=True)
        for h in range(8):
            sch.op("sp", lambda e, h=h: e.dma_start(out=bst[0], in_=bias_d[:, h].rearrange("p a b -> p (a b)")),
                   writes=["bst"], dma=True)
            sch.op("act", lambda e: e.activation(out=bst[0], in_=bst[0], func=AF.Exp), reads=["bst"],
                   writes=["bst"])
            sch.op("dve", lambda e, h=h: e.tensor_tensor(out=eo[h % 2], in0=bst[0], in1=mk, op=ALU.mult),
                   reads=["bst", "mk"], writes=[("eo", h % 2)])
            sch.op("sp", lambda e, h=h: e.dma_start(out=etab_d[h], in_=eo[h % 2]), reads=[("eo", h % 2)],
                   writes=[("etab", h)], dma=True)
            ada(0, [0, 1], True, only=(2 * h, 2 * h + 1), fin=False)
        ada(0, [0, 1], True, only=())
        mk_aco(0, 0)
        sch.op("dve", lambda e: e.scalar_tensor_tensor(
            out=AcoC, in0=modsC[:, 16:32], scalar=1.0, in1=vecs[:, 0, :], op0=ALU.add, op1=ALU.mult),
            reads=[("modsC", 1), "vecs"], writes=["AcoC"])
        ADAW = [T(OFF_XRES + i * 4 * KB, BF16, 16, 128) for i in range(2)]
        ROWS = [T(OFF_SCR + 30 * KB + i * 512, F32, 128) for i in range(2)]
        ada_todo = [(l_, j_, n_) for (l_, j_) in [(0, 2), (0, 3), (0, 4), (0, 5), (1, 0), (1, 1), (1, 2), (1, 3),
                                                (1, 4), (1, 5)] for n_ in range(16)]
        ada_state = {"t": 0}

        def ada_step():
            if os.environ.get("NOADA", "") == "1" and not ada_state.get("flush"):
                return False
            t = ada_state["t"]
            if t >= len(ada_todo):
                return False
            ada_state["t"] = t + 1
            layer, j, n = ada_todo[t]
            i = t % 2
            col0 = j * 2048 + n * 128
            wt = ADAW[i]
            src = ada_w[layer][:, col0:col0 + 128].rearrange("(kc p) n -> p kc n", p=128)
            sch.op("pool", lambda e: e.dma_start(out=wt, in_=src), writes=[("adaw", i)], dma=True, exempt=True)
            psa = bank(6 + i)[0:2, 256:384]

            def mm(e):
                for k in range(16):
                    ins = e.matmul(psa, lhsT=s2[:, k, :], rhs=wt[:, k, :], start=(k == 0), stop=(k == 15))
                return ins
            sch.op("pe", mm, reads=[("adaw", i), "s2"], writes=[("ps", 6 + i)])
            rt = ROWS[i]
            sch.op("act", lambda e: e.copy(out=rt[0:2, :], in_=psa), reads=[("ps", 6 + i)], writes=[("arow", i)])
            sch.op("sp", lambda e: e.dma_start(out=modrow_d[layer, :, col0:col0 + 128], in_=rt[0:2, :]),
                   reads=[("arow", i)], writes=[("modrow", layer, j)], dma=True)
            if n == 15:
                srcm = modrow_d[layer, 0, j * 2048:(j + 1) * 2048].rearrange("(c p) -> p c", p=128)
                sch.op("sp", lambda e: e.dma_start(
                    out=modsT[:, layer, j * 16:(j + 1) * 16], in_=srcm, allow_slow_non_contiguous=True),
                    reads=[("modrow", layer, j)], writes=[("modsT", layer, j)], dma=True)
                sch.op("dve", lambda e: e.tensor_tensor(
                    out=modsT[:, layer, j * 16:(j + 1) * 16], in0=modsT[:, layer, j * 16:(j + 1) * 16],
                    in1=adabT[:, layer, j * 16:(j + 1) * 16], op=ALU.add),
                    reads=[("modsT", layer, j), "adabT"], writes=[("modsT", layer, j)])
                if j == 1:
                    mk_aco(layer, 0)
                if j == 4:
                    mk_aco(layer, 1)
            return True

        sch.barrier()

        def norm_tile(XT, xkeys, ntok, aco, bvec, out_fn, okeys_fn, nb, final=False, fin_out=None):
            sq, rstd, tmp = nb
            pss = bank(6)
            for k in range(16):
                sb = sq[k % 2]
                sch.op("act", lambda e, k=k, sb=sb: e.activation(out=sb[:, 0:ntok], in_=XT[:, k, 0:ntok],
                                                                func=AF.Square),
                       reads=[xkeys[k]], writes=[("sq", k % 2)])
                sch.op("pe", lambda e, k=k, sb=sb: e.matmul(pss[:, 0:ntok], lhsT=ones, rhs=sb[:, 0:ntok],
                                                            start=(k == 0), stop=(k == 15)),
                       reads=[("sq", k % 2), "ones"], writes=[("ps", 6)])
            sch.op("act", lambda e: e.activation(out=rstd[:, 0:ntok], in_=pss[:, 0:ntok], func=AF.Sqrt,
                                                 scale=1.0 / D, bias=epsT),
                   reads=[("ps", 6), "eps"], writes=["rstd"])
            sch.op("dve", lambda e: e.reciprocal(out=rstd[:, 0:ntok], in_=rstd[:, 0:ntok]),
                   reads=["rstd"], writes=["rstd"])
            for k in range(16):
                if final:
                    sch.op("dve", lambda e, k=k: e.scalar_tensor_tensor(
                        out=fin_out[:, k, 0:ntok], in0=XT[:, k, 0:ntok], scalar=aco[:, k:k + 1],
                        in1=rstd[:, 0:ntok], op0=ALU.mult, op1=ALU.mult),
                        reads=[xkeys[k], "rstd"] + aco_keys[0], writes=[("fin", k)])
                    continue
                tb = tmp[k % 2]
                sch.op("dve", lambda e, k=k, tb=tb: e.scalar_tensor_tensor(
                    out=tb[:, 0:ntok], in0=XT[:, k, 0:ntok], scalar=aco[:, k:k + 1], in1=rstd[:, 0:ntok],
                    op0=ALU.mult, op1=ALU.mult),
                    reads=[xkeys[k], "rstd"] + aco_keys[0], writes=[("ntmp", k % 2)])
                sch.op("act", lambda e, k=k, tb=tb: e.activation(out=out_fn(k), in_=tb[:, 0:ntok],
                                                                func=AF.Identity, bias=bvec[:, k:k + 1]),
                       reads=[("ntmp", k % 2)] + aco_keys[0], writes=okeys_fn(k))

        aco_keys = [[]]

        def norm_bufs():
            sq = [scr.alloc(BF16, 512) for _ in range(2)]
            rstd = scr.alloc(F32, 512)
            tmp = [scr.alloc(F32, 512) for _ in range(2)]
            return sq, rstd, tmp

        def norm_phase(layer, which):
            scr.reset()
            nb = norm_bufs()
            XTs = [T(OFF_AT + i * 32 * KB, F32, 16, 512) for i in range(2)]
            j_shift = 0 if which == 0 else 3
            aco = Aco[:, layer * 2 + which, :]
            bvec = modsT[:, layer, j_shift * 16:(j_shift + 1) * 16]
            aco_keys[0] = [("Aco", layer, which), ("modsT", layer, j_shift)]
            for tt in range(4):
                XT = XTs[tt % 2]
                for k in range(16):
                    sch.op("sp", lambda e, k=k, tt=tt, XT=XT: e.dma_start(
                        out=XT[:, k, :], in_=xT_d[k, :, tt * 512:(tt + 1) * 512]),
                        reads=[("xT", k, tt)], writes=[("XT", tt % 2, k)], dma=True)
                norm_tile(XT, [("XT", tt % 2, k) for k in range(16)], 512, aco, bvec,
                          lambda k, tt=tt: HX[:, k, tt * 512:(tt + 1) * 512],
                          lambda k, tt=tt: [("HX", k, t) for t in range(tt * 4, tt * 4 + 4)], nb)
            sch.barrier()

        scr.reset()
        nbA = norm_bufs()
        xin = T(OFF_AT, F32, 4, 2048)
        XTa = T(OFF_AT + 32 * KB, F32, 16, 512)
        aco_keys[0] = [("Aco", 0, 0), ("modsT", 0, 0)]
        for tt in range(5):
            nblk = 4 if tt < 4 else 2
            ntok = nblk * 128
            srcd = x_d if tt < 4 else ctx_d
            for t in range(nblk):
                r0 = (tt * 512 if tt < 4 else 0) + t * 128
                sch.op("sp", lambda e, t=t, r0=r0, srcd=srcd: e.dma_start(out=xin[:, t, :],
                                                                          in_=srcd[r0:r0 + 128, :]),
                       writes=[("xin", t)], dma=True)
            for k in range(16):
                b = k % 2
                psb = bank(b)

                def tr(e, k=k, psb=psb, nblk=nblk):
                    for t in range(nblk):
                        ins = e.transpose(psb[:, t * 128:(t + 1) * 128], xin[:, t, k * 128:(k + 1) * 128], ident)
                    return ins
                sch.op("pe", tr, reads=["ident"] + [("xin", t) for t in range(nblk)], writes=[("ps", b)])
                sch.op("dve" if k % 2 == 0 else "act",
                       (lambda e, k=k, psb=psb, ntok=ntok: e.tensor_copy(out=XTa[:, k, 0:ntok], in_=psb[:, 0:ntok]))
                       if k % 2 == 0 else
                       (lambda e, k=k, psb=psb, ntok=ntok: e.copy(out=XTa[:, k, 0:ntok], in_=psb[:, 0:ntok])),
                       reads=[("ps", b)], writes=[("XTa", k)])
                if tt < 4:
                    sch.op("sp", lambda e, k=k, tt=tt: e.dma_start(out=xT_d[k, :, tt * 512:(tt + 1) * 512],
                                                                   in_=XTa[:, k, :]),
                           reads=[("XTa", k)], writes=[("xT", k, tt)], dma=True)
            if tt < 4:
                norm_tile(XTa, [("XTa", k) for k in range(16)], 512, Aco[:, 0, :], modsT[:, 0, 0:16],
                          lambda k, tt=tt: HX[:, k, tt * 512:(tt + 1) * 512],
                          lambda k, tt=tt: [("HX", k, t) for t in range(tt * 4, tt * 4 + 4)], nbA)
            else:
                aco_keys[0] = ["AcoC", ("modsC", 0)]
                norm_tile(XTa, [("XTa", k) for k in range(16)], 256, AcoC, modsC[:, 0:16],
                          lambda k: HXC[:, k, :],
                          lambda k: [("HX", k, 16), ("HX", k, 17)], nbA)
        sch.barrier()

        def dump(kind):
            if kind == "HX":
                st_ = scr.alloc(F32, 2304) if False else None
                for k in range(16):
                    for tt in range(5):
                        n = 512 if tt < 4 else 256
                        xb = XRES[(k * 5 + tt) % 4]
                        srcv = HX[:, k, tt * 512:(tt + 1) * 512] if tt < 4 else HXC[:, k, :]
                        sch.op("dve", lambda e, xb=xb, srcv=srcv, n=n: e.tensor_copy(out=xb[:, 0:n], in_=srcv),
                               reads=hx_keys(tt * 512, n), writes=[("xres", (k * 5 + tt) % 4)])
                        sch.op("sp", lambda e, xb=xb, k=k, tt=tt, n=n: e.dma_start(
                            out=dbg_d[k, :, tt * 512:tt * 512 + n], in_=xb[:, 0:n]),
                            reads=[("xres", (k * 5 + tt) % 4)], writes=[("dbg", k, tt)], dma=True)
            elif kind == "AT":
                for k in range(16):
                    for tt in range(4):
                        xb = XRES[(k * 4 + tt) % 4]
                        sch.op("dve", lambda e, xb=xb, k=k, tt=tt: e.tensor_copy(
                            out=xb, in_=AT[:, k, tt * 512:(tt + 1) * 512]),
                            reads=[("AT", k, t) for t in range(16)], writes=[("xres", (k * 4 + tt) % 4)])
                        sch.op("sp", lambda e, xb=xb, k=k, tt=tt: e.dma_start(
                            out=dbg_d[k, :, tt * 512:(tt + 1) * 512], in_=xb),
                            reads=[("xres", (k * 4 + tt) % 4)], writes=[("dbg", k, tt)], dma=True)
            elif kind == "xT":
                for k in range(16):
                    for tt in range(4):
                        xb = XRES[(k * 4 + tt) % 4]
                        sch.op("sp", lambda e, xb=xb, k=k, tt=tt: e.dma_start(
                            out=xb, in_=xT_d[k, :, tt * 512:(tt + 1) * 512]),
                            reads=[("xT", k, tt)], writes=[("xres", (k * 4 + tt) % 4)], dma=True)
                        sch.op("sp", lambda e, xb=xb, k=k, tt=tt: e.dma_start(
                            out=dbg_d[k, :, tt * 512:(tt + 1) * 512], in_=xb),
                            reads=[("xres", (k * 4 + tt) % 4)], writes=[("dbg", k, tt)], dma=True)
            sch.barrier()
            sch.op("sp", lambda e: None)
            sch.emit()

        if stop == "A":
            dump("HX")
            return nc

        def gemm(psb, pskey, wt, wkey, K, rhs_fn, rkeys):
            def f(e):
                for k in range(K):
                    ins = e.matmul(psb, lhsT=wt[:, k, :], rhs=rhs_fn(k), start=(k == 0), stop=(k == K - 1))
                return ins
            sch.op("pe", f, reads=wkeys[id(wt)] + rkeys, writes=[pskey])

        scr.reset()
        qT = scr.alloc(BF16, 2048)
        kT = scr.alloc(BF16, 2304)
        Vh = scr.alloc(BF16, 18, 128)
        Et = scr.alloc(BF16, 21, 128)
        PTf = [scr.alloc(F32, 640) for _ in range(2)]
        PTb = [scr.alloc(BF16, 896) for _ in range(2)]
        rZ = [scr.alloc(F32, 128) for _ in range(2)]
        pre = {}

        def load_head(h):
            g = wslot()
            return (load_w(g, 0, e_win[:, h * 128:(h + 1) * 128], 16, 128, 0),
                    load_w(g, 4 * KB, e_win[:, 1024 + h * 128:1024 + (h + 1) * 128], 16, 128, 1),
                    load_w(g, 8 * KB, e_win[:, 2048 + h * 128:2048 + (h + 1) * 128], 16, 128, 2))

        def load_conv(j):
            g = wslot()
            return (load_w(g, 0, e_win[:, 3072 + j * 128:3072 + (j + 1) * 128], 16, 128, 0),
                    load_w(g, 4 * KB, e_win[:, 4096 + j * 128:4096 + (j + 1) * 128], 16, 128, 1),
                    load_w(g, 8 * KB, e_win[:, 5120 + j * 128:5120 + (j + 1) * 128], 16, 128, 2))

        for h in range(8):
            if h == 0:
                pre[("head", 0)] = load_head(0)
            wq, wk, wv = pre.pop(("head", h))
            if h + 1 < 8:
                pre[("head", h + 1)] = load_head(h + 1)
            else:
                pre[("conv", 0)] = load_conv(0)
            sch.op("sp", lambda e, h=h: e.dma_start(out=Et.rearrange("p a b -> p (a b)"), in_=etab_d[h]),
                   reads=[("etab", h)], writes=["Et"], dma=True)
            gi = 0
            for tt in range(4):
                b = gi % 2
                gi += 1
                gemm(bank(b), ("ps", b), wq, None, 16, lambda k, tt=tt: hx_rhs(k, tt * 512, 512),
                     hx_keys(tt * 512, 512))
                sch.op("act", lambda e, b=b, tt=tt: e.activation(out=qT[:, tt * 512:(tt + 1) * 512], in_=bank(b),
                                                                func=AF.Identity, scale=SCALE),
                       reads=[("ps", b)], writes=[("qT", tt)])
                if gi % 2 == 0:
                    ada_step()
            for tt in range(5):
                n = 512 if tt < 4 else 256
                b = gi % 2
                gi += 1
                gemm(bank(b)[:, 0:n], ("ps", b), wk, None, 16, lambda k, tt=tt, n=n: hx_rhs(k, tt * 512, n),
                     hx_keys(tt * 512, n))
                sch.op("dve", lambda e, b=b, tt=tt, n=n: e.tensor_copy(out=kT[:, tt * 512:tt * 512 + n],
                                                                       in_=bank(b)[:, 0:n]),
                       reads=[("ps", b)], writes=[("kT", tt)])
                if gi % 2 == 0:
                    ada_step()
            for vb in range(5):
                nblk = 4 if vb < 4 else 2
                b = gi % 2
                gi += 1

                def vmm(e, vb=vb, nblk=nblk, b=b, wv=wv):
                    for t in range(nblk):
                        blk = vb * 4 + t
                        for k in range(16):
                            ins = e.matmul(bank(b)[:, t * 128:(t + 1) * 128],
                                           lhsT=hx_rhs(k, blk * 128, 128), rhs=wv[:, k, :],
                                           start=(k == 0), stop=(k == 15))
                    return ins
                sch.op("pe", vmm, reads=wkeys[id(wv)] + hx_keys(vb * 512, nblk * 128), writes=[("ps", b)])
                sch.op("act" if vb % 2 == 0 else "dve",
                       (lambda e, vb=vb, nblk=nblk, b=b: e.copy(
                           out=Vh[:, vb * 4:vb * 4 + nblk, :],
                           in_=bank(b)[:, 0:nblk * 128].rearrange("p (a b) -> p a b", a=nblk)))
                       if vb % 2 == 0 else
                       (lambda e, vb=vb, nblk=nblk, b=b: e.tensor_copy(
                           out=Vh[:, vb * 4:vb * 4 + nblk, :],
                           in_=bank(b)[:, 0:nblk * 128].rearrange("p (a b) -> p a b", a=nblk))),
                       reads=[("ps", b)], writes=[("Vh", vb)])
                if vb % 2 == 0:
                    ada_step()
            if stop == "Q" and h == 0:
                for idx_, (srcT, n_) in enumerate([(qT, 2048), (kT, 2304), (Vh.rearrange("p a b -> p (a b)"), 2304)]):
                    for c0_ in range(0, n_, 512):
                        w_ = min(512, n_ - c0_)
                        xb = XRES[(idx_ * 5 + c0_ // 512) % 4]
                        kk_ = (idx_ * 5 + c0_ // 512) % 4
                        sch.op("dve", lambda e, xb=xb, srcT=srcT, c0_=c0_, w_=w_: e.tensor_copy(
                            out=xb[:, 0:w_], in_=srcT[:, c0_:c0_ + w_]),
                            reads=[("qT", t) for t in range(4)] + [("kT", t) for t in range(5)] + [("Vh", t) for t in range(5)],
                            writes=[("xres", kk_)])
                        sch.op("sp", lambda e, xb=xb, idx_=idx_, c0_=c0_, w_=w_: e.dma_start(
                            out=dbg_d[idx_, :, c0_:c0_ + w_], in_=xb[:, 0:w_]),
                            reads=[("xres", kk_)], writes=[("dbg", idx_, c0_)], dma=True)
                sch.barrier()
                sch.op("sp", lambda e: None)
                sch.emit()
                return nc
            def stage1(m, h=h):
                pl = pairs_of(m)
                nl = len(pl)
                tb = tile_base(m)
                pb = m % 2
                psS = PS[1 + pb]
                qs = qT[:, m * 128:(m + 1) * 128]

                def qk(e):
                    for c in range(2):
                        ins = e.matmul(psS[:, c * 128:(c + 1) * 128], lhsT=kT[:, 2048 + c * 128:2048 + (c + 1) * 128],
                                       rhs=qs, start=True, stop=True)
                    for i, p in enumerate(pl):
                        ins = e.matmul(psS[:, 256 + i * 128:256 + (i + 1) * 128],
                                       lhsT=kT[:, p * 128:(p + 1) * 128], rhs=qs, start=True, stop=True)
                    return ins
                sch.op("pe", qk, reads=[("qT", m // 4), ("kT", 4)] + [("kT", p // 4) for p in pl],
                       writes=[("psS", pb)])
                sch.op("act", lambda e: e.activation(out=PTb[pb][:, 0:256], in_=psS[:, 0:256], func=AF.Exp),
                       reads=[("psS", pb)], writes=[("PTb", pb, 0)])
                sch.op("act", lambda e: e.activation(out=PTf[pb][:, 0:nl * 128], in_=psS[:, 256:256 + nl * 128],
                                                     func=AF.Exp),
                       reads=[("psS", pb)], writes=[("PTf", pb)])
                sch.op("dve", lambda e: e.tensor_tensor(
                    out=PTb[pb][:, 256:256 + nl * 128], in0=PTf[pb][:, 0:nl * 128],
                    in1=Et[:, tb:tb + nl, :].rearrange("p a b -> p (a b)"), op=ALU.mult),
                    reads=[("PTf", pb), "Et"], writes=[("PTb", pb, 1)])

            def stage2(m, h=h):
                pl = pairs_of(m)
                pb = m % 2
                psO = bank(6 + pb)

                def pv(e):
                    tiles = [(16 + c, c) for c in range(2)] + [(p, 2 + i) for i, p in enumerate(pl)]
                    for i, (vblk, ti) in enumerate(tiles):
                        ins = e.matmul(psO[:, 0:128], lhsT=Vh[:, vblk, :], rhs=PTb[pb][:, ti * 128:(ti + 1) * 128],
                                       start=(i == 0), stop=(i == len(tiles) - 1))
                    for i, (vblk, ti) in enumerate(tiles):
                        ins = e.matmul(psO[:, 128:256], lhsT=ones, rhs=PTb[pb][:, ti * 128:(ti + 1) * 128],
                                       start=(i == 0), stop=(i == len(tiles) - 1))
                    return ins
                sch.op("pe", pv, reads=[("PTb", pb, 0), ("PTb", pb, 1), "ones", ("Vh", 4)] +
                       [("Vh", p // 4) for p in pl], writes=[("ps", 6 + pb)])
                sch.op("dve", lambda e: e.reciprocal(out=rZ[pb], in_=psO[:, 128:256]),
                       reads=[("ps", 6 + pb)], writes=[("rZ", pb)])
                sch.op("dve", lambda e: e.tensor_tensor(
                    out=AT[:, h, m * 128:(m + 1) * 128], in0=psO[:, 0:128], in1=rZ[pb], op=ALU.mult),
                    reads=[("ps", 6 + pb), ("rZ", pb)], writes=[("AT", h, m)])

            skip = os.environ.get("DBGSKIP", "")
            if not (skip == "attn" and h == 1):
                stage1(0)
                for m in range(16):
                    if m + 1 < 16:
                        stage1(m + 1)
                    if not (skip == "s2" and h == 1):
                        stage2(m)
                    if m % 2 == 1:
                        ada_step()
            if stop in ("C0", "C1") and h == int(stop[1]):
                sch.barrier()
                dump("AT")
                return nc
        sch.barrier()

        scr.reset()
        zb = [scr.alloc(F32, 2050) for _ in range(2)]
        ctmp = [scr.alloc(F32, 512) for _ in range(2)]
        ytmp = [scr.alloc(F32, 512) for _ in range(2)]
        for i in range(2):
            sch.op("dve", lambda e, i=i: e.memset(zb[i][:, 0:1], 0.0), writes=[("z", i, 0)])
            sch.op("dve", lambda e, i=i: e.memset(zb[i][:, 2049:2050], 0.0), writes=[("z", i, 3)])
        gi = 0
        for j in range(8):
            wb, wc, wu = pre.pop(("conv", j))
            if j + 1 < 8:
                pre[("conv", j + 1)] = load_conv(j + 1)
            z = zb[j % 2]
            for tt in range(4):
                b = gi % 4
                gi += 1
                gemm(bank(b), ("ps", b), wc, None, 16, lambda k, tt=tt: hx_rhs(k, tt * 512, 512),
                     hx_keys(tt * 512, 512))
                ct = ctmp[tt % 2]
                sch.op("act", lambda e, b=b, ct=ct: e.copy(out=ct, in_=bank(b)), reads=[("ps", b)],
                       writes=[("ctmp", tt % 2)])
                b2 = gi % 4
                gi += 1
                gemm(bank(b2), ("ps", b2), wu, None, 16, lambda k, tt=tt: hx_rhs(k, tt * 512, 512),
                     hx_keys(tt * 512, 512))
                sch.op("dve", lambda e, b2=b2, ct=ct, z=z, tt=tt: e.tensor_tensor(
                    out=z[:, 1 + tt * 512:1 + (tt + 1) * 512], in0=bank(b2), in1=ct, op=ALU.mult),
                    reads=[("ps", b2), ("ctmp", tt % 2)], writes=[("z", j % 2, 1, tt)])
                ada_step()
            zkeys = [("z", j % 2, 0), ("z", j % 2, 3)] + [("z", j % 2, 1, t) for t in range(4)]
            for tt in range(4):
                b = gi % 4
                gi += 1
                gemm(bank(b), ("ps", b), wb, None, 16, lambda k, tt=tt: hx_rhs(k, tt * 512, 512),
                     hx_keys(tt * 512, 512))
                yt = ytmp[tt % 2]
                c0 = 1 + tt * 512
                sch.op("dve", lambda e, z=z, yt=yt, c0=c0, j=j: e.tensor_scalar(
                    out=yt, in0=z[:, c0:c0 + 512], scalar1=convw[:, j, 1:2], scalar2=None, op0=ALU.mult),
                    reads=zkeys + ["convw"], writes=[("ytmp", tt % 2)])
                sch.op("dve", lambda e, z=z, yt=yt, c0=c0, j=j: e.scalar_tensor_tensor(
                    out=yt, in0=z[:, c0 - 1:c0 + 511], scalar=convw[:, j, 0:1], in1=yt, op0=ALU.mult, op1=ALU.add),
                    reads=zkeys + ["convw", ("ytmp", tt % 2)], writes=[("ytmp", tt % 2)])
                sch.op("dve", lambda e, z=z, yt=yt, c0=c0, j=j: e.scalar_tensor_tensor(
                    out=yt, in0=z[:, c0 + 1:c0 + 513], scalar=convw[:, j, 2:3], in1=yt, op0=ALU.mult, op1=ALU.add),
                    reads=zkeys + ["convw", ("ytmp", tt % 2)], writes=[("ytmp", tt % 2)])
                sch.op("dve", lambda e, b=b, yt=yt, j=j, tt=tt: e.tensor_tensor(
                    out=AT[:, 8 + j, tt * 512:(tt + 1) * 512], in0=bank(b), in1=yt, op=ALU.mult),
                    reads=[("ps", b), ("ytmp", tt % 2)], writes=[("AT", 8 + j, 4 * tt + t) for t in range(4)])
                if tt % 2 == 1:
                    ada_step()
        ada_state["flush"] = True
        while ada_step():
            pass
        sch.barrier()
        if stop == "C":
            dump("AT")
            return nc

        def outproj_phase(w2d, layer, jgate, tok0, ntt, at_fn, at_keys_fn):
            gi = 0
            for d in range(16):
                g = wslot()
                wt = load_w(g, 0, w2d[:, d * 128:(d + 1) * 128], 16, 128, 0)
                for sub in range(ntt):
                    tt = tok0 // 512 + sub
                    b = gi % 4
                    xb = XRES[gi % 4]
                    gi += 1
                    sch.op("sp", lambda e, xb=xb, d=d, tt=tt: e.dma_start(
                        out=xb, in_=xT_d[d, :, tt * 512:(tt + 1) * 512]),
                        reads=[("xT", d, tt)], writes=[("xres", (gi - 1) % 4)], dma=True)
                    gemm(bank(b), ("ps", b), wt, ("W", g, 0), 16, lambda k, sub=sub: at_fn(k, sub),
                         at_keys_fn(sub))
                    sch.op("dve", lambda e, xb=xb, b=b, d=d: e.scalar_tensor_tensor(
                        out=xb, in0=bank(b), scalar=modsT[:, layer, jgate * 16 + d:jgate * 16 + d + 1], in1=xb,
                        op0=ALU.mult, op1=ALU.add),
                        reads=[("ps", b), ("xres", (gi - 1) % 4), ("modsT", layer, jgate)],
                        writes=[("xres", (gi - 1) % 4)])
                    sch.op("sp", lambda e, xb=xb, d=d, tt=tt: e.dma_start(
                        out=xT_d[d, :, tt * 512:(tt + 1) * 512], in_=xb),
                        reads=[("xres", (gi - 1) % 4)], writes=[("xT", d, tt)], dma=True)

        outproj_phase(e_wout, 0, 2, 0, 4, lambda k, sub: AT[:, k, sub * 512:(sub + 1) * 512],
                      lambda sub: [("AT", k, 4 * sub + t) for k in range(16) for t in range(4)])
        sch.barrier()
        if stop == "D":
            dump("xT")
            return nc

        def ffn_phase(layer):
            norm_phase(layer, 1)
            HT = T(OFF_AT, BF16, NF, 1024)
            sg = [T(OFF_HXC + i * 2 * KB, F32, 512) for i in range(2)]
            for half in range(2):
                gi = 0
                for f in range(NF):
                    g = wslot()
                    wg_ = load_w(g, 0, w_gate[layer][:, f * 128:(f + 1) * 128], 16, 128, 0)
                    wu_ = load_w(g, 4 * KB, w_up[layer][:, f * 128:(f + 1) * 128], 16, 128, 1)
                    for sub in range(2):
                        lo = half * 1024 + sub * 512
                        b = gi % 4
                        b2 = (gi + 1) % 4
                        gi += 2
                        gemm(bank(b), ("ps", b), wg_, ("W", g, 0), 16, lambda k, lo=lo: hx_rhs(k, lo, 512),
                             hx_keys(lo, 512))
                        gemm(bank(b2), ("ps", b2), wu_, ("W", g, 1), 16, lambda k, lo=lo: hx_rhs(k, lo, 512),
                             hx_keys(lo, 512))
                        sgt = sg[(gi // 2) % 2]
                        sch.op("act", lambda e, b=b, sgt=sgt: e.activation(out=sgt, in_=bank(b), func=AF.Silu),
                               reads=[("ps", b)], writes=[("sg", (gi // 2) % 2)])
                        sch.op("dve", lambda e, b2=b2, sgt=sgt, f=f, sub=sub: e.tensor_tensor(
                            out=HT[:, f, sub * 512:(sub + 1) * 512], in0=bank(b2), in1=sgt, op=ALU.mult),
                            reads=[("ps", b2), ("sg", (gi // 2) % 2)], writes=[("HT", f, sub)])
                gi = 0
                for d in range(16):
                    g = wslot()
                    wd_ = load_w(g, 0, w_down[layer][:, d * 128:(d + 1) * 128], NF, 128, 0)
                    for sub in range(2):
                        tt = half * 2 + sub
                        b = 4 + gi % 2
                        xi = gi % 4
                        xb = XRES[xi]
                        gi += 1
                        sch.op("sp", lambda e, xb=xb, d=d, tt=tt: e.dma_start(
                            out=xb, in_=xT_d[d, :, tt * 512:(tt + 1) * 512]),
                            reads=[("xT", d, tt)], writes=[("xres", xi)], dma=True)
                        gemm(bank(b), ("ps", b), wd_, ("W", g, 0), NF,
                             lambda k, sub=sub: HT[:, k, sub * 512:(sub + 1) * 512],
                             [("HT", f, sub) for f in range(NF)])
                        sch.op("dve", lambda e, xb=xb, b=b, d=d: e.scalar_tensor_tensor(
                            out=xb, in0=bank(b), scalar=modsT[:, layer, 5 * 16 + d:5 * 16 + d + 1], in1=xb,
                            op0=ALU.mult, op1=ALU.add),
                            reads=[("ps", b), ("xres", xi), ("modsT", layer, 5)], writes=[("xres", xi)])
                        sch.op("sp", lambda e, xb=xb, d=d, tt=tt: e.dma_start(
                            out=xT_d[d, :, tt * 512:(tt + 1) * 512], in_=xb),
                            reads=[("xres", xi)], writes=[("xT", d, tt)], dma=True)
            sch.barrier()

        ffn_phase(0)
        if stop == "F0":
            dump("xT")
            return nc

        norm_phase(1, 0)
        scr.reset()
        utmp = [scr.alloc(F32, 512) for _ in range(2)]
        mtmp = [scr.alloc(F32, 512) for _ in range(2)]
        Bt = scr.alloc(F32, 16, 128)
        bsbc = scr.alloc(F32, 16, 128)
        junk = scr.alloc(BF16, 256)
        sgwT = scr.alloc(BF16, 16, 128)
        sch.op("pool", lambda e: e.dma_start(out=sgwT, in_=sgwT_d), writes=["sgwT"], dma=True)
        sch.op("sp", lambda e: e.dma_start(out=bsbc.rearrange("p a b -> p (a b)"),
                                           in_=sgb_d.rearrange("a b -> (a b)").partition_broadcast(128)),
               writes=["bsbc"], dma=True)
        for q4 in range(4):
            def rmm(e, q4=q4):
                return e.matmul(bank(q4), lhsT=ones, rhs=sgwT[:, q4 * 4:(q4 + 1) * 4, :].rearrange("p a b -> p (a b)"),
                                start=True, stop=True)
            sch.op("pe", rmm, reads=["ones", "sgwT"], writes=[("ps", q4)])
            for gg in range(4):
                gidx = q4 * 4 + gg
                sch.op("dve", lambda e, q4=q4, gg=gg, gidx=gidx: e.scalar_tensor_tensor(
                    out=Bt[:, gidx, :], in0=bank(q4)[:, gg * 128:(gg + 1) * 128], scalar=vecs[:, 6, gidx:gidx + 1],
                    in1=bsbc[:, gidx, :], op0=ALU.mult, op1=ALU.add),
                    reads=[("ps", q4), "vecs", "bsbc"], writes=[("Bt", gidx)])
        VH = T(OFF_AT, BF16, 8, 2048)
        ATh = T(OFF_AT + 32 * KB, BF16, 16, 1024)
        for half in range(2):
            gi = 0
            for cg in range(8):
                g = wslot()
                wv_ = load_w(g, 0, o_win[:, 2048 + cg * 256:2048 + (cg + 1) * 256], 16, 256, 0)
                for blk in range(8):
                    tok = half * 1024 + blk * 128
                    b = gi % 4
                    gi += 1

                    def vmm(e, tok=tok, b=b, wv_=wv_):
                        for k in range(16):
                            ins = e.matmul(bank(b)[:, 0:256], lhsT=hx_rhs(k, tok, 128), rhs=wv_[:, k, :],
                                           start=(k == 0), stop=(k == 15))
                        return ins
                    sch.op("pe", vmm, reads=wkeys[id(wv_)] + hx_keys(tok, 128), writes=[("ps", b)])
                    sch.op("act", lambda e, b=b, blk=blk, cg=cg: e.activation(
                        out=VH[:, blk, cg * 256:(cg + 1) * 256], in_=bank(b)[:, 0:256], func=AF.Gelu_apprx_tanh,
                        accum_out=st1[:, blk, cg:cg + 1]),
                        reads=[("ps", b)], writes=[("VH", blk, cg), ("st1", blk, cg)])
                    sch.op("act", lambda e, blk=blk, cg=cg: e.activation(
                        out=junk, in_=VH[:, blk, cg * 256:(cg + 1) * 256], func=AF.Square,
                        accum_out=st2[:, blk, cg:cg + 1]),
                        reads=[("VH", blk, cg)], writes=["junk", ("st2", blk, cg)])
            allst = [("st1", b_, c_) for b_ in range(8) for c_ in range(8)] + \
                    [("st2", b_, c_) for b_ in range(8) for c_ in range(8)]
            sch.op("dve", lambda e: e.tensor_reduce(out=lnmu, in_=st1, axis=mybir.AxisListType.X, op=ALU.add),
                   reads=allst, writes=["lnmu"])
            sch.op("dve", lambda e: e.tensor_reduce(out=lnex, in_=st2, axis=mybir.AxisListType.X, op=ALU.add),
                   reads=allst, writes=["lnex"])
            sch.op("dve", lambda e: e.tensor_scalar(out=lnmu, in0=lnmu, scalar1=1.0 / 2048, scalar2=None,
                                                    op0=ALU.mult), reads=["lnmu"], writes=["lnmu"])
            sch.op("dve", lambda e: e.tensor_tensor(out=lnnm, in0=lnmu, in1=lnmu, op=ALU.mult),
                   reads=["lnmu"], writes=["lnnm"])
            sch.op("dve", lambda e: e.scalar_tensor_tensor(out=lnrs, in0=lnex, scalar=1.0 / 2048, in1=lnnm,
                                                           op0=ALU.mult, op1=ALU.subtract),
                   reads=["lnex", "lnnm"], writes=["lnrs"])
            sch.op("act", lambda e: e.activation(out=lnrs, in_=lnrs, func=AF.Sqrt, scale=1.0, bias=epsT),
                   reads=["lnrs", "eps"], writes=["lnrs"])
            sch.op("dve", lambda e: e.reciprocal(out=lnrs, in_=lnrs), reads=["lnrs"], writes=["lnrs"])
            sch.op("dve", lambda e: e.scalar_tensor_tensor(out=lnnm, in0=lnmu, scalar=-1.0, in1=lnrs,
                                                           op0=ALU.mult, op1=ALU.mult),
                   reads=["lnmu", "lnrs"], writes=["lnnm"])
            for blk in range(8):
                sch.op("dve", lambda e, blk=blk: e.tensor_scalar(
                    out=VH[:, blk, :], in0=VH[:, blk, :], scalar1=lnrs[:, blk:blk + 1], scalar2=lnnm[:, blk:blk + 1],
                    op0=ALU.mult, op1=ALU.add),
                    reads=[("VH", blk, c_) for c_ in range(8)] + ["lnrs", "lnnm"],
                    writes=[("VH", blk, c_) for c_ in range(8)])
            gi = 0
            for gch in range(16):
                g = wslot()
                wu_ = load_w(g, 0, o_win[:, gch * 128:(gch + 1) * 128], 16, 128, 0)
                for sub in range(2):
                    lo = half * 1024 + sub * 512
                    b = gi % 2
                    bm = 2 + gi % 2
                    gi += 1
                    gemm(bank(b), ("ps", b), wu_, ("W", g, 0), 16, lambda k, lo=lo: hx_rhs(k, lo, 512),
                         hx_keys(lo, 512))
                    ut = utmp[gi % 2]
                    sch.op("act", lambda e, b=b, ut=ut: e.activation(out=ut, in_=bank(b), func=AF.Gelu_apprx_tanh),
                           reads=[("ps", b)], writes=[("utmp", gi % 2)])

                    def smm(e, sub=sub, gch=gch, bm=bm):
                        for n4 in range(4):
                            blk = sub * 4 + n4
                            ins = e.matmul(bank(bm)[:, n4 * 128:(n4 + 1) * 128],
                                           lhsT=VH[:, blk, gch * 128:(gch + 1) * 128], rhs=sgwT[:, gch, :],
                                           start=True, stop=True)
                        return ins
                    sch.op("pe", smm, reads=["sgwT"] + [("VH", sub * 4 + n4, gch // 2) for n4 in range(4)],
                           writes=[("ps", bm)])
                    mt = mtmp[gi % 2]
                    sch.op("dve", lambda e, bm=bm, mt=mt, gch=gch: e.scalar_tensor_tensor(
                        out=mt.rearrange("p (a b) -> p a b", a=4),
                        in0=bank(bm).rearrange("p (a b) -> p a b", a=4), scalar=vecs[:, 5, gch:gch + 1],
                        in1=Bt[:, gch:gch + 1, :].broadcast_to([128, 4, 128]), op0=ALU.mult, op1=ALU.add),
                        reads=[("ps", bm), "vecs", ("Bt", gch)], writes=[("mtmp", gi % 2)])
                    sch.op("dve", lambda e, mt=mt, ut=ut, gch=gch, sub=sub: e.tensor_tensor(
                        out=ATh[:, gch, sub * 512:(sub + 1) * 512], in0=mt, in1=ut, op=ALU.mult),
                        reads=[("mtmp", gi % 2), ("utmp", gi % 2)], writes=[("ATh", gch, sub)])
            if stop == "G" and half == 0:
                for k in range(16):
                    for sub in range(2):
                        xb = XRES[(k * 2 + sub) % 4]
                        sch.op("dve", lambda e, xb=xb, k=k, sub=sub: e.tensor_copy(
                            out=xb, in_=ATh[:, k, sub * 512:(sub + 1) * 512]),
                            reads=[("ATh", k, sub)], writes=[("xres", (k * 2 + sub) % 4)])
                        sch.op("sp", lambda e, xb=xb, k=k, sub=sub: e.dma_start(
                            out=dbg_d[k, :, sub * 512:(sub + 1) * 512], in_=xb),
                            reads=[("xres", (k * 2 + sub) % 4)], writes=[("dbg", k, sub)], dma=True)
                sch.barrier()
                sch.op("sp", lambda e: None)
                sch.emit()
                return nc
            outproj_phase(o_wout, 1, 2, half * 1024, 2, lambda k, sub: ATh[:, k, sub * 512:(sub + 1) * 512],
                          lambda sub: [("ATh", k, sub) for k in range(16)])
            sch.barrier()
        if stop == "H":
            dump("xT")
            return nc

        ffn_phase(1)

        scr.reset()
        nbF = norm_bufs()
        otile = [scr.alloc(F32, 2048) for _ in range(2)]
        XTf = T(OFF_AT, F32, 16, 512)
        FIN = T(OFF_AT + 32 * KB, F32, 16, 512)
        aco_keys[0] = ["vecs"]
        oi = 0
        for tt in range(4):
            for k in range(16):
                sch.op("sp", lambda e, k=k, tt=tt: e.dma_start(out=XTf[:, k, :], in_=xT_d[k, :, tt * 512:(tt + 1) * 512]),
                       reads=[("xT", k, tt)], writes=[("XT", 0, k)], dma=True)
            norm_tile(XTf, [("XT", 0, k) for k in range(16)], 512, vecs[:, 4, :], None, None, None, nbF,
                      final=True, fin_out=FIN)
            for t in range(4):
                ot = otile[oi % 2]
                for q4 in range(4):
                    b = (oi * 4 + q4) % 4

                    def trb(e, t=t, q4=q4, b=b):
                        for kk in range(4):
                            k = q4 * 4 + kk
                            ins = e.transpose(bank(b)[:, kk * 128:(kk + 1) * 128], FIN[:, k, t * 128:(t + 1) * 128],
                                              ident)
                        return ins
                    sch.op("pe", trb, reads=["ident"] + [("fin", q4 * 4 + kk) for kk in range(4)],
                           writes=[("ps", b)])
                    sch.op("act" if q4 % 2 == 0 else "dve",
                           (lambda e, ot=ot, q4=q4, b=b: e.copy(out=ot[:, q4 * 512:(q4 + 1) * 512], in_=bank(b)))
                           if q4 % 2 == 0 else
                           (lambda e, ot=ot, q4=q4, b=b: e.tensor_copy(out=ot[:, q4 * 512:(q4 + 1) * 512], in_=bank(b))),
                           reads=[("ps", b)], writes=[("ot", oi % 2, q4)])
                r0 = tt * 512 + t * 128
                sch.op("sp", lambda e, ot=ot, r0=r0: e.dma_start(out=out_d[r0:r0 + 128, :], in_=ot),
                       reads=[("ot", oi % 2, q4) for q4 in range(4)], writes=[("out", r0)], dma=True)
                oi += 1
        sch.barrier()
        sch.op("sp", lambda e: None)
        sch.emit()
    return nc


def _vec_pc(v):
    return np.ascontiguousarray(np.asarray(v, np.float32).reshape(16, 128).T)


def _bias_tables(rpb):
    tiles = [(6, 6 + d) for d in range(-2, 3)]
    for m in (0, 1):
        tiles += [(m, p) for p in range(4)]
    for m in (14, 15):
        tiles += [(m, p) for p in range(12, 16)]
    kk = np.arange(128)
    ki, kc = kk // 64, kk % 64
    qi, qc = kk // 64, kk % 64
    bias = np.zeros((128, 8, 21, 128), np.float32)
    mask = np.zeros((128, 21, 128), np.float32)
    for t, (m, p) in enumerate(tiles):
        kr = (2 * p + ki)[:, None]
        r = (2 * m + qi)[None, :]
        rs = np.clip(r - 4, 0, 24)
        okr = (kr >= rs) & (kr < rs + 8)
        cs = np.clip(qc - 8, 0, 48)[None, :]
        okc = (kc[:, None] >= cs) & (kc[:, None] < cs + 16)
        dr = np.clip(kr - r + 7, 0, 14)
        dc = np.clip(kc[:, None] - qc[None, :], -15, 15) + 15
        bias[:, :, t, :] = rpb[:, dr, dc].transpose(1, 0, 2)
        mask[:, t, :] = (okr & okc)
    return bias, mask


def prep_inputs(inputs):
    f = lambda a: np.ascontiguousarray(np.asarray(a, np.float32))
    x = f(inputs["x"])
    c = f(inputs["c"])
    ctx = f(inputs["ctx"])
    c_ctx = f(inputs["c_ctx"])
    ada_b = f(inputs["ada_b"])
    adabT = np.ascontiguousarray(ada_b.reshape(2, 96, 128).transpose(2, 0, 1))
    vecs = np.stack([_vec_pc(inputs["norm_mix_g"][0]), _vec_pc(inputs["norm_mix_g"][1]),
                     _vec_pc(inputs["norm_ffn_g"][0]), _vec_pc(inputs["norm_ffn_g"][1]),
                     _vec_pc(inputs["final_norm_g"]), _vec_pc(inputs["sg_ln_g"][0]),
                     _vec_pc(inputs["sg_ln_b"][0])], axis=1)
    convw = np.ascontiguousarray(f(inputs["conv_w"])[0].reshape(3, 8, 128).transpose(2, 1, 0))
    bias, mask = _bias_tables(f(inputs["rpb"])[0])
    sgwT = np.ascontiguousarray(f(inputs["sg_w"])[0].transpose(2, 0, 1))
    shared = {
        "ada_w": f(inputs["ada_w"]), "adabT": adabT, "vecs": np.ascontiguousarray(vecs), "convw": convw,
        "ffn_w_gate": f(inputs["ffn_w_gate"]), "ffn_w_up": f(inputs["ffn_w_up"]),
        "ffn_w_down": f(inputs["ffn_w_down"]), "even_w_in": f(inputs["even_w_in"])[0],
        "even_w_out": f(inputs["even_w_out"])[0], "biastab": bias, "masktab": mask,
        "odd_w_in": f(inputs["odd_w_in"])[0], "odd_w_out": f(inputs["odd_w_out"])[0],
        "sgwT": sgwT, "sgb": f(inputs["sg_b"])[0], "ident": np.eye(128, dtype=np.float32),
    }
    in_maps = []
    for b in range(x.shape[0]):
        cT = np.stack([_vec_pc(c[b]), _vec_pc(c_ctx)], axis=2)
        m = dict(shared)
        m["x"] = x[b]
        m["ctx"] = ctx[b]
        m["cT"] = np.ascontiguousarray(cT)
        in_maps.append(m)
    return in_maps


_NC_CACHE = {}


def kernel(**inputs):
    in_maps = prep_inputs(inputs)
    if "nc" not in _NC_CACHE:
        _NC_CACHE["nc"] = build()
    nc = _NC_CACHE["nc"]
    res = run_bass_kernel_spmd(nc, in_maps, core_ids=list(range(8)))
    return np.stack([r["out"] for r in res.results], axis=0).astype(np.float32)
```

```python
import math
import os
from contextlib import ExitStack
import numpy as np
import concourse.bass as bass
import concourse.mybir as mybir
from concourse.bass_utils import run_bass_kernel_spmd

F32 = mybir.dt.float32
BF16 = mybir.dt.bfloat16
AF = mybir.ActivationFunctionType
ALU = mybir.AluOpType
KB = 1024


class _Op:
    __slots__ = ("eng", "fn", "deps", "dma", "needed", "sem", "val", "idx")

    def __init__(self, eng, fn, dma):
        self.eng = eng
        self.fn = fn
        self.dma = dma
        self.deps = set()
        self.needed = False
        self.sem = None
        self.val = None


class Sched:
    def __init__(self, nc, stack, n_dma_sems=20):
        self.nc = nc
        self.engs = {"pe": nc.tensor, "act": nc.scalar, "dve": nc.vector,
                     "pool": nc.gpsimd, "sp": nc.sync}
        self.ops = []
        self.lw = {}
        self.rd = {}
        self.stack = stack
        self.n_dma_sems = n_dma_sems
        self.last_eng_op = {}
        self.dmas_since_barrier = []
        self.pending_barrier = {}

    def op(self, eng, fn, reads=(), writes=(), dma=False, exempt=False):
        o = _Op(eng, fn, dma)
        o.idx = len(self.ops)
        for k in reads:
            w = self.lw.get(k)
            if w is not None:
                o.deps.add(w)
        for k in writes:
            w = self.lw.get(k)
            if w is not None:
                o.deps.add(w)
            for r in self.rd.get(k, ()):
                o.deps.add(r)
        if not exempt and eng in self.pending_barrier:
            for d in self.pending_barrier.pop(eng):
                o.deps.add(d)
        o.deps.discard(o)
        for k in reads:
            self.rd.setdefault(k, []).append(o)
        for k in writes:
            self.lw[k] = o
            self.rd[k] = []
        self.ops.append(o)
        if dma:
            self.dmas_since_barrier.append(o)
        else:
            self.last_eng_op[eng] = o
        return o

    def barrier(self):
        tails = list(self.last_eng_op.values()) + list(self.dmas_since_barrier)
        self.dmas_since_barrier = []
        for e in self.engs:
            prev = self.pending_barrier.get(e, [])
            self.pending_barrier[e] = prev + tails

    def emit(self):
        nc = self.nc
        for o in self.ops:
            for d in o.deps:
                if (not d.dma) and d.eng == "pe" and o.eng == "pe" and not o.dma:
                    continue
                d.needed = True
        esem = {e: self.stack.enter_context(nc.semaphore("s_" + e)) for e in self.engs}
        dsem = {e: [self.stack.enter_context(nc.semaphore("d_%s%d" % (e, i)))
                    for i in range(self.n_dma_sems)] for e in ("sp", "pool", "act")}
        dcount = {e: [0] * self.n_dma_sems for e in dsem}
        drr = {e: 0 for e in dsem}
        ecount = {e: 0 for e in self.engs}
        seen = {e: {} for e in self.engs}

        def need(e, sem, val):
            key = id(sem)
            if seen[e].get(key, 0) >= val:
                return
            seen[e][key] = val
            self.engs[e].wait_ge(sem, val)

        for o in self.ops:
            e = o.eng
            if o.dma:
                i = drr[e]
                drr[e] = (i + 1) % self.n_dma_sems
                o.sem = dsem[e][i]
                dcount[e][i] += 16
                o.val = dcount[e][i]
                if o.val > 16:
                    need(e, o.sem, o.val - 16)
            for d in sorted(o.deps, key=lambda x: x.idx):
                if (not d.dma) and d.eng == "pe" and e == "pe" and not o.dma:
                    continue
                need(e, d.sem, d.val)
            ins = o.fn(self.engs[e])
            if ins is None:
                assert not o.needed and not o.dma
                continue
            if o.dma:
                ins.then_inc(o.sem, 16)
            elif o.needed:
                ecount[e] += 1
                o.sem = esem[e]
                o.val = ecount[e]
                ins.then_inc(o.sem, 1)
        self.ecount = ecount


D = 2048
S = 2048
CTX = 256
FF = 5632
NF = FF // 128
SCALE = 128 ** -0.5
EPS = 1e-6

OFF_MISC = 0
OFF_XRES = 8 * KB
OFF_W = 16 * KB
OFF_HX = 40 * KB
OFF_HXC = 104 * KB
OFF_AT = 112 * KB
OFF_SCR = 176 * KB
ARENA_BYTES = 207 * KB
WG_BYTES = 12 * KB


def pairs_of(m):
    if m <= 1:
        return [0, 1, 2, 3]
    if m >= 14:
        return [12, 13, 14, 15]
    return list(range(m - 2, m + 3))


def tile_base(m):
    if m == 0:
        return 5
    if m == 1:
        return 9
    if m == 14:
        return 13
    if m == 15:
        return 17
    return 0


def build(stop=None):
    nc = bass.Bass("TRN2", target_bir_lowering=False)

    def din(name, shape, dt=F32):
        return nc.dram_tensor(name, list(shape), dt, kind="ExternalInput").ap()

    x_d = din("x", [S, D])
    ctx_d = din("ctx", [CTX, D])
    cT_d = din("cT", [128, 16, 2])
    ada_w = din("ada_w", [2, D, 6 * D])
    adabT_d = din("adabT", [128, 2, 96])
    vecs_d = din("vecs", [128, 7, 16])
    convw_d = din("convw", [128, 8, 3])
    w_gate = din("ffn_w_gate", [2, D, FF])
    w_up = din("ffn_w_up", [2, D, FF])
    w_down = din("ffn_w_down", [2, FF, D])
    e_win = din("even_w_in", [D, 6144])
    e_wout = din("even_w_out", [D, D])
    bias_d = din("biastab", [128, 8, 21, 128])
    mask_d = din("masktab", [128, 21, 128])
    o_win = din("odd_w_in", [D, 4096])
    o_wout = din("odd_w_out", [D, D])
    sgwT_d = din("sgwT", [128, 16, 128])
    sgb_d = din("sgb", [16, 128])
    ident_d = din("ident", [128, 128])
    out_d = nc.dram_tensor("out", [S, D], F32, kind="ExternalOutput").ap()
    dbg_d = None
    if stop is not None:
        dbg_d = nc.dram_tensor("dbg", [16, 128, 2304], F32, kind="ExternalOutput").ap()

    xT_d = nc.dram_tensor("xT_scr", [16, 128, S], F32).ap()
    modrow_d = nc.dram_tensor("modrow_scr", [2, 2, 6 * D], F32).ap()
    etab_d = nc.dram_tensor("etab_scr", [8, 128, 21 * 128], BF16).ap()

    with ExitStack() as st:
        sch = Sched(nc, st)
        arena = st.enter_context(nc.sbuf_tensor("arena", [128, ARENA_BYTES // 2], BF16))
        PS = [st.enter_context(nc.psum_tensor("ps%d" % i, [128, 1024], F32)) for i in range(4)]

        def bank(i):
            return PS[i // 2][:, (i % 2) * 512:(i % 2) * 512 + 512]

        def T(off, dt, *shape):
            n = 1
            for s_ in shape:
                n *= s_
            esz = 2 if dt == BF16 else 4
            a = arena[:, off // 2: off // 2 + n * esz // 2]
            if dt != BF16:
                a = a.bitcast(dt)
            if len(shape) == 2:
                a = a.rearrange("p (a b) -> p a b", a=shape[0])
            elif len(shape) == 3:
                a = a.rearrange("p (a b c) -> p a b c", a=shape[0], b=shape[1])
            return a

        class Bump:
            def __init__(self, base, size):
                self.base = base
                self.size = size
                self.cur = base

            def reset(self):
                self.cur = self.base

            def alloc(self, dt, *shape):
                n = 1
                for s_ in shape:
                    n *= s_
                nb = n * (2 if dt == BF16 else 4)
                nb = (nb + 31) // 32 * 32
                off = self.cur
                self.cur += nb
                assert self.cur <= self.base + self.size, ("bump overflow", self.cur - self.base, self.size)
                return T(off, dt, *shape)

        misc = Bump(OFF_MISC, 8 * KB)
        scr = Bump(OFF_SCR, 31 * KB)
        atr = Bump(OFF_AT, 64 * KB)

        ident = misc.alloc(F32, 128)
        ones = misc.alloc(BF16, 128)
        modsT = misc.alloc(F32, 2, 96)
        modsC = misc.alloc(F32, 32)
        adabT = misc.alloc(F32, 2, 96)
        vecs = misc.alloc(F32, 7, 16)
        Aco = misc.alloc(F32, 4, 16)
        AcoC = misc.alloc(F32, 16)
        convw = misc.alloc(F32, 8, 3)
        epsT = misc.alloc(F32, 1)
        cT = misc.alloc(F32, 16, 2)
        s2 = misc.alloc(BF16, 16, 2)
        st1 = misc.alloc(F32, 8, 8)
        st2 = misc.alloc(F32, 8, 8)
        lnmu = misc.alloc(F32, 8)
        lnex = misc.alloc(F32, 8)
        lnrs = misc.alloc(F32, 8)
        lnnm = misc.alloc(F32, 8)

        XRES = [T(OFF_XRES + i * 2 * KB, F32, 512) for i in range(4)]
        HX = T(OFF_HX, BF16, 16, 2048)
        HXC = T(OFF_HXC, BF16, 16, 256)
        AT = T(OFF_AT, BF16, 16, 2048)

        def hx_rhs(k, lo, n):
            if lo >= 2048:
                return HXC[:, k, lo - 2048: lo - 2048 + n]
            return HX[:, k, lo: lo + n]

        def hx_keys(lo, n):
            return [("HX", k, t) for k in range(16) for t in range(lo // 128, (lo + n) // 128)]

        wstate = {"i": 0}

        def wslot():
            g = wstate["i"] % 2
            wstate["i"] += 1
            return g

        def WT(g, off, kc, n):
            return T(OFF_W + g * WG_BYTES + off, BF16, kc, n)

        def load_w(g, off, src2d, kc, n, key_extra=0):
            t = WT(g, off, kc, n)
            src = src2d.rearrange("(kc p) n -> p kc n", p=128)
            nbytes = kc * n * 2
            keys = [("W", g, s_) for s_ in range(off // (4 * KB), (off + nbytes + 4 * KB - 1) // (4 * KB))]
            sch.op("pool", lambda e: e.dma_start(out=t, in_=src), writes=keys, dma=True, exempt=True)
            wkeys[id(t)] = keys
            return t

        wkeys = {}

        dve_copy = lambda out, in_: (lambda e: e.tensor_copy(out=out, in_=in_))

        sch.op("sp", lambda e: e.dma_start(out=ident, in_=ident_d), writes=["ident"], dma=True)
        sch.op("sp", lambda e: e.dma_start(out=adabT, in_=adabT_d), writes=["adabT"], dma=True)
        sch.op("sp", lambda e: e.dma_start(out=vecs, in_=vecs_d), writes=["vecs"], dma=True)
        sch.op("sp", lambda e: e.dma_start(out=convw, in_=convw_d), writes=["convw"], dma=True)
        sch.op("sp", lambda e: e.dma_start(out=cT, in_=cT_d), writes=["cT"], dma=True)
        sch.op("dve", lambda e: e.memset(ones, 1.0), writes=["ones"])
        sch.op("dve", lambda e: e.memset(epsT, EPS), writes=["eps"])
        sch.op("act", lambda e: e.activation(out=s2, in_=cT, func=AF.Silu), reads=["cT"], writes=["s2"])

        def ada(layer, jlist, with_ctx, only=None, fin=True):
            rowst = [scr_rows[i] for i in range(4)]
            cnt = 0
            for j in jlist:
                for n in range(8):
                    if only is not None and (j * 8 + n) not in only:
                        cnt += 1
                        continue
                    col0 = j * 2048 + n * 256
                    g = wslot()
                    wt = load_w(g, 0, ada_w[layer][:, col0:col0 + 256], 16, 256)
                    b = cnt % 2
                    psb = bank(b)

                    def mm(e, wt=wt, psb=psb):
                        for k in range(16):
                            ins = e.matmul(psb[0:2, 0:256], lhsT=s2[:, k, :], rhs=wt[:, k, :],
                                           start=(k == 0), stop=(k == 15))
                        return ins
                    sch.op("pe", mm, reads=wkeys[id(wt)] + ["s2"], writes=[("ps", b)])
                    rt = rowst[cnt % 4]
                    sch.op("dve", dve_copy(rt[0:2, :], psb[0:2, 0:256]), reads=[("ps", b)],
                           writes=[("row", cnt % 4)])
                    sch.op("sp", lambda e, rt=rt, col0=col0: e.dma_start(
                        out=modrow_d[layer, :, col0:col0 + 256], in_=rt[0:2, :]),
                        reads=[("row", cnt % 4)], writes=[("modrow", layer, j)], dma=True)
                    cnt += 1
            if not fin:
                return
            for j in jlist:
                src = modrow_d[layer, 0, j * 2048:(j + 1) * 2048].rearrange("(c p) -> p c", p=128)
                sch.op("sp", lambda e, src=src, j=j: e.dma_start(
                    out=modsT[:, layer, j * 16:(j + 1) * 16], in_=src, allow_slow_non_contiguous=True),
                    reads=[("modrow", layer, j)], writes=[("modsT", layer, j)], dma=True)
                sch.op("dve", lambda e, j=j: e.tensor_tensor(
                    out=modsT[:, layer, j * 16:(j + 1) * 16], in0=modsT[:, layer, j * 16:(j + 1) * 16],
                    in1=adabT[:, layer, j * 16:(j + 1) * 16], op=ALU.add),
                    reads=[("modsT", layer, j), "adabT"], writes=[("modsT", layer, j)])
                if with_ctx and j < 2:
                    srcc = modrow_d[layer, 1, j * 2048:(j + 1) * 2048].rearrange("(c p) -> p c", p=128)
                    sch.op("sp", lambda e, srcc=srcc, j=j: e.dma_start(
                        out=modsC[:, j * 16:(j + 1) * 16], in_=srcc, allow_slow_non_contiguous=True),
                        reads=[("modrow", layer, j)], writes=[("modsC", j)], dma=True)
                    sch.op("dve", lambda e, j=j: e.tensor_tensor(
                        out=modsC[:, j * 16:(j + 1) * 16], in0=modsC[:, j * 16:(j + 1) * 16],
                        in1=adabT[:, layer, j * 16:(j + 1) * 16], op=ALU.add),
                        reads=[("modsC", j), "adabT"], writes=[("modsC", j)])

        def mk_aco(layer, which):
            gi = (0 if which == 0 else 2) + layer
            j = 1 if which == 0 else 4
            sch.op("dve", lambda e: e.scalar_tensor_tensor(
                out=Aco[:, layer * 2 + which, :], in0=modsT[:, layer, j * 16:(j + 1) * 16], scalar=1.0,
                in1=vecs[:, gi, :], op0=ALU.add, op1=ALU.mult),
                reads=[("modsT", layer, j), "vecs"], writes=[("Aco", layer, which)])

        scr.reset()
        scr_rows = [scr.alloc(F32, 256) for _ in range(4)]
        mk = atr.alloc(F32, 21 * 128)
        bst = [atr.alloc(F32, 21 * 128) for _ in range(1)]
        eo = [atr.alloc(BF16, 21 * 128) for _ in range(2)]
        sch.op("sp", lambda e: e.dma_start(out=mk, in_=mask_d.rearrange("p a b -> p (a b)")), writes=["mk"],
               dma=True)
        for h in range(8):
            sch.op("sp", lambda e, h=h: e.dma_start(out=bst[0], in_=bias_d[:, h].rearrange("p a b -> p (a b)")),
                   writes=["bst"], dma=True)
            sch.op("act", lambda e: e.activation(out=bst[0], in_=bst[0], func=AF.Exp), reads=["bst"],
                   writes=["bst"])
            sch.op("dve", lambda e, h=h: e.tensor_tensor(out=eo[h % 2], in0=bst[0], in1=mk, op=ALU.mult),
                   reads=["bst", "mk"], writes=[("eo", h % 2)])
            sch.op("sp", lambda e, h=h: e.dma_start(out=etab_d[h], in_=eo[h % 2]), reads=[("eo", h % 2)],
                   writes=[("etab", h)], dma=True)
            ada(0, [0, 1], True, only=(2 * h, 2 * h + 1), fin=False)
        ada(0, [0, 1], True, only=())
        mk_aco(0, 0)
        sch.op("dve", lambda e: e.scalar_tensor_tensor(
            out=AcoC, in0=modsC[:, 16:32], scalar=1.0, in1=vecs[:, 0, :], op0=ALU.add, op1=ALU.mult),
            reads=[("modsC", 1), "vecs"], writes=["AcoC"])
        ADAW = [T(OFF_XRES + i * 4 * KB, BF16, 16, 128) for i in range(2)]
        ROWS = [T(OFF_SCR + 30 * KB + i * 512, F32, 128) for i in range(2)]
        ada_todo = [(l_, j_, n_) for (l_, j_) in [(0, 2), (0, 3), (0, 4), (0, 5)] for n_ in range(16)]
        ada_todo2 = [(1, j_, n_) for j_ in range(6) for n_ in range(16)]
        ada_state = {"t": 0, "t2": 0}
        ADAW2 = T(OFF_HXC + 4 * KB, BF16, 16, 128)

        def ada_step_ffn():
            t = ada_state["t2"]
            if t >= len(ada_todo2):
                return False
            ada_state["t2"] = t + 1
            layer, j, n = ada_todo2[t]
            i = t % 2
            col0 = j * 2048 + n * 128
            wt = ADAW2
            src = ada_w[layer][:, col0:col0 + 128].rearrange("(kc p) n -> p kc n", p=128)
            sch.op("pool", lambda e: e.dma_start(out=wt, in_=src), writes=["adaw2"], dma=True, exempt=True)
            psa = bank(6 + i)[0:2, 0:128]

            def mm(e):
                for k in range(16):
                    ins = e.matmul(psa, lhsT=s2[:, k, :], rhs=wt[:, k, :], start=(k == 0), stop=(k == 15))
                return ins
            sch.op("pe", mm, reads=["adaw2", "s2"], writes=[("ps", 6 + i)])
            rt = ROWS[i]
            sch.op("act", lambda e: e.copy(out=rt[0:2, :], in_=psa), reads=[("ps", 6 + i)], writes=[("arow", i)])
            sch.op("sp", lambda e: e.dma_start(out=modrow_d[layer, :, col0:col0 + 128], in_=rt[0:2, :]),
                   reads=[("arow", i)], writes=[("modrow", layer, j)], dma=True)
            if n == 15:
                srcm = modrow_d[layer, 0, j * 2048:(j + 1) * 2048].rearrange("(c p) -> p c", p=128)
                sch.op("sp", lambda e: e.dma_start(
                    out=modsT[:, layer, j * 16:(j + 1) * 16], in_=srcm, allow_slow_non_contiguous=True),
                    reads=[("modrow", layer, j)], writes=[("modsT", layer, j)], dma=True)
                sch.op("dve", lambda e: e.tensor_tensor(
                    out=modsT[:, layer, j * 16:(j + 1) * 16], in0=modsT[:, layer, j * 16:(j + 1) * 16],
                    in1=adabT[:, layer, j * 16:(j + 1) * 16], op=ALU.add),
                    reads=[("modsT", layer, j), "adabT"], writes=[("modsT", layer, j)])
                if j == 1:
                    mk_aco(layer, 0)
                if j == 4:
                    mk_aco(layer, 1)
            return True

        def ada_step():
            if os.environ.get("NOADA", "") == "1" and not ada_state.get("flush"):
                return False
            t = ada_state["t"]
            if t >= len(ada_todo):
                return False
            ada_state["t"] = t + 1
            layer, j, n = ada_todo[t]
            i = t % 2
            col0 = j * 2048 + n * 128
            wt = ADAW[i]
            src = ada_w[layer][:, col0:col0 + 128].rearrange("(kc p) n -> p kc n", p=128)
            sch.op("pool", lambda e: e.dma_start(out=wt, in_=src), writes=[("adaw", i)], dma=True, exempt=True)
            psa = bank(6 + i)[0:2, 256:384]

            def mm(e):
                for k in range(16):
                    ins = e.matmul(psa, lhsT=s2[:, k, :], rhs=wt[:, k, :], start=(k == 0), stop=(k == 15))
                return ins
            sch.op("pe", mm, reads=[("adaw", i), "s2"], writes=[("ps", 6 + i)])
            rt = ROWS[i]
            sch.op("act", lambda e: e.copy(out=rt[0:2, :], in_=psa), reads=[("ps", 6 + i)], writes=[("arow", i)])
            sch.op("sp", lambda e: e.dma_start(out=modrow_d[layer, :, col0:col0 + 128], in_=rt[0:2, :]),
                   reads=[("arow", i)], writes=[("modrow", layer, j)], dma=True)
            if n == 15:
                srcm = modrow_d[layer, 0, j * 2048:(j + 1) * 2048].rearrange("(c p) -> p c", p=128)
                sch.op("sp", lambda e: e.dma_start(
                    out=modsT[:, layer, j * 16:(j + 1) * 16], in_=srcm, allow_slow_non_contiguous=True),
                    reads=[("modrow", layer, j)], writes=[("modsT", layer, j)], dma=True)
                sch.op("dve", lambda e: e.tensor_tensor(
                    out=modsT[:, layer, j * 16:(j + 1) * 16], in0=modsT[:, layer, j * 16:(j + 1) * 16],
                    in1=adabT[:, layer, j * 16:(j + 1) * 16], op=ALU.add),
                    reads=[("modsT", layer, j), "adabT"], writes=[("modsT", layer, j)])
                if j == 1:
                    mk_aco(layer, 0)
                if j == 4:
                    mk_aco(layer, 1)
            return True

        sch.barrier()

        def norm_tile(XT, xkeys, ntok, aco, bvec, out_fn, okeys_fn, nb, final=False, fin_out=None):
            sq, rstd, tmp = nb
            pss = bank(6)
            for k in range(16):
                sb = sq[k % 2]
                sch.op("act", lambda e, k=k, sb=sb: e.activation(out=sb[:, 0:ntok], in_=XT[:, k, 0:ntok],
                                                                func=AF.Square),
                       reads=[xkeys[k]], writes=[("sq", k % 2)])
                sch.op("pe", lambda e, k=k, sb=sb: e.matmul(pss[:, 0:ntok], lhsT=ones, rhs=sb[:, 0:ntok],
                                                            start=(k == 0), stop=(k == 15)),
                       reads=[("sq", k % 2), "ones"], writes=[("ps", 6)])
            sch.op("act", lambda e: e.activation(out=rstd[:, 0:ntok], in_=pss[:, 0:ntok], func=AF.Sqrt,
                                                 scale=1.0 / D, bias=epsT),
                   reads=[("ps", 6), "eps"], writes=["rstd"])
            sch.op("dve", lambda e: e.reciprocal(out=rstd[:, 0:ntok], in_=rstd[:, 0:ntok]),
                   reads=["rstd"], writes=["rstd"])
            for k in range(16):
                if final:
                    sch.op("dve", lambda e, k=k: e.scalar_tensor_tensor(
                        out=fin_out[:, k, 0:ntok], in0=XT[:, k, 0:ntok], scalar=aco[:, k:k + 1],
                        in1=rstd[:, 0:ntok], op0=ALU.mult, op1=ALU.mult),
                        reads=[xkeys[k], "rstd"] + aco_keys[0], writes=[("fin", k)])
                    continue
                tb = tmp[k % 2]
                sch.op("dve", lambda e, k=k, tb=tb: e.scalar_tensor_tensor(
                    out=tb[:, 0:ntok], in0=XT[:, k, 0:ntok], scalar=aco[:, k:k + 1], in1=rstd[:, 0:ntok],
                    op0=ALU.mult, op1=ALU.mult),
                    reads=[xkeys[k], "rstd"] + aco_keys[0], writes=[("ntmp", k % 2)])
                sch.op("act", lambda e, k=k, tb=tb: e.activation(out=out_fn(k), in_=tb[:, 0:ntok],
                                                                func=AF.Identity, bias=bvec[:, k:k + 1]),
                       reads=[("ntmp", k % 2)] + aco_keys[0], writes=okeys_fn(k))

        aco_keys = [[]]

        def norm_bufs():
            sq = [scr.alloc(BF16, 512) for _ in range(2)]
            rstd = scr.alloc(F32, 512)
            tmp = [scr.alloc(F32, 512) for _ in range(2)]
            return sq, rstd, tmp

        def norm_phase(layer, which):
            scr.reset()
            nb = norm_bufs()
            XTs = [T(OFF_AT + i * 32 * KB, F32, 16, 512) for i in range(2)]
            j_shift = 0 if which == 0 else 3
            aco = Aco[:, layer * 2 + which, :]
            bvec = modsT[:, layer, j_shift * 16:(j_shift + 1) * 16]
            aco_keys[0] = [("Aco", layer, which), ("modsT", layer, j_shift)]
            for tt in range(4):
                XT = XTs[tt % 2]
                for k in range(16):
                    sch.op("sp", lambda e, k=k, tt=tt, XT=XT: e.dma_start(
                        out=XT[:, k, :], in_=xT_d[k, :, tt * 512:(tt + 1) * 512]),
                        reads=[("xT", k, tt)], writes=[("XT", tt % 2, k)], dma=True)
                norm_tile(XT, [("XT", tt % 2, k) for k in range(16)], 512, aco, bvec,
                          lambda k, tt=tt: HX[:, k, tt * 512:(tt + 1) * 512],
                          lambda k, tt=tt: [("HX", k, t) for t in range(tt * 4, tt * 4 + 4)], nb)
            sch.barrier()

        scr.reset()
        nbA = norm_bufs()
        xin = T(OFF_AT, F32, 4, 2048)
        XTa = T(OFF_AT + 32 * KB, F32, 16, 512)
        aco_keys[0] = [("Aco", 0, 0), ("modsT", 0, 0)]
        for tt in range(5):
            nblk = 4 if tt < 4 else 2
            ntok = nblk * 128
            srcd = x_d if tt < 4 else ctx_d
            for t in range(nblk):
                r0 = (tt * 512 if tt < 4 else 0) + t * 128
                sch.op("sp", lambda e, t=t, r0=r0, srcd=srcd: e.dma_start(out=xin[:, t, :],
                                                                          in_=srcd[r0:r0 + 128, :]),
                       writes=[("xin", t)], dma=True)
            for k in range(16):
                b = k % 2
                psb = bank(b)

                def tr(e, k=k, psb=psb, nblk=nblk):
                    for t in range(nblk):
                        ins = e.transpose(psb[:, t * 128:(t + 1) * 128], xin[:, t, k * 128:(k + 1) * 128], ident)
                    return ins
                sch.op("pe", tr, reads=["ident"] + [("xin", t) for t in range(nblk)], writes=[("ps", b)])
                sch.op("dve" if k % 2 == 0 else "act",
                       (lambda e, k=k, psb=psb, ntok=ntok: e.tensor_copy(out=XTa[:, k, 0:ntok], in_=psb[:, 0:ntok]))
                       if k % 2 == 0 else
                       (lambda e, k=k, psb=psb, ntok=ntok: e.copy(out=XTa[:, k, 0:ntok], in_=psb[:, 0:ntok])),
                       reads=[("ps", b)], writes=[("XTa", k)])
                if tt < 4:
                    sch.op("sp", lambda e, k=k, tt=tt: e.dma_start(out=xT_d[k, :, tt * 512:(tt + 1) * 512],
                                                                   in_=XTa[:, k, :]),
                           reads=[("XTa", k)], writes=[("xT", k, tt)], dma=True)
            if tt < 4:
                norm_tile(XTa, [("XTa", k) for k in range(16)], 512, Aco[:, 0, :], modsT[:, 0, 0:16],
                          lambda k, tt=tt: HX[:, k, tt * 512:(tt + 1) * 512],
                          lambda k, tt=tt: [("HX", k, t) for t in range(tt * 4, tt * 4 + 4)], nbA)
            else:
                aco_keys[0] = ["AcoC", ("modsC", 0)]
                norm_tile(XTa, [("XTa", k) for k in range(16)], 256, AcoC, modsC[:, 0:16],
                          lambda k: HXC[:, k, :],
                          lambda k: [("HX", k, 16), ("HX", k, 17)], nbA)
        sch.barrier()

        def dump(kind):
            if kind == "HX":
                st_ = scr.alloc(F32, 2304) if False else None
                for k in range(16):
                    for tt in range(5):
                        n = 512 if tt < 4 else 256
                        xb = XRES[(k * 5 + tt) % 4]
                        srcv = HX[:, k, tt * 512:(tt + 1) * 512] if tt < 4 else HXC[:, k, :]
                        sch.op("dve", lambda e, xb=xb, srcv=srcv, n=n: e.tensor_copy(out=xb[:, 0:n], in_=srcv),
                               reads=hx_keys(tt * 512, n), writes=[("xres", (k * 5 + tt) % 4)])
                        sch.op("sp", lambda e, xb=xb, k=k, tt=tt, n=n: e.dma_start(
                            out=dbg_d[k, :, tt * 512:tt * 512 + n], in_=xb[:, 0:n]),
                            reads=[("xres", (k * 5 + tt) % 4)], writes=[("dbg", k, tt)], dma=True)
            elif kind == "AT":
                for k in range(16):
                    for tt in range(4):
                        xb = XRES[(k * 4 + tt) % 4]
                        sch.op("dve", lambda e, xb=xb, k=k, tt=tt: e.tensor_copy(
                            out=xb, in_=AT[:, k, tt * 512:(tt + 1) * 512]),
                            reads=[("AT", k, t) for t in range(16)], writes=[("xres", (k * 4 + tt) % 4)])
                        sch.op("sp", lambda e, xb=xb, k=k, tt=tt: e.dma_start(
                            out=dbg_d[k, :, tt * 512:(tt + 1) * 512], in_=xb),
                            reads=[("xres", (k * 4 + tt) % 4)], writes=[("dbg", k, tt)], dma=True)
            elif kind == "xT":
                for k in range(16):
                    for tt in range(4):
                        xb = XRES[(k * 4 + tt) % 4]
                        sch.op("sp", lambda e, xb=xb, k=k, tt=tt: e.dma_start(
                            out=xb, in_=xT_d[k, :, tt * 512:(tt + 1) * 512]),
                            reads=[("xT", k, tt)], writes=[("xres", (k * 4 + tt) % 4)], dma=True)
                        sch.op("sp", lambda e, xb=xb, k=k, tt=tt: e.dma_start(
                            out=dbg_d[k, :, tt * 512:(tt + 1) * 512], in_=xb),
                            reads=[("xres", (k * 4 + tt) % 4)], writes=[("dbg", k, tt)], dma=True)
            sch.barrier()
            sch.op("sp", lambda e: None)
            sch.emit()

        if stop == "A":
            dump("HX")
            return nc

        def gemm(psb, pskey, wt, wkey, K, rhs_fn, rkeys):
            def f(e):
                for k in range(K):
                    ins = e.matmul(psb, lhsT=wt[:, k, :], rhs=rhs_fn(k), start=(k == 0), stop=(k == K - 1))
                return ins
            sch.op("pe", f, reads=wkeys[id(wt)] + rkeys, writes=[pskey])

        scr.reset()
        qT = scr.alloc(BF16, 2048)
        kT = scr.alloc(BF16, 2304)
        Vh = scr.alloc(BF16, 18, 128)
        Et = scr.alloc(BF16, 21, 128)
        PTf = [scr.alloc(F32, 640) for _ in range(2)]
        PTb = [scr.alloc(BF16, 896) for _ in range(2)]
        rZ = [scr.alloc(F32, 128) for _ in range(2)]
        pre = {}

        def load_head(h):
            g = wslot()
            return (load_w(g, 0, e_win[:, h * 128:(h + 1) * 128], 16, 128, 0),
                    load_w(g, 4 * KB, e_win[:, 1024 + h * 128:1024 + (h + 1) * 128], 16, 128, 1),
                    load_w(g, 8 * KB, e_win[:, 2048 + h * 128:2048 + (h + 1) * 128], 16, 128, 2))

        def load_conv(j):
            g = wslot()
            return (load_w(g, 0, e_win[:, 3072 + j * 128:3072 + (j + 1) * 128], 16, 128, 0),
                    load_w(g, 4 * KB, e_win[:, 4096 + j * 128:4096 + (j + 1) * 128], 16, 128, 1),
                    load_w(g, 8 * KB, e_win[:, 5120 + j * 128:5120 + (j + 1) * 128], 16, 128, 2))

        for h in range(8):
            if h == 0:
                pre[("head", 0)] = load_head(0)
            wq, wk, wv = pre.pop(("head", h))
            if h + 1 < 8:
                pre[("head", h + 1)] = load_head(h + 1)
            else:
                pre[("conv", 0)] = load_conv(0)
            sch.op("sp", lambda e, h=h: e.dma_start(out=Et.rearrange("p a b -> p (a b)"), in_=etab_d[h]),
                   reads=[("etab", h)], writes=["Et"], dma=True)
            gi = 0
            for tt in range(4):
                b = gi % 2
                gi += 1
                gemm(bank(b), ("ps", b), wq, None, 16, lambda k, tt=tt: hx_rhs(k, tt * 512, 512),
                     hx_keys(tt * 512, 512))
                sch.op("act", lambda e, b=b, tt=tt: e.activation(out=qT[:, tt * 512:(tt + 1) * 512], in_=bank(b),
                                                                func=AF.Identity, scale=SCALE),
                       reads=[("ps", b)], writes=[("qT", tt)])
                if gi % 2 == 0:
                    ada_step()
            for tt in range(5):
                n = 512 if tt < 4 else 256
                b = gi % 2
                gi += 1
                gemm(bank(b)[:, 0:n], ("ps", b), wk, None, 16, lambda k, tt=tt, n=n: hx_rhs(k, tt * 512, n),
                     hx_keys(tt * 512, n))
                sch.op("dve", lambda e, b=b, tt=tt, n=n: e.tensor_copy(out=kT[:, tt * 512:tt * 512 + n],
                                                                       in_=bank(b)[:, 0:n]),
                       reads=[("ps", b)], writes=[("kT", tt)])
                if gi % 2 == 0:
                    ada_step()
            for vb in range(5):
                nblk = 4 if vb < 4 else 2
                b = gi % 2
                gi += 1

                def vmm(e, vb=vb, nblk=nblk, b=b, wv=wv):
                    for t in range(nblk):
                        blk = vb * 4 + t
                        for k in range(16):
                            ins = e.matmul(bank(b)[:, t * 128:(t + 1) * 128],
                                           lhsT=hx_rhs(k, blk * 128, 128), rhs=wv[:, k, :],
                                           start=(k == 0), stop=(k == 15))
                    return ins
                sch.op("pe", vmm, reads=wkeys[id(wv)] + hx_keys(vb * 512, nblk * 128), writes=[("ps", b)])
                sch.op("act" if vb % 2 == 0 else "dve",
                       (lambda e, vb=vb, nblk=nblk, b=b: e.copy(
                           out=Vh[:, vb * 4:vb * 4 + nblk, :],
                           in_=bank(b)[:, 0:nblk * 128].rearrange("p (a b) -> p a b", a=nblk)))
                       if vb % 2 == 0 else
                       (lambda e, vb=vb, nblk=nblk, b=b: e.tensor_copy(
                           out=Vh[:, vb * 4:vb * 4 + nblk, :],
                           in_=bank(b)[:, 0:nblk * 128].rearrange("p (a b) -> p a b", a=nblk))),
                       reads=[("ps", b)], writes=[("Vh", vb)])
                if vb % 2 == 0:
                    ada_step()
            if stop == "Q" and h == 0:
                for idx_, (srcT, n_) in enumerate([(qT, 2048), (kT, 2304), (Vh.rearrange("p a b -> p (a b)"), 2304)]):
                    for c0_ in range(0, n_, 512):
                        w_ = min(512, n_ - c0_)
                        xb = XRES[(idx_ * 5 + c0_ // 512) % 4]
                        kk_ = (idx_ * 5 + c0_ // 512) % 4
                        sch.op("dve", lambda e, xb=xb, srcT=srcT, c0_=c0_, w_=w_: e.tensor_copy(
                            out=xb[:, 0:w_], in_=srcT[:, c0_:c0_ + w_]),
                            reads=[("qT", t) for t in range(4)] + [("kT", t) for t in range(5)] + [("Vh", t) for t in range(5)],
                            writes=[("xres", kk_)])
                        sch.op("sp", lambda e, xb=xb, idx_=idx_, c0_=c0_, w_=w_: e.dma_start(
                            out=dbg_d[idx_, :, c0_:c0_ + w_], in_=xb[:, 0:w_]),
                            reads=[("xres", kk_)], writes=[("dbg", idx_, c0_)], dma=True)
                sch.barrier()
                sch.op("sp", lambda e: None)
                sch.emit()
                return nc
            def stage1(m, h=h):
                pl = pairs_of(m)
                nl = len(pl)
                tb = tile_base(m)
                pb = m % 2
                psS = PS[1 + pb]
                qs = qT[:, m * 128:(m + 1) * 128]

                def qk(e):
                    for c in range(2):
                        ins = e.matmul(psS[:, c * 128:(c + 1) * 128], lhsT=kT[:, 2048 + c * 128:2048 + (c + 1) * 128],
                                       rhs=qs, start=True, stop=True)
                    for i, p in enumerate(pl):
                        ins = e.matmul(psS[:, 256 + i * 128:256 + (i + 1) * 128],
                                       lhsT=kT[:, p * 128:(p + 1) * 128], rhs=qs, start=True, stop=True)
                    return ins
                sch.op("pe", qk, reads=[("qT", m // 4), ("kT", 4)] + [("kT", p // 4) for p in pl],
                       writes=[("psS", pb)])
                sch.op("act", lambda e: e.activation(out=PTb[pb][:, 0:256], in_=psS[:, 0:256], func=AF.Exp),
                       reads=[("psS", pb)], writes=[("PTb", pb, 0)])
                sch.op("act", lambda e: e.activation(out=PTf[pb][:, 0:nl * 128], in_=psS[:, 256:256 + nl * 128],
                                                     func=AF.Exp),
                       reads=[("psS", pb)], writes=[("PTf", pb)])
                sch.op("dve", lambda e: e.tensor_tensor(
                    out=PTb[pb][:, 256:256 + nl * 128], in0=PTf[pb][:, 0:nl * 128],
                    in1=Et[:, tb:tb + nl, :].rearrange("p a b -> p (a b)"), op=ALU.mult),
                    reads=[("PTf", pb), "Et"], writes=[("PTb", pb, 1)])

            def stage2(m, h=h):
                pl = pairs_of(m)
                pb = m % 2
                psO = bank(6 + pb)

                def pv(e):
                    tiles = [(16 + c, c) for c in range(2)] + [(p, 2 + i) for i, p in enumerate(pl)]
                    for i, (vblk, ti) in enumerate(tiles):
                        ins = e.matmul(psO[:, 0:128], lhsT=Vh[:, vblk, :], rhs=PTb[pb][:, ti * 128:(ti + 1) * 128],
                                       start=(i == 0), stop=(i == len(tiles) - 1))
                    for i, (vblk, ti) in enumerate(tiles):
                        ins = e.matmul(psO[:, 128:256], lhsT=ones, rhs=PTb[pb][:, ti * 128:(ti + 1) * 128],
                                       start=(i == 0), stop=(i == len(tiles) - 1))
                    return ins
                sch.op("pe", pv, reads=[("PTb", pb, 0), ("PTb", pb, 1), "ones", ("Vh", 4)] +
                       [("Vh", p // 4) for p in pl], writes=[("ps", 6 + pb)])
                sch.op("dve", lambda e: e.reciprocal(out=rZ[pb], in_=psO[:, 128:256]),
                       reads=[("ps", 6 + pb)], writes=[("rZ", pb)])
                sch.op("dve", lambda e: e.tensor_tensor(
                    out=AT[:, h, m * 128:(m + 1) * 128], in0=psO[:, 0:128], in1=rZ[pb], op=ALU.mult),
                    reads=[("ps", 6 + pb), ("rZ", pb)], writes=[("AT", h, m)])

            skip = os.environ.get("DBGSKIP", "")
            if not (skip == "attn" and h == 1):
                stage1(0)
                for m in range(16):
                    if m + 1 < 16:
                        stage1(m + 1)
                    if not (skip == "s2" and h == 1):
                        stage2(m)
                    if m % 2 == 1:
                        ada_step()
            if stop in ("C0", "C1") and h == int(stop[1]):
                sch.barrier()
                dump("AT")
                return nc
        sch.barrier()

        scr.reset()
        zb = [scr.alloc(F32, 2050) for _ in range(2)]
        ctmp = [scr.alloc(F32, 512) for _ in range(2)]
        ytmp = [scr.alloc(F32, 512) for _ in range(2)]
        for i in range(2):
            sch.op("dve", lambda e, i=i: e.memset(zb[i][:, 0:1], 0.0), writes=[("z", i, 0)])
            sch.op("dve", lambda e, i=i: e.memset(zb[i][:, 2049:2050], 0.0), writes=[("z", i, 3)])
        gi = 0
        for j in range(8):
            wb, wc, wu = pre.pop(("conv", j))
            if j + 1 < 8:
                pre[("conv", j + 1)] = load_conv(j + 1)
            z = zb[j % 2]
            for tt in range(4):
                b = gi % 4
                gi += 1
                gemm(bank(b), ("ps", b), wc, None, 16, lambda k, tt=tt: hx_rhs(k, tt * 512, 512),
                     hx_keys(tt * 512, 512))
                ct = ctmp[tt % 2]
                sch.op("act", lambda e, b=b, ct=ct: e.copy(out=ct, in_=bank(b)), reads=[("ps", b)],
                       writes=[("ctmp", tt % 2)])
                b2 = gi % 4
                gi += 1
                gemm(bank(b2), ("ps", b2), wu, None, 16, lambda k, tt=tt: hx_rhs(k, tt * 512, 512),
                     hx_keys(tt * 512, 512))
                sch.op("dve", lambda e, b2=b2, ct=ct, z=z, tt=tt: e.tensor_tensor(
                    out=z[:, 1 + tt * 512:1 + (tt + 1) * 512], in0=bank(b2), in1=ct, op=ALU.mult),
                    reads=[("ps", b2), ("ctmp", tt % 2)], writes=[("z", j % 2, 1, tt)])
                ada_step()
            zkeys = [("z", j % 2, 0), ("z", j % 2, 3)] + [("z", j % 2, 1, t) for t in range(4)]
            for tt in range(4):
                b = gi % 4
                gi += 1
                gemm(bank(b), ("ps", b), wb, None, 16, lambda k, tt=tt: hx_rhs(k, tt * 512, 512),
                     hx_keys(tt * 512, 512))
                yt = ytmp[tt % 2]
                c0 = 1 + tt * 512
                sch.op("dve", lambda e, z=z, yt=yt, c0=c0, j=j: e.tensor_scalar(
                    out=yt, in0=z[:, c0:c0 + 512], scalar1=convw[:, j, 1:2], scalar2=None, op0=ALU.mult),
                    reads=zkeys + ["convw"], writes=[("ytmp", tt % 2)])
                sch.op("dve", lambda e, z=z, yt=yt, c0=c0, j=j: e.scalar_tensor_tensor(
                    out=yt, in0=z[:, c0 - 1:c0 + 511], scalar=convw[:, j, 0:1], in1=yt, op0=ALU.mult, op1=ALU.add),
                    reads=zkeys + ["convw", ("ytmp", tt % 2)], writes=[("ytmp", tt % 2)])
                sch.op("dve", lambda e, z=z, yt=yt, c0=c0, j=j: e.scalar_tensor_tensor(
                    out=yt, in0=z[:, c0 + 1:c0 + 513], scalar=convw[:, j, 2:3], in1=yt, op0=ALU.mult, op1=ALU.add),
                    reads=zkeys + ["convw", ("ytmp", tt % 2)], writes=[("ytmp", tt % 2)])
                sch.op("dve", lambda e, b=b, yt=yt, j=j, tt=tt: e.tensor_tensor(
                    out=AT[:, 8 + j, tt * 512:(tt + 1) * 512], in0=bank(b), in1=yt, op=ALU.mult),
                    reads=[("ps", b), ("ytmp", tt % 2)], writes=[("AT", 8 + j, 4 * tt + t) for t in range(4)])
                if tt % 2 == 1:
                    ada_step()
        ada_state["flush"] = True
        while ada_step():
            pass
        sch.barrier()
        if stop == "C":
            dump("AT")
            return nc

        def outproj_phase(w2d, layer, jgate, tok0, ntt, at_fn, at_keys_fn):
            gi = 0
            for d in range(16):
                g = wslot()
                wt = load_w(g, 0, w2d[:, d * 128:(d + 1) * 128], 16, 128, 0)
                for sub in range(ntt):
                    tt = tok0 // 512 + sub
                    b = gi % 4
                    xb = XRES[gi % 4]
                    gi += 1
                    sch.op("sp", lambda e, xb=xb, d=d, tt=tt: e.dma_start(
                        out=xb, in_=xT_d[d, :, tt * 512:(tt + 1) * 512]),
                        reads=[("xT", d, tt)], writes=[("xres", (gi - 1) % 4)], dma=True)
                    gemm(bank(b), ("ps", b), wt, ("W", g, 0), 16, lambda k, sub=sub: at_fn(k, sub),
                         at_keys_fn(sub))
                    sch.op("dve", lambda e, xb=xb, b=b, d=d: e.scalar_tensor_tensor(
                        out=xb, in0=bank(b), scalar=modsT[:, layer, jgate * 16 + d:jgate * 16 + d + 1], in1=xb,
                        op0=ALU.mult, op1=ALU.add),
                        reads=[("ps", b), ("xres", (gi - 1) % 4), ("modsT", layer, jgate)],
                        writes=[("xres", (gi - 1) % 4)])
                    sch.op("sp", lambda e, xb=xb, d=d, tt=tt: e.dma_start(
                        out=xT_d[d, :, tt * 512:(tt + 1) * 512], in_=xb),
                        reads=[("xres", (gi - 1) % 4)], writes=[("xT", d, tt)], dma=True)

        outproj_phase(e_wout, 0, 2, 0, 4, lambda k, sub: AT[:, k, sub * 512:(sub + 1) * 512],
                      lambda sub: [("AT", k, 4 * sub + t) for k in range(16) for t in range(4)])
        sch.barrier()
        if stop == "D":
            dump("xT")
            return nc

        def ffn_phase(layer):
            norm_phase(layer, 1)
            HT = T(OFF_AT, BF16, NF, 1024)
            sg = [T(OFF_HXC + i * 2 * KB, F32, 512) for i in range(2)]
            for half in range(2):
                gi = 0
                for f in range(NF):
                    g = wslot()
                    wg_ = load_w(g, 0, w_gate[layer][:, f * 128:(f + 1) * 128], 16, 128, 0)
                    wu_ = load_w(g, 4 * KB, w_up[layer][:, f * 128:(f + 1) * 128], 16, 128, 1)
                    for sub in range(2):
                        lo = half * 1024 + sub * 512
                        b = gi % 4
                        b2 = (gi + 1) % 4
                        gi += 2
                        gemm(bank(b), ("ps", b), wg_, ("W", g, 0), 16, lambda k, lo=lo: hx_rhs(k, lo, 512),
                             hx_keys(lo, 512))
                        gemm(bank(b2), ("ps", b2), wu_, ("W", g, 1), 16, lambda k, lo=lo: hx_rhs(k, lo, 512),
                             hx_keys(lo, 512))
                        sgt = sg[(gi // 2) % 2]
                        sch.op("act", lambda e, b=b, sgt=sgt: e.activation(out=sgt, in_=bank(b), func=AF.Silu),
                               reads=[("ps", b)], writes=[("sg", (gi // 2) % 2)])
                        sch.op("dve", lambda e, b2=b2, sgt=sgt, f=f, sub=sub: e.tensor_tensor(
                            out=HT[:, f, sub * 512:(sub + 1) * 512], in0=bank(b2), in1=sgt, op=ALU.mult),
                            reads=[("ps", b2), ("sg", (gi // 2) % 2)], writes=[("HT", f, sub)])
                    if layer == 0:
                        ada_step_ffn()
                gi = 0
                for d in range(16):
                    g = wslot()
                    wd_ = load_w(g, 0, w_down[layer][:, d * 128:(d + 1) * 128], NF, 128, 0)
                    for sub in range(2):
                        tt = half * 2 + sub
                        b = 4 + gi % 2
                        xi = gi % 4
                        xb = XRES[xi]
                        gi += 1
                        sch.op("sp", lambda e, xb=xb, d=d, tt=tt: e.dma_start(
                            out=xb, in_=xT_d[d, :, tt * 512:(tt + 1) * 512]),
                            reads=[("xT", d, tt)], writes=[("xres", xi)], dma=True)
                        gemm(bank(b), ("ps", b), wd_, ("W", g, 0), NF,
                             lambda k, sub=sub: HT[:, k, sub * 512:(sub + 1) * 512],
                             [("HT", f, sub) for f in range(NF)])
                        sch.op("dve", lambda e, xb=xb, b=b, d=d: e.scalar_tensor_tensor(
                            out=xb, in0=bank(b), scalar=modsT[:, layer, 5 * 16 + d:5 * 16 + d + 1], in1=xb,
                            op0=ALU.mult, op1=ALU.add),
                            reads=[("ps", b), ("xres", xi), ("modsT", layer, 5)], writes=[("xres", xi)])
                        sch.op("sp", lambda e, xb=xb, d=d, tt=tt: e.dma_start(
                            out=xT_d[d, :, tt * 512:(tt + 1) * 512], in_=xb),
                            reads=[("xres", xi)], writes=[("xT", d, tt)], dma=True)
                    if layer == 0:
                        ada_step_ffn()
            if layer == 0:
                while ada_step_ffn():
                    pass
            sch.barrier()

        ffn_phase(0)
        if stop == "F0":
            dump("xT")
            return nc

        norm_phase(1, 0)
        scr.reset()
        utmp = [scr.alloc(F32, 512) for _ in range(2)]
        mtmp = [scr.alloc(F32, 512) for _ in range(2)]
        Bt = scr.alloc(F32, 16, 128)
        bsbc = scr.alloc(F32, 16, 128)
        junk = scr.alloc(BF16, 256)
        sgwT = scr.alloc(BF16, 16, 128)
        sch.op("pool", lambda e: e.dma_start(out=sgwT, in_=sgwT_d), writes=["sgwT"], dma=True)
        sch.op("sp", lambda e: e.dma_start(out=bsbc.rearrange("p a b -> p (a b)"),
                                           in_=sgb_d.rearrange("a b -> (a b)").partition_broadcast(128)),
               writes=["bsbc"], dma=True)
        for q4 in range(4):
            def rmm(e, q4=q4):
                return e.matmul(bank(q4), lhsT=ones, rhs=sgwT[:, q4 * 4:(q4 + 1) * 4, :].rearrange("p a b -> p (a b)"),
                                start=True, stop=True)
            sch.op("pe", rmm, reads=["ones", "sgwT"], writes=[("ps", q4)])
            for gg in range(4):
                gidx = q4 * 4 + gg
                sch.op("dve", lambda e, q4=q4, gg=gg, gidx=gidx: e.scalar_tensor_tensor(
                    out=Bt[:, gidx, :], in0=bank(q4)[:, gg * 128:(gg + 1) * 128], scalar=vecs[:, 6, gidx:gidx + 1],
                    in1=bsbc[:, gidx, :], op0=ALU.mult, op1=ALU.add),
                    reads=[("ps", q4), "vecs", "bsbc"], writes=[("Bt", gidx)])
        VH = T(OFF_AT, BF16, 8, 2048)
        ATh = T(OFF_AT + 32 * KB, BF16, 16, 1024)
        for half in range(2):
            gi = 0
            for cg in range(8):
                g = wslot()
                wv_ = load_w(g, 0, o_win[:, 2048 + cg * 256:2048 + (cg + 1) * 256], 16, 256, 0)
                for blk in range(8):
                    tok = half * 1024 + blk * 128
                    b = gi % 4
                    gi += 1

                    def vmm(e, tok=tok, b=b, wv_=wv_):
                        for k in range(16):
                            ins = e.matmul(bank(b)[:, 0:256], lhsT=hx_rhs(k, tok, 128), rhs=wv_[:, k, :],
                                           start=(k == 0), stop=(k == 15))
                        return ins
                    sch.op("pe", vmm, reads=wkeys[id(wv_)] + hx_keys(tok, 128), writes=[("ps", b)])
                    sch.op("act", lambda e, b=b, blk=blk, cg=cg: e.activation(
                        out=VH[:, blk, cg * 256:(cg + 1) * 256], in_=bank(b)[:, 0:256], func=AF.Gelu_apprx_tanh,
                        accum_out=st1[:, blk, cg:cg + 1]),
                        reads=[("ps", b)], writes=[("VH", blk, cg), ("st1", blk, cg)])
                    sch.op("act", lambda e, blk=blk, cg=cg: e.activation(
                        out=junk, in_=VH[:, blk, cg * 256:(cg + 1) * 256], func=AF.Square,
                        accum_out=st2[:, blk, cg:cg + 1]),
                        reads=[("VH", blk, cg)], writes=["junk", ("st2", blk, cg)])
            allst = [("st1", b_, c_) for b_ in range(8) for c_ in range(8)] + \
                    [("st2", b_, c_) for b_ in range(8) for c_ in range(8)]
            sch.op("dve", lambda e: e.tensor_reduce(out=lnmu, in_=st1, axis=mybir.AxisListType.X, op=ALU.add),
                   reads=allst, writes=["lnmu"])
            sch.op("dve", lambda e: e.tensor_reduce(out=lnex, in_=st2, axis=mybir.AxisListType.X, op=ALU.add),
                   reads=allst, writes=["lnex"])
            sch.op("dve", lambda e: e.tensor_scalar(out=lnmu, in0=lnmu, scalar1=1.0 / 2048, scalar2=None,
                                                    op0=ALU.mult), reads=["lnmu"], writes=["lnmu"])
            sch.op("dve", lambda e: e.tensor_tensor(out=lnnm, in0=lnmu, in1=lnmu, op=ALU.mult),
                   reads=["lnmu"], writes=["lnnm"])
            sch.op("dve", lambda e: e.scalar_tensor_tensor(out=lnrs, in0=lnex, scalar=1.0 / 2048, in1=lnnm,
                                                           op0=ALU.mult, op1=ALU.subtract),
                   reads=["lnex", "lnnm"], writes=["lnrs"])
            sch.op("act", lambda e: e.activation(out=lnrs, in_=lnrs, func=AF.Sqrt, scale=1.0, bias=epsT),
                   reads=["lnrs", "eps"], writes=["lnrs"])
            sch.op("dve", lambda e: e.reciprocal(out=lnrs, in_=lnrs), reads=["lnrs"], writes=["lnrs"])
            sch.op("dve", lambda e: e.scalar_tensor_tensor(out=lnnm, in0=lnmu, scalar=-1.0, in1=lnrs,
                                                           op0=ALU.mult, op1=ALU.mult),
                   reads=["lnmu", "lnrs"], writes=["lnnm"])
            for blk in range(8):
                sch.op("dve", lambda e, blk=blk: e.tensor_scalar(
                    out=VH[:, blk, :], in0=VH[:, blk, :], scalar1=lnrs[:, blk:blk + 1], scalar2=lnnm[:, blk:blk + 1],
                    op0=ALU.mult, op1=ALU.add),
                    reads=[("VH", blk, c_) for c_ in range(8)] + ["lnrs", "lnnm"],
                    writes=[("VH", blk, c_) for c_ in range(8)])
            gi = 0
            for gch in range(16):
                g = wslot()
                wu_ = load_w(g, 0, o_win[:, gch * 128:(gch + 1) * 128], 16, 128, 0)
                for sub in range(2):
                    lo = half * 1024 + sub * 512
                    b = gi % 2
                    bm = 2 + gi % 2
                    gi += 1
                    gemm(bank(b), ("ps", b), wu_, ("W", g, 0), 16, lambda k, lo=lo: hx_rhs(k, lo, 512),
                         hx_keys(lo, 512))
                    ut = utmp[gi % 2]
                    sch.op("act", lambda e, b=b, ut=ut: e.activation(out=ut, in_=bank(b), func=AF.Gelu_apprx_tanh),
                           reads=[("ps", b)], writes=[("utmp", gi % 2)])

                    def smm(e, sub=sub, gch=gch, bm=bm):
                        for n4 in range(4):
                            blk = sub * 4 + n4
                            ins = e.matmul(bank(bm)[:, n4 * 128:(n4 + 1) * 128],
                                           lhsT=VH[:, blk, gch * 128:(gch + 1) * 128], rhs=sgwT[:, gch, :],
                                           start=True, stop=True)
                        return ins
                    sch.op("pe", smm, reads=["sgwT"] + [("VH", sub * 4 + n4, gch // 2) for n4 in range(4)],
                           writes=[("ps", bm)])
                    mt = mtmp[gi % 2]
                    sch.op("dve", lambda e, bm=bm, mt=mt, gch=gch: e.scalar_tensor_tensor(
                        out=mt.rearrange("p (a b) -> p a b", a=4),
                        in0=bank(bm).rearrange("p (a b) -> p a b", a=4), scalar=vecs[:, 5, gch:gch + 1],
                        in1=Bt[:, gch:gch + 1, :].broadcast_to([128, 4, 128]), op0=ALU.mult, op1=ALU.add),
                        reads=[("ps", bm), "vecs", ("Bt", gch)], writes=[("mtmp", gi % 2)])
                    sch.op("dve", lambda e, mt=mt, ut=ut, gch=gch, sub=sub: e.tensor_tensor(
                        out=ATh[:, gch, sub * 512:(sub + 1) * 512], in0=mt, in1=ut, op=ALU.mult),
                        reads=[("mtmp", gi % 2), ("utmp", gi % 2)], writes=[("ATh", gch, sub)])
            if stop == "G" and half == 0:
                for k in range(16):
                    for sub in range(2):
                        xb = XRES[(k * 2 + sub) % 4]
                        sch.op("dve", lambda e, xb=xb, k=k, sub=sub: e.tensor_copy(
                            out=xb, in_=ATh[:, k, sub * 512:(sub + 1) * 512]),
                            reads=[("ATh", k, sub)], writes=[("xres", (k * 2 + sub) % 4)])
                        sch.op("sp", lambda e, xb=xb, k=k, sub=sub: e.dma_start(
                            out=dbg_d[k, :, sub * 512:(sub + 1) * 512], in_=xb),
                            reads=[("xres", (k * 2 + sub) % 4)], writes=[("dbg", k, sub)], dma=True)
                sch.barrier()
                sch.op("sp", lambda e: None)
                sch.emit()
                return nc
            outproj_phase(o_wout, 1, 2, half * 1024, 2, lambda k, sub: ATh[:, k, sub * 512:(sub + 1) * 512],
                          lambda sub: [("ATh", k, sub) for k in range(16)])
            sch.barrier()
        if stop == "H":
            dump("xT")
            return nc

        ffn_phase(1)

        scr.reset()
        nbF = norm_bufs()
        otile = [scr.alloc(F32, 2048) for _ in range(2)]
        XTf = T(OFF_AT, F32, 16, 512)
        FIN = T(OFF_AT + 32 * KB, F32, 16, 512)
        aco_keys[0] = ["vecs"]
        oi = 0
        for tt in range(4):
            for k in range(16):
                sch.op("sp", lambda e, k=k, tt=tt: e.dma_start(out=XTf[:, k, :], in_=xT_d[k, :, tt * 512:(tt + 1) * 512]),
                       reads=[("xT", k, tt)], writes=[("XT", 0, k)], dma=True)
            norm_tile(XTf, [("XT", 0, k) for k in range(16)], 512, vecs[:, 4, :], None, None, None, nbF,
                      final=True, fin_out=FIN)
            for t in range(4):
                ot = otile[oi % 2]
                for q4 in range(4):
                    b = (oi * 4 + q4) % 4

                    def trb(e, t=t, q4=q4, b=b):
                        for kk in range(4):
                            k = q4 * 4 + kk
                            ins = e.transpose(bank(b)[:, kk * 128:(kk + 1) * 128], FIN[:, k, t * 128:(t + 1) * 128],
                                              ident)
                        return ins
                    sch.op("pe", trb, reads=["ident"] + [("fin", q4 * 4 + kk) for kk in range(4)],
                           writes=[("ps", b)])
                    sch.op("act" if q4 % 2 == 0 else "dve",
                           (lambda e, ot=ot, q4=q4, b=b: e.copy(out=ot[:, q4 * 512:(q4 + 1) * 512], in_=bank(b)))
                           if q4 % 2 == 0 else
                           (lambda e, ot=ot, q4=q4, b=b: e.tensor_copy(out=ot[:, q4 * 512:(q4 + 1) * 512], in_=bank(b))),
                           reads=[("ps", b)], writes=[("ot", oi % 2, q4)])
                r0 = tt * 512 + t * 128
                sch.op("sp", lambda e, ot=ot, r0=r0: e.dma_start(out=out_d[r0:r0 + 128, :], in_=ot),
                       reads=[("ot", oi % 2, q4) for q4 in range(4)], writes=[("out", r0)], dma=True)
                oi += 1
        sch.barrier()
        sch.op("sp", lambda e: None)
        sch.emit()
    return nc


def _vec_pc(v):
    return np.ascontiguousarray(np.asarray(v, np.float32).reshape(16, 128).T)


def _bias_tables(rpb):
    tiles = [(6, 6 + d) for d in range(-2, 3)]
    for m in (0, 1):
        tiles += [(m, p) for p in range(4)]
    for m in (14, 15):
        tiles += [(m, p) for p in range(12, 16)]
    kk = np.arange(128)
    ki, kc = kk // 64, kk % 64
    qi, qc = kk // 64, kk % 64
    bias = np.zeros((128, 8, 21, 128), np.float32)
    mask = np.zeros((128, 21, 128), np.float32)
    for t, (m, p) in enumerate(tiles):
        kr = (2 * p + ki)[:, None]
        r = (2 * m + qi)[None, :]
        rs = np.clip(r - 4, 0, 24)
        okr = (kr >= rs) & (kr < rs + 8)
        cs = np.clip(qc - 8, 0, 48)[None, :]
        okc = (kc[:, None] >= cs) & (kc[:, None] < cs + 16)
        dr = np.clip(kr - r + 7, 0, 14)
        dc = np.clip(kc[:, None] - qc[None, :], -15, 15) + 15
        bias[:, :, t, :] = rpb[:, dr, dc].transpose(1, 0, 2)
        mask[:, t, :] = (okr & okc)
    return bias, mask


def prep_inputs(inputs):
    f = lambda a: np.ascontiguousarray(np.asarray(a, np.float32))
    x = f(inputs["x"])
    c = f(inputs["c"])
    ctx = f(inputs["ctx"])
    c_ctx = f(inputs["c_ctx"])
    ada_b = f(inputs["ada_b"])
    adabT = np.ascontiguousarray(ada_b.reshape(2, 96, 128).transpose(2, 0, 1))
    vecs = np.stack([_vec_pc(inputs["norm_mix_g"][0]), _vec_pc(inputs["norm_mix_g"][1]),
                     _vec_pc(inputs["norm_ffn_g"][0]), _vec_pc(inputs["norm_ffn_g"][1]),
                     _vec_pc(inputs["final_norm_g"]), _vec_pc(inputs["sg_ln_g"][0]),
                     _vec_pc(inputs["sg_ln_b"][0])], axis=1)
    convw = np.ascontiguousarray(f(inputs["conv_w"])[0].reshape(3, 8, 128).transpose(2, 1, 0))
    bias, mask = _bias_tables(f(inputs["rpb"])[0])
    sgwT = np.ascontiguousarray(f(inputs["sg_w"])[0].transpose(2, 0, 1))
    shared = {
        "ada_w": f(inputs["ada_w"]), "adabT": adabT, "vecs": np.ascontiguousarray(vecs), "convw": convw,
        "ffn_w_gate": f(inputs["ffn_w_gate"]), "ffn_w_up": f(inputs["ffn_w_up"]),
        "ffn_w_down": f(inputs["ffn_w_down"]), "even_w_in": f(inputs["even_w_in"])[0],
        "even_w_out": f(inputs["even_w_out"])[0], "biastab": bias, "masktab": mask,
        "odd_w_in": f(inputs["odd_w_in"])[0], "odd_w_out": f(inputs["odd_w_out"])[0],
        "sgwT": sgwT, "sgb": f(inputs["sg_b"])[0], "ident": np.eye(128, dtype=np.float32),
    }
    in_maps = []
    for b in range(x.shape[0]):
        cT = np.stack([_vec_pc(c[b]), _vec_pc(c_ctx)], axis=2)
        m = dict(shared)
        m["x"] = x[b]
        m["ctx"] = ctx[b]
        m["cT"] = np.ascontiguousarray(cT)
        in_maps.append(m)
    return in_maps


_NC_CACHE = {}


def kernel(**inputs):
    in_maps = prep_inputs(inputs)
    if "nc" not in _NC_CACHE:
        _NC_CACHE["nc"] = build()
    nc = _NC_CACHE["nc"]
    res = run_bass_kernel_spmd(nc, in_maps, core_ids=list(range(8)))
    return np.stack([r["out"] for r in res.results], axis=0).astype(np.float32)
```

```python
import math
import os
from contextlib import ExitStack
import numpy as np
import concourse.bass as bass
import concourse.mybir as mybir
from concourse.bass_utils import run_bass_kernel_spmd

F32 = mybir.dt.float32
BF16 = mybir.dt.bfloat16
AF = mybir.ActivationFunctionType
ALU = mybir.AluOpType
KB = 1024


class _Op:
    __slots__ = ("eng", "fn", "deps", "dma", "needed", "sem", "val", "idx")

    def __init__(self, eng, fn, dma):
        self.eng = eng
        self.fn = fn
        self.dma = dma
        self.deps = set()
        self.needed = False
        self.sem = None
        self.val = None


class Sched:
    def __init__(self, nc, stack, n_dma_sems=20):
        self.nc = nc
        self.engs = {"pe": nc.tensor, "act": nc.scalar, "dve": nc.vector,
                     "pool": nc.gpsimd, "sp": nc.sync}
        self.ops = []
        self.lw = {}
        self.rd = {}
        self.stack = stack
        self.n_dma_sems = n_dma_sems
        self.last_eng_op = {}
        self.dmas_since_barrier = []
        self.pending_barrier = {}

    def op(self, eng, fn, reads=(), writes=(), dma=False, exempt=False):
        o = _Op(eng, fn, dma)
        o.idx = len(self.ops)
        for k in reads:
            w = self.lw.get(k)
            if w is not None:
                o.deps.add(w)
        for k in writes:
            w = self.lw.get(k)
            if w is not None:
                o.deps.add(w)
            for r in self.rd.get(k, ()):
                o.deps.add(r)
        if not exempt and eng in self.pending_barrier:
            for d in self.pending_barrier.pop(eng):
                o.deps.add(d)
        o.deps.discard(o)
        for k in reads:
            self.rd.setdefault(k, []).append(o)
        for k in writes:
            self.lw[k] = o
            self.rd[k] = []
        self.ops.append(o)
        if dma:
            self.dmas_since_barrier.append(o)
        else:
            self.last_eng_op[eng] = o
        return o

    def barrier(self):
        tails = list(self.last_eng_op.values()) + list(self.dmas_since_barrier)
        self.dmas_since_barrier = []
        for e in self.engs:
            prev = self.pending_barrier.get(e, [])
            self.pending_barrier[e] = prev + tails

    def emit(self):
        nc = self.nc
        for o in self.ops:
            for d in o.deps:
                if (not d.dma) and d.eng == "pe" and o.eng == "pe" and not o.dma:
                    continue
                d.needed = True
        esem = {e: self.stack.enter_context(nc.semaphore("s_" + e)) for e in self.engs}
        dsem = {e: [self.stack.enter_context(nc.semaphore("d_%s%d" % (e, i)))
                    for i in range(self.n_dma_sems)] for e in ("sp", "pool", "act")}
        dcount = {e: [0] * self.n_dma_sems for e in dsem}
        drr = {e: 0 for e in dsem}
        ecount = {e: 0 for e in self.engs}
        seen = {e: {} for e in self.engs}

        def need(e, sem, val):
            key = id(sem)
            if seen[e].get(key, 0) >= val:
                return
            seen[e][key] = val
            self.engs[e].wait_ge(sem, val)

        for o in self.ops:
            e = o.eng
            if o.dma:
                i = drr[e]
                drr[e] = (i + 1) % self.n_dma_sems
                o.sem = dsem[e][i]
                dcount[e][i] += 16
                o.val = dcount[e][i]
                if o.val > 16:
                    need(e, o.sem, o.val - 16)
            for d in sorted(o.deps, key=lambda x: x.idx):
                if (not d.dma) and d.eng == "pe" and e == "pe" and not o.dma:
                    continue
                need(e, d.sem, d.val)
            ins = o.fn(self.engs[e])
            if ins is None:
                assert not o.needed and not o.dma
                continue
            if o.dma:
                ins.then_inc(o.sem, 16)
            elif o.needed:
                ecount[e] += 1
                o.sem = esem[e]
                o.val = ecount[e]
                ins.then_inc(o.sem, 1)
        self.ecount = ecount


D = 2048
S = 2048
CTX = 256
FF = 5632
NF = FF // 128
SCALE = 128 ** -0.5
EPS = 1e-6

OFF_MISC = 0
OFF_XRES = 8 * KB
OFF_W = 16 * KB
OFF_HX = 40 * KB
OFF_HXC = 104 * KB
OFF_AT = 112 * KB
OFF_SCR = 176 * KB
ARENA_BYTES = 207 * KB
WG_BYTES = 12 * KB


def pairs_of(m):
    if m <= 1:
        return [0, 1, 2, 3]
    if m >= 14:
        return [12, 13, 14, 15]
    return list(range(m - 2, m + 3))


def tile_base(m):
    if m == 0:
        return 5
    if m == 1:
        return 9
    if m == 14:
        return 13
    if m == 15:
        return 17
    return 0


def build(stop=None):
    nc = bass.Bass("TRN2", target_bir_lowering=False)

    def din(name, shape, dt=F32):
        return nc.dram_tensor(name, list(shape), dt, kind="ExternalInput").ap()

    x_d = din("x", [S, D])
    ctx_d = din("ctx", [CTX, D])
    cT_d = din("cT", [128, 16, 2])
    ada_w = din("ada_w", [2, D, 6 * D])
    adabT_d = din("adabT", [128, 2, 96])
    vecs_d = din("vecs", [128, 7, 16])
    convw_d = din("convw", [128, 8, 3])
    w_gate = din("ffn_w_gate", [2, D, FF])
    w_up = din("ffn_w_up", [2, D, FF])
    w_down = din("ffn_w_down", [2, FF, D])
    e_win = din("even_w_in", [D, 6144])
    e_wout = din("even_w_out", [D, D])
    bias_d = din("biastab", [128, 8, 21, 128])
    mask_d = din("masktab", [128, 21, 128])
    o_win = din("odd_w_in", [D, 4096])
    o_wout = din("odd_w_out", [D, D])
    sgwT_d = din("sgwT", [128, 16, 128])
    sgb_d = din("sgb", [16, 128])
    ident_d = din("ident", [128, 128])
    out_d = nc.dram_tensor("out", [S, D], F32, kind="ExternalOutput").ap()
    dbg_d = None
    if stop is not None:
        dbg_d = nc.dram_tensor("dbg", [16, 128, 2304], F32, kind="ExternalOutput").ap()

    xT_d = nc.dram_tensor("xT_scr", [16, 128, S], F32).ap()
    modrow_d = nc.dram_tensor("modrow_scr", [2, 2, 6 * D], F32).ap()
    etab_d = nc.dram_tensor("etab_scr", [8, 128, 21 * 128], BF16).ap()

    with ExitStack() as st:
        sch = Sched(nc, st)
        arena = st.enter_context(nc.sbuf_tensor("arena", [128, ARENA_BYTES // 2], BF16))
        PS = [st.enter_context(nc.psum_tensor("ps%d" % i, [128, 1024], F32)) for i in range(4)]

        def bank(i):
            return PS[i // 2][:, (i % 2) * 512:(i % 2) * 512 + 512]

        def T(off, dt, *shape):
            n = 1
            for s_ in shape:
                n *= s_
            esz = 2 if dt == BF16 else 4
            a = arena[:, off // 2: off // 2 + n * esz // 2]
            if dt != BF16:
                a = a.bitcast(dt)
            if len(shape) == 2:
                a = a.rearrange("p (a b) -> p a b", a=shape[0])
            elif len(shape) == 3:
                a = a.rearrange("p (a b c) -> p a b c", a=shape[0], b=shape[1])
            return a

        class Bump:
            def __init__(self, base, size):
                self.base = base
                self.size = size
                self.cur = base

            def reset(self):
                self.cur = self.base

            def alloc(self, dt, *shape):
                n = 1
                for s_ in shape:
                    n *= s_
                nb = n * (2 if dt == BF16 else 4)
                nb = (nb + 31) // 32 * 32
                off = self.cur
                self.cur += nb
                assert self.cur <= self.base + self.size, ("bump overflow", self.cur - self.base, self.size)
                return T(off, dt, *shape)

        misc = Bump(OFF_MISC, 8 * KB)
        scr = Bump(OFF_SCR, 31 * KB)
        atr = Bump(OFF_AT, 64 * KB)

        ident = misc.alloc(F32, 128)
        ones = misc.alloc(BF16, 128)
        modsT = misc.alloc(F32, 2, 96)
        modsC = misc.alloc(F32, 32)
        adabT = misc.alloc(F32, 2, 96)
        vecs = misc.alloc(F32, 7, 16)
        Aco = misc.alloc(F32, 4, 16)
        AcoC = misc.alloc(F32, 16)
        convw = misc.alloc(F32, 8, 3)
        epsT = misc.alloc(F32, 1)
        cT = misc.alloc(F32, 16, 2)
        s2 = misc.alloc(BF16, 16, 2)
        st1 = misc.alloc(F32, 8, 8)
        st2 = misc.alloc(F32, 8, 8)
        lnmu = misc.alloc(F32, 8)
        lnex = misc.alloc(F32, 8)
        lnrs = misc.alloc(F32, 8)
        lnnm = misc.alloc(F32, 8)

        XRES = [T(OFF_XRES + i * 2 * KB, F32, 512) for i in range(4)]
        HX = T(OFF_HX, BF16, 16, 2048)
        HXC = T(OFF_HXC, BF16, 16, 256)
        AT = T(OFF_AT, BF16, 16, 2048)

        def hx_rhs(k, lo, n):
            if lo >= 2048:
                return HXC[:, k, lo - 2048: lo - 2048 + n]
            return HX[:, k, lo: lo + n]

        def hx_keys(lo, n):
            return [("HX", k, t) for k in range(16) for t in range(lo // 128, (lo + n) // 128)]

        wstate = {"i": 0}

        def wslot():
            g = wstate["i"] % 2
            wstate["i"] += 1
            return g

        def WT(g, off, kc, n):
            return T(OFF_W + g * WG_BYTES + off, BF16, kc, n)

        def load_w(g, off, src2d, kc, n, key_extra=0):
            t = WT(g, off, kc, n)
            src = src2d.rearrange("(kc p) n -> p kc n", p=128)
            nbytes = kc * n * 2
            keys = [("W", g, s_) for s_ in range(off // (4 * KB), (off + nbytes + 4 * KB - 1) // (4 * KB))]
            sch.op("pool", lambda e: e.dma_start(out=t, in_=src), writes=keys, dma=True, exempt=True)
            wkeys[id(t)] = keys
            return t

        wkeys = {}

        dve_copy = lambda out, in_: (lambda e: e.tensor_copy(out=out, in_=in_))

        sch.op("sp", lambda e: e.dma_start(out=ident, in_=ident_d), writes=["ident"], dma=True)
        sch.op("sp", lambda e: e.dma_start(out=adabT, in_=adabT_d), writes=["adabT"], dma=True)
        sch.op("sp", lambda e: e.dma_start(out=vecs, in_=vecs_d), writes=["vecs"], dma=True)
        sch.op("sp", lambda e: e.dma_start(out=convw, in_=convw_d), writes=["convw"], dma=True)
        sch.op("sp", lambda e: e.dma_start(out=cT, in_=cT_d), writes=["cT"], dma=True)
        sch.op("dve", lambda e: e.memset(ones, 1.0), writes=["ones"])
        sch.op("dve", lambda e: e.memset(epsT, EPS), writes=["eps"])
        sch.op("act", lambda e: e.activation(out=s2, in_=cT, func=AF.Silu), reads=["cT"], writes=["s2"])

        def ada(layer, jlist, with_ctx, only=None, fin=True):
            rowst = [scr_rows[i] for i in range(4)]
            cnt = 0
            for j in jlist:
                for n in range(8):
                    if only is not None and (j * 8 + n) not in only:
                        cnt += 1
                        continue
                    col0 = j * 2048 + n * 256
                    g = wslot()
                    wt = load_w(g, 0, ada_w[layer][:, col0:col0 + 256], 16, 256)
                    b = cnt % 2
                    psb = bank(b)

                    def mm(e, wt=wt, psb=psb):
                        for k in range(16):
                            ins = e.matmul(psb[0:2, 0:256], lhsT=s2[:, k, :], rhs=wt[:, k, :],
                                           start=(k == 0), stop=(k == 15))
                        return ins
                    sch.op("pe", mm, reads=wkeys[id(wt)] + ["s2"], writes=[("ps", b)])
                    rt = rowst[cnt % 4]
                    sch.op("dve", dve_copy(rt[0:2, :], psb[0:2, 0:256]), reads=[("ps", b)],
                           writes=[("row", cnt % 4)])
                    sch.op("sp", lambda e, rt=rt, col0=col0: e.dma_start(
                        out=modrow_d[layer, :, col0:col0 + 256], in_=rt[0:2, :]),
                        reads=[("row", cnt % 4)], writes=[("modrow", layer, j)], dma=True)
                    cnt += 1
            if not fin:
                return
            for j in jlist:
                src = modrow_d[layer, 0, j * 2048:(j + 1) * 2048].rearrange("(c p) -> p c", p=128)
                sch.op("sp", lambda e, src=src, j=j: e.dma_start(
                    out=modsT[:, layer, j * 16:(j + 1) * 16], in_=src, allow_slow_non_contiguous=True),
                    reads=[("modrow", layer, j)], writes=[("modsT", layer, j)], dma=True)
                sch.op("dve", lambda e, j=j: e.tensor_tensor(
                    out=modsT[:, layer, j * 16:(j + 1) * 16], in0=modsT[:, layer, j * 16:(j + 1) * 16],
                    in1=adabT[:, layer, j * 16:(j + 1) * 16], op=ALU.add),
                    reads=[("modsT", layer, j), "adabT"], writes=[("modsT", layer, j)])
                if with_ctx and j < 2:
                    srcc = modrow_d[layer, 1, j * 2048:(j + 1) * 2048].rearrange("(c p) -> p c", p=128)
                    sch.op("sp", lambda e, srcc=srcc, j=j: e.dma_start(
                        out=modsC[:, j * 16:(j + 1) * 16], in_=srcc, allow_slow_non_contiguous=True),
                        reads=[("modrow", layer, j)], writes=[("modsC", j)], dma=True)
                    sch.op("dve", lambda e, j=j: e.tensor_tensor(
                        out=modsC[:, j * 16:(j + 1) * 16], in0=modsC[:, j * 16:(j + 1) * 16],
                        in1=adabT[:, layer, j * 16:(j + 1) * 16], op=ALU.add),
                        reads=[("modsC", j), "adabT"], writes=[("modsC", j)])

        def mk_aco(layer, which):
            gi = (0 if which == 0 else 2) + layer
            j = 1 if which == 0 else 4
            sch.op("dve", lambda e: e.scalar_tensor_tensor(
                out=Aco[:, layer * 2 + which, :], in0=modsT[:, layer, j * 16:(j + 1) * 16], scalar=1.0,
                in1=vecs[:, gi, :], op0=ALU.add, op1=ALU.mult),
                reads=[("modsT", layer, j), "vecs"], writes=[("Aco", layer, which)])

        scr.reset()
        scr_rows = [scr.alloc(F32, 256) for _ in range(4)]
        mk = atr.alloc(F32, 21 * 128)
        bst = [atr.alloc(F32, 21 * 128) for _ in range(1)]
        eo = [atr.alloc(BF16, 21 * 128) for _ in range(2)]
        sch.op("sp", lambda e: e.dma_start(out=mk, in_=mask_d.rearrange("p a b -> p (a b)")), writes=["mk"],
               dma=True)
        for h in range(8):
            sch.op("sp", lambda e, h=h: e.dma_start(out=bst[0], in_=bias_d[:, h].rearrange("p a b -> p (a b)")),
                   writes=["bst"], dma=True)
            sch.op("act", lambda e: e.activation(out=bst[0], in_=bst[0], func=AF.Exp), reads=["bst"],
                   writes=["bst"])
            sch.op("dve", lambda e, h=h: e.tensor_tensor(out=eo[h % 2], in0=bst[0], in1=mk, op=ALU.mult),
                   reads=["bst", "mk"], writes=[("eo", h % 2)])
            sch.op("sp", lambda e, h=h: e.dma_start(out=etab_d[h], in_=eo[h % 2]), reads=[("eo", h % 2)],
                   writes=[("etab", h)], dma=True)
            ada(0, [0, 1], True, only=(2 * h, 2 * h + 1), fin=False)
        ada(0, [0, 1], True, only=())
        mk_aco(0, 0)
        sch.op("dve", lambda e: e.scalar_tensor_tensor(
            out=AcoC, in0=modsC[:, 16:32], scalar=1.0, in1=vecs[:, 0, :], op0=ALU.add, op1=ALU.mult),
            reads=[("modsC", 1), "vecs"], writes=["AcoC"])
        ADAW = [T(OFF_XRES + i * 4 * KB, BF16, 16, 128) for i in range(2)]
        ROWS = [T(OFF_SCR + 30 * KB + i * 512, F32, 128) for i in range(2)]
        ada_todo = [(l_, j_, n_) for (l_, j_) in [(0, 2), (0, 3), (0, 4)] for n_ in range(16)]
        ada_todo2 = [(0, 5, n_) for n_ in range(16)] + [(1, j_, n_) for j_ in range(6) for n_ in range(16)]
        ada_state = {"t": 0, "t2": 0}
        ADAW2 = T(OFF_HXC + 4 * KB, BF16, 16, 128)

        def ada_step_ffn():
            t = ada_state["t2"]
            if t >= len(ada_todo2):
                return False
            ada_state["t2"] = t + 1
            layer, j, n = ada_todo2[t]
            i = t % 2
            col0 = j * 2048 + n * 128
            wt = ADAW2
            src = ada_w[layer][:, col0:col0 + 128].rearrange("(kc p) n -> p kc n", p=128)
            sch.op("pool", lambda e: e.dma_start(out=wt, in_=src), writes=["adaw2"], dma=True, exempt=True)
            psa = bank(6 + i)[0:2, 0:128]

            def mm(e):
                for k in range(16):
                    ins = e.matmul(psa, lhsT=s2[:, k, :], rhs=wt[:, k, :], start=(k == 0), stop=(k == 15))
                return ins
            sch.op("pe", mm, reads=["adaw2", "s2"], writes=[("ps", 6 + i)])
            rt = ROWS[i]
            sch.op("act", lambda e: e.copy(out=rt[0:2, :], in_=psa), reads=[("ps", 6 + i)], writes=[("arow", i)])
            sch.op("sp", lambda e: e.dma_start(out=modrow_d[layer, :, col0:col0 + 128], in_=rt[0:2, :]),
                   reads=[("arow", i)], writes=[("modrow", layer, j)], dma=True)
            if n == 15:
                srcm = modrow_d[layer, 0, j * 2048:(j + 1) * 2048].rearrange("(c p) -> p c", p=128)
                sch.op("sp", lambda e: e.dma_start(
                    out=modsT[:, layer, j * 16:(j + 1) * 16], in_=srcm, allow_slow_non_contiguous=True),
                    reads=[("modrow", layer, j)], writes=[("modsT", layer, j)], dma=True)
                sch.op("dve", lambda e: e.tensor_tensor(
                    out=modsT[:, layer, j * 16:(j + 1) * 16], in0=modsT[:, layer, j * 16:(j + 1) * 16],
                    in1=adabT[:, layer, j * 16:(j + 1) * 16], op=ALU.add),
                    reads=[("modsT", layer, j), "adabT"], writes=[("modsT", layer, j)])
                if j == 1:
                    mk_aco(layer, 0)
                if j == 4:
                    mk_aco(layer, 1)
            return True

        def ada_step():
            if os.environ.get("NOADA", "") == "1" and not ada_state.get("flush"):
                return False
            t = ada_state["t"]
            if t >= len(ada_todo):
                return False
            ada_state["t"] = t + 1
            layer, j, n = ada_todo[t]
            i = t % 2
            col0 = j * 2048 + n * 128
            wt = ADAW[i]
            src = ada_w[layer][:, col0:col0 + 128].rearrange("(kc p) n -> p kc n", p=128)
            sch.op("pool", lambda e: e.dma_start(out=wt, in_=src), writes=[("adaw", i)], dma=True, exempt=True)
            psa = bank(6 + i)[0:2, 256:384]

            def mm(e):
                for k in range(16):
                    ins = e.matmul(psa, lhsT=s2[:, k, :], rhs=wt[:, k, :], start=(k == 0), stop=(k == 15))
                return ins
            sch.op("pe", mm, reads=[("adaw", i), "s2"], writes=[("ps", 6 + i)])
            rt = ROWS[i]
            sch.op("act", lambda e: e.copy(out=rt[0:2, :], in_=psa), reads=[("ps", 6 + i)], writes=[("arow", i)])
            sch.op("sp", lambda e: e.dma_start(out=modrow_d[layer, :, col0:col0 + 128], in_=rt[0:2, :]),
                   reads=[("arow", i)], writes=[("modrow", layer, j)], dma=True)
            if n == 15:
                srcm = modrow_d[layer, 0, j * 2048:(j + 1) * 2048].rearrange("(c p) -> p c", p=128)
                sch.op("sp", lambda e: e.dma_start(
                    out=modsT[:, layer, j * 16:(j + 1) * 16], in_=srcm, allow_slow_non_contiguous=True),
                    reads=[("modrow", layer, j)], writes=[("modsT", layer, j)], dma=True)
                sch.op("dve", lambda e: e.tensor_tensor(
                    out=modsT[:, layer, j * 16:(j + 1) * 16], in0=modsT[:, layer, j * 16:(j + 1) * 16],
                    in1=adabT[:, layer, j * 16:(j + 1) * 16], op=ALU.add),
                    reads=[("modsT", layer, j), "adabT"], writes=[("modsT", layer, j)])
                if j == 1:
                    mk_aco(layer, 0)
                if j == 4:
                    mk_aco(layer, 1)
            return True

        sch.barrier()

        npar = {"i": 0}

        def norm_tile(XT, xkeys, ntok, aco, bvec, out_fn, okeys_fn, nb, final=False, fin_out=None):
            sq, rstds, tmp = nb
            par = npar["i"] % 2
            npar["i"] += 1
            rstd = rstds[par]
            pss = bank(6 + par)
            RK = ("rstd", par)
            PK = ("ps", 6 + par)
            for k in range(16):
                sb = sq[k % 2]
                sch.op("act", lambda e, k=k, sb=sb: e.activation(out=sb[:, 0:ntok], in_=XT[:, k, 0:ntok],
                                                                func=AF.Square),
                       reads=[xkeys[k]], writes=[("sq", k % 2)])
                sch.op("pe", lambda e, k=k, sb=sb: e.matmul(pss[:, 0:ntok], lhsT=ones, rhs=sb[:, 0:ntok],
                                                            start=(k == 0), stop=(k == 15)),
                       reads=[("sq", k % 2), "ones"], writes=[PK])
            sch.op("act", lambda e: e.activation(out=rstd[:, 0:ntok], in_=pss[:, 0:ntok], func=AF.Sqrt,
                                                 scale=1.0 / D, bias=epsT),
                   reads=[PK, "eps"], writes=[RK])
            sch.op("dve", lambda e: e.reciprocal(out=rstd[:, 0:ntok], in_=rstd[:, 0:ntok]),
                   reads=[RK], writes=[RK])
            for k in range(16):
                if final:
                    sch.op("dve", lambda e, k=k: e.scalar_tensor_tensor(
                        out=fin_out[:, k, 0:ntok], in0=XT[:, k, 0:ntok], scalar=aco[:, k:k + 1],
                        in1=rstd[:, 0:ntok], op0=ALU.mult, op1=ALU.mult),
                        reads=[xkeys[k], RK] + aco_keys[0], writes=[("fin", k)])
                    continue
                tb = tmp[k % 2]
                sch.op("dve", lambda e, k=k, tb=tb: e.scalar_tensor_tensor(
                    out=tb[:, 0:ntok], in0=XT[:, k, 0:ntok], scalar=aco[:, k:k + 1], in1=rstd[:, 0:ntok],
                    op0=ALU.mult, op1=ALU.mult),
                    reads=[xkeys[k], RK] + aco_keys[0], writes=[("ntmp", k % 2)])
                sch.op("act", lambda e, k=k, tb=tb: e.activation(out=out_fn(k), in_=tb[:, 0:ntok],
                                                                func=AF.Identity, bias=bvec[:, k:k + 1]),
                       reads=[("ntmp", k % 2)] + aco_keys[0], writes=okeys_fn(k))

        aco_keys = [[]]

        def norm_bufs():
            sq = [scr.alloc(BF16, 512) for _ in range(2)]
            rstd = [scr.alloc(F32, 512) for _ in range(2)]
            tmp = [scr.alloc(F32, 512) for _ in range(2)]
            return sq, rstd, tmp

        def norm_phase(layer, which):
            scr.reset()
            nb = norm_bufs()
            XTs = [T(OFF_AT + i * 32 * KB, F32, 16, 512) for i in range(2)]
            j_shift = 0 if which == 0 else 3
            aco = Aco[:, layer * 2 + which, :]
            bvec = modsT[:, layer, j_shift * 16:(j_shift + 1) * 16]
            aco_keys[0] = [("Aco", layer, which), ("modsT", layer, j_shift)]
            for tt in range(4):
                XT = XTs[tt % 2]
                for k in range(16):
                    sch.op("sp", lambda e, k=k, tt=tt, XT=XT: e.dma_start(
                        out=XT[:, k, :], in_=xT_d[k, :, tt * 512:(tt + 1) * 512]),
                        reads=[("xT", k, tt)], writes=[("XT", tt % 2, k)], dma=True)
                norm_tile(XT, [("XT", tt % 2, k) for k in range(16)], 512, aco, bvec,
                          lambda k, tt=tt: HX[:, k, tt * 512:(tt + 1) * 512],
                          lambda k, tt=tt: [("HX", k, t) for t in range(tt * 4, tt * 4 + 4)], nb)
            sch.barrier()

        scr.reset()
        nbA = norm_bufs()
        xin = T(OFF_AT, F32, 4, 2048)
        XTa = T(OFF_AT + 32 * KB, F32, 16, 512)
        aco_keys[0] = [("Aco", 0, 0), ("modsT", 0, 0)]
        for tt in range(5):
            nblk = 4 if tt < 4 else 2
            ntok = nblk * 128
            srcd = x_d if tt < 4 else ctx_d
            for t in range(nblk):
                r0 = (tt * 512 if tt < 4 else 0) + t * 128
                sch.op("sp", lambda e, t=t, r0=r0, srcd=srcd: e.dma_start(out=xin[:, t, :],
                                                                          in_=srcd[r0:r0 + 128, :]),
                       writes=[("xin", t)], dma=True)
            for k in range(16):
                b = k % 2
                psb = bank(b)

                def tr(e, k=k, psb=psb, nblk=nblk):
                    for t in range(nblk):
                        ins = e.transpose(psb[:, t * 128:(t + 1) * 128], xin[:, t, k * 128:(k + 1) * 128], ident)
                    return ins
                sch.op("pe", tr, reads=["ident"] + [("xin", t) for t in range(nblk)], writes=[("ps", b)])
                sch.op("dve" if k % 2 == 0 else "act",
                       (lambda e, k=k, psb=psb, ntok=ntok: e.tensor_copy(out=XTa[:, k, 0:ntok], in_=psb[:, 0:ntok]))
                       if k % 2 == 0 else
                       (lambda e, k=k, psb=psb, ntok=ntok: e.copy(out=XTa[:, k, 0:ntok], in_=psb[:, 0:ntok])),
                       reads=[("ps", b)], writes=[("XTa", k)])
                if tt < 4:
                    sch.op("sp", lambda e, k=k, tt=tt: e.dma_start(out=xT_d[k, :, tt * 512:(tt + 1) * 512],
                                                                   in_=XTa[:, k, :]),
                           reads=[("XTa", k)], writes=[("xT", k, tt)], dma=True)
            if tt < 4:
                norm_tile(XTa, [("XTa", k) for k in range(16)], 512, Aco[:, 0, :], modsT[:, 0, 0:16],
                          lambda k, tt=tt: HX[:, k, tt * 512:(tt + 1) * 512],
                          lambda k, tt=tt: [("HX", k, t) for t in range(tt * 4, tt * 4 + 4)], nbA)
            else:
                aco_keys[0] = ["AcoC", ("modsC", 0)]
                norm_tile(XTa, [("XTa", k) for k in range(16)], 256, AcoC, modsC[:, 0:16],
                          lambda k: HXC[:, k, :],
                          lambda k: [("HX", k, 16), ("HX", k, 17)], nbA)
        sch.barrier()

        def dump(kind):
            if kind == "HX":
                st_ = scr.alloc(F32, 2304) if False else None
                for k in range(16):
                    for tt in range(5):
                        n = 512 if tt < 4 else 256
                        xb = XRES[(k * 5 + tt) % 4]
                        srcv = HX[:, k, tt * 512:(tt + 1) * 512] if tt < 4 else HXC[:, k, :]
                        sch.op("dve", lambda e, xb=xb, srcv=srcv, n=n: e.tensor_copy(out=xb[:, 0:n], in_=srcv),
                               reads=hx_keys(tt * 512, n), writes=[("xres", (k * 5 + tt) % 4)])
                        sch.op("sp", lambda e, xb=xb, k=k, tt=tt, n=n: e.dma_start(
                            out=dbg_d[k, :, tt * 512:tt * 512 + n], in_=xb[:, 0:n]),
                            reads=[("xres", (k * 5 + tt) % 4)], writes=[("dbg", k, tt)], dma=True)
            elif kind == "AT":
                for k in range(16):
                    for tt in range(4):
                        xb = XRES[(k * 4 + tt) % 4]
                        sch.op("dve", lambda e, xb=xb, k=k, tt=tt: e.tensor_copy(
                            out=xb, in_=AT[:, k, tt * 512:(tt + 1) * 512]),
                            reads=[("AT", k, t) for t in range(16)], writes=[("xres", (k * 4 + tt) % 4)])
                        sch.op("sp", lambda e, xb=xb, k=k, tt=tt: e.dma_start(
                            out=dbg_d[k, :, tt * 512:(tt + 1) * 512], in_=xb),
                            reads=[("xres", (k * 4 + tt) % 4)], writes=[("dbg", k, tt)], dma=True)
            elif kind == "xT":
                for k in range(16):
                    for tt in range(4):
                        xb = XRES[(k * 4 + tt) % 4]
                        sch.op("sp", lambda e, xb=xb, k=k, tt=tt: e.dma_start(
                            out=xb, in_=xT_d[k, :, tt * 512:(tt + 1) * 512]),
                            reads=[("xT", k, tt)], writes=[("xres", (k * 4 + tt) % 4)], dma=True)
                        sch.op("sp", lambda e, xb=xb, k=k, tt=tt: e.dma_start(
                            out=dbg_d[k, :, tt * 512:(tt + 1) * 512], in_=xb),
                            reads=[("xres", (k * 4 + tt) % 4)], writes=[("dbg", k, tt)], dma=True)
            sch.barrier()
            sch.op("sp", lambda e: None)
            sch.emit()

        if stop == "A":
            dump("HX")
            return nc

        def gemm(psb, pskey, wt, wkey, K, rhs_fn, rkeys):
            def f(e):
                for k in range(K):
                    ins = e.matmul(psb, lhsT=wt[:, k, :], rhs=rhs_fn(k), start=(k == 0), stop=(k == K - 1))
                return ins
            sch.op("pe", f, reads=wkeys[id(wt)] + rkeys, writes=[pskey])

        scr.reset()
        qT = scr.alloc(BF16, 2048)
        kT = scr.alloc(BF16, 2304)
        Vh = scr.alloc(BF16, 18, 128)
        Et = scr.alloc(BF16, 21, 128)
        PTf = [scr.alloc(F32, 640) for _ in range(2)]
        PTb = [scr.alloc(BF16, 896) for _ in range(2)]
        rZ = [scr.alloc(F32, 128) for _ in range(2)]
        pre = {}

        def load_head(h):
            g = wslot()
            return (load_w(g, 0, e_win[:, h * 128:(h + 1) * 128], 16, 128, 0),
                    load_w(g, 4 * KB, e_win[:, 1024 + h * 128:1024 + (h + 1) * 128], 16, 128, 1),
                    load_w(g, 8 * KB, e_win[:, 2048 + h * 128:2048 + (h + 1) * 128], 16, 128, 2))

        def load_conv(j):
            g = wslot()
            return (load_w(g, 0, e_win[:, 3072 + j * 128:3072 + (j + 1) * 128], 16, 128, 0),
                    load_w(g, 4 * KB, e_win[:, 4096 + j * 128:4096 + (j + 1) * 128], 16, 128, 1),
                    load_w(g, 8 * KB, e_win[:, 5120 + j * 128:5120 + (j + 1) * 128], 16, 128, 2))

        for h in range(8):
            if h == 0:
                pre[("head", 0)] = load_head(0)
            wq, wk, wv = pre.pop(("head", h))
            if h + 1 < 8:
                pre[("head", h + 1)] = load_head(h + 1)
            else:
                pre[("conv", 0)] = load_conv(0)
            sch.op("sp", lambda e, h=h: e.dma_start(out=Et.rearrange("p a b -> p (a b)"), in_=etab_d[h]),
                   reads=[("etab", h)], writes=["Et"], dma=True)
            gi = 0
            for tt in range(4):
                b = gi % 2
                gi += 1
                gemm(bank(b), ("ps", b), wq, None, 16, lambda k, tt=tt: hx_rhs(k, tt * 512, 512),
                     hx_keys(tt * 512, 512))
                sch.op("act", lambda e, b=b, tt=tt: e.activation(out=qT[:, tt * 512:(tt + 1) * 512], in_=bank(b),
                                                                func=AF.Identity, scale=SCALE),
                       reads=[("ps", b)], writes=[("qT", tt)])
                if gi % 2 == 0:
                    ada_step()
            for tt in range(5):
                n = 512 if tt < 4 else 256
                b = gi % 2
                gi += 1
                gemm(bank(b)[:, 0:n], ("ps", b), wk, None, 16, lambda k, tt=tt, n=n: hx_rhs(k, tt * 512, n),
                     hx_keys(tt * 512, n))
                sch.op("dve", lambda e, b=b, tt=tt, n=n: e.tensor_copy(out=kT[:, tt * 512:tt * 512 + n],
                                                                       in_=bank(b)[:, 0:n]),
                       reads=[("ps", b)], writes=[("kT", tt)])
                if gi % 2 == 0:
                    ada_step()
            for vb in range(5):
                nblk = 4 if vb < 4 else 2
                b = gi % 2
                gi += 1

                def vmm(e, vb=vb, nblk=nblk, b=b, wv=wv):
                    for t in range(nblk):
                        blk = vb * 4 + t
                        for k in range(16):
                            ins = e.matmul(bank(b)[:, t * 128:(t + 1) * 128],
                                           lhsT=hx_rhs(k, blk * 128, 128), rhs=wv[:, k, :],
                                           start=(k == 0), stop=(k == 15))
                    return ins
                sch.op("pe", vmm, reads=wkeys[id(wv)] + hx_keys(vb * 512, nblk * 128), writes=[("ps", b)])
                sch.op("act" if vb % 2 == 0 else "dve",
                       (lambda e, vb=vb, nblk=nblk, b=b: e.copy(
                           out=Vh[:, vb * 4:vb * 4 + nblk, :],
                           in_=bank(b)[:, 0:nblk * 128].rearrange("p (a b) -> p a b", a=nblk)))
                       if vb % 2 == 0 else
                       (lambda e, vb=vb, nblk=nblk, b=b: e.tensor_copy(
                           out=Vh[:, vb * 4:vb * 4 + nblk, :],
                           in_=bank(b)[:, 0:nblk * 128].rearrange("p (a b) -> p a b", a=nblk))),
                       reads=[("ps", b)], writes=[("Vh", vb)])
                if vb % 2 == 0:
                    ada_step()
            if stop == "Q" and h == 0:
                for idx_, (srcT, n_) in enumerate([(qT, 2048), (kT, 2304), (Vh.rearrange("p a b -> p (a b)"), 2304)]):
                    for c0_ in range(0, n_, 512):
                        w_ = min(512, n_ - c0_)
                        xb = XRES[(idx_ * 5 + c0_ // 512) % 4]
                        kk_ = (idx_ * 5 + c0_ // 512) % 4
                        sch.op("dve", lambda e, xb=xb, srcT=srcT, c0_=c0_, w_=w_: e.tensor_copy(
                            out=xb[:, 0:w_], in_=srcT[:, c0_:c0_ + w_]),
                            reads=[("qT", t) for t in range(4)] + [("kT", t) for t in range(5)] + [("Vh", t) for t in range(5)],
                            writes=[("xres", kk_)])
                        sch.op("sp", lambda e, xb=xb, idx_=idx_, c0_=c0_, w_=w_: e.dma_start(
                            out=dbg_d[idx_, :, c0_:c0_ + w_], in_=xb[:, 0:w_]),
                            reads=[("xres", kk_)], writes=[("dbg", idx_, c0_)], dma=True)
                sch.barrier()
                sch.op("sp", lambda e: None)
                sch.emit()
                return nc
            def stage1(m, h=h):
                pl = pairs_of(m)
                nl = len(pl)
                tb = tile_base(m)
                pb = m % 2
                psS = PS[1 + pb]
                qs = qT[:, m * 128:(m + 1) * 128]

                def qk(e):
                    for c in range(2):
                        ins = e.matmul(psS[:, c * 128:(c + 1) * 128], lhsT=kT[:, 2048 + c * 128:2048 + (c + 1) * 128],
                                       rhs=qs, start=True, stop=True)
                    for i, p in enumerate(pl):
                        ins = e.matmul(psS[:, 256 + i * 128:256 + (i + 1) * 128],
                                       lhsT=kT[:, p * 128:(p + 1) * 128], rhs=qs, start=True, stop=True)
                    return ins
                sch.op("pe", qk, reads=[("qT", m // 4), ("kT", 4)] + [("kT", p // 4) for p in pl],
                       writes=[("psS", pb)])
                sch.op("act", lambda e: e.activation(out=PTb[pb][:, 0:256], in_=psS[:, 0:256], func=AF.Exp),
                       reads=[("psS", pb)], writes=[("PTb", pb, 0)])
                sch.op("act", lambda e: e.activation(out=PTf[pb][:, 0:nl * 128], in_=psS[:, 256:256 + nl * 128],
                                                     func=AF.Exp),
                       reads=[("psS", pb)], writes=[("PTf", pb)])
                sch.op("dve", lambda e: e.tensor_tensor(
                    out=PTb[pb][:, 256:256 + nl * 128], in0=PTf[pb][:, 0:nl * 128],
                    in1=Et[:, tb:tb + nl, :].rearrange("p a b -> p (a b)"), op=ALU.mult),
                    reads=[("PTf", pb), "Et"], writes=[("PTb", pb, 1)])

            def stage2(m, h=h):
                pl = pairs_of(m)
                pb = m % 2
                psO = bank(6 + pb)

                def pv(e):
                    tiles = [(16 + c, c) for c in range(2)] + [(p, 2 + i) for i, p in enumerate(pl)]
                    for i, (vblk, ti) in enumerate(tiles):
                        ins = e.matmul(psO[:, 0:128], lhsT=Vh[:, vblk, :], rhs=PTb[pb][:, ti * 128:(ti + 1) * 128],
                                       start=(i == 0), stop=(i == len(tiles) - 1))
                    for i, (vblk, ti) in enumerate(tiles):
                        ins = e.matmul(psO[:, 128:256], lhsT=ones, rhs=PTb[pb][:, ti * 128:(ti + 1) * 128],
                                       start=(i == 0), stop=(i == len(tiles) - 1))
                    return ins
                sch.op("pe", pv, reads=[("PTb", pb, 0), ("PTb", pb, 1), "ones", ("Vh", 4)] +
                       [("Vh", p // 4) for p in pl], writes=[("ps", 6 + pb)])
                sch.op("dve", lambda e: e.reciprocal(out=rZ[pb], in_=psO[:, 128:256]),
                       reads=[("ps", 6 + pb)], writes=[("rZ", pb)])
                sch.op("dve", lambda e: e.tensor_tensor(
                    out=AT[:, h, m * 128:(m + 1) * 128], in0=psO[:, 0:128], in1=rZ[pb], op=ALU.mult),
                    reads=[("ps", 6 + pb), ("rZ", pb)], writes=[("AT", h, m)])

            skip = os.environ.get("DBGSKIP", "")
            if not (skip == "attn" and h == 1):
                stage1(0)
                for m in range(16):
                    if m + 1 < 16:
                        stage1(m + 1)
                    if not (skip == "s2" and h == 1):
                        stage2(m)
                    if m % 2 == 1:
                        ada_step()
            if stop in ("C0", "C1") and h == int(stop[1]):
                sch.barrier()
                dump("AT")
                return nc
        sch.barrier()

        scr.reset()
        zb = [scr.alloc(F32, 2050) for _ in range(2)]
        ctmp = [scr.alloc(F32, 512) for _ in range(2)]
        ytmp = [scr.alloc(F32, 512) for _ in range(2)]
        for i in range(2):
            sch.op("dve", lambda e, i=i: e.memset(zb[i][:, 0:1], 0.0), writes=[("z", i, 0)])
            sch.op("dve", lambda e, i=i: e.memset(zb[i][:, 2049:2050], 0.0), writes=[("z", i, 3)])
        gi = 0
        for j in range(8):
            wb, wc, wu = pre.pop(("conv", j))
            if j + 1 < 8:
                pre[("conv", j + 1)] = load_conv(j + 1)
            z = zb[j % 2]
            for tt in range(4):
                b = gi % 4
                gi += 1
                gemm(bank(b), ("ps", b), wc, None, 16, lambda k, tt=tt: hx_rhs(k, tt * 512, 512),
                     hx_keys(tt * 512, 512))
                ct = ctmp[tt % 2]
                sch.op("act", lambda e, b=b, ct=ct: e.copy(out=ct, in_=bank(b)), reads=[("ps", b)],
                       writes=[("ctmp", tt % 2)])
                b2 = gi % 4
                gi += 1
                gemm(bank(b2), ("ps", b2), wu, None, 16, lambda k, tt=tt: hx_rhs(k, tt * 512, 512),
                     hx_keys(tt * 512, 512))
                sch.op("dve", lambda e, b2=b2, ct=ct, z=z, tt=tt: e.tensor_tensor(
                    out=z[:, 1 + tt * 512:1 + (tt + 1) * 512], in0=bank(b2), in1=ct, op=ALU.mult),
                    reads=[("ps", b2), ("ctmp", tt % 2)], writes=[("z", j % 2, 1, tt)])
                ada_step()
            zkeys = [("z", j % 2, 0), ("z", j % 2, 3)] + [("z", j % 2, 1, t) for t in range(4)]
            for tt in range(4):
                b = gi % 4
                gi += 1
                gemm(bank(b), ("ps", b), wb, None, 16, lambda k, tt=tt: hx_rhs(k, tt * 512, 512),
                     hx_keys(tt * 512, 512))
                yt = ytmp[tt % 2]
                c0 = 1 + tt * 512
                sch.op("dve", lambda e, z=z, yt=yt, c0=c0, j=j: e.tensor_scalar(
                    out=yt, in0=z[:, c0:c0 + 512], scalar1=convw[:, j, 1:2], scalar2=None, op0=ALU.mult),
                    reads=zkeys + ["convw"], writes=[("ytmp", tt % 2)])
                sch.op("dve", lambda e, z=z, yt=yt, c0=c0, j=j: e.scalar_tensor_tensor(
                    out=yt, in0=z[:, c0 - 1:c0 + 511], scalar=convw[:, j, 0:1], in1=yt, op0=ALU.mult, op1=ALU.add),
                    reads=zkeys + ["convw", ("ytmp", tt % 2)], writes=[("ytmp", tt % 2)])
                sch.op("dve", lambda e, z=z, yt=yt, c0=c0, j=j: e.scalar_tensor_tensor(
                    out=yt, in0=z[:, c0 + 1:c0 + 513], scalar=convw[:, j, 2:3], in1=yt, op0=ALU.mult, op1=ALU.add),
                    reads=zkeys + ["convw", ("ytmp", tt % 2)], writes=[("ytmp", tt % 2)])
                sch.op("dve", lambda e, b=b, yt=yt, j=j, tt=tt: e.tensor_tensor(
                    out=AT[:, 8 + j, tt * 512:(tt + 1) * 512], in0=bank(b), in1=yt, op=ALU.mult),
                    reads=[("ps", b), ("ytmp", tt % 2)], writes=[("AT", 8 + j, 4 * tt + t) for t in range(4)])
                if tt % 2 == 1:
                    ada_step()
        ada_state["flush"] = True
        while ada_step():
            pass
        sch.barrier()
        if stop == "C":
            dump("AT")
            return nc

        def outproj_phase(w2d, layer, jgate, tok0, ntt, at_fn, at_keys_fn):
            gi = 0
            for d in range(16):
                g = wslot()
                wt = load_w(g, 0, w2d[:, d * 128:(d + 1) * 128], 16, 128, 0)
                for sub in range(ntt):
                    tt = tok0 // 512 + sub
                    b = gi % 4
                    xb = XRES[gi % 4]
                    gi += 1
                    sch.op("sp", lambda e, xb=xb, d=d, tt=tt: e.dma_start(
                        out=xb, in_=xT_d[d, :, tt * 512:(tt + 1) * 512]),
                        reads=[("xT", d, tt)], writes=[("xres", (gi - 1) % 4)], dma=True)
                    gemm(bank(b), ("ps", b), wt, ("W", g, 0), 16, lambda k, sub=sub: at_fn(k, sub),
                         at_keys_fn(sub))
                    sch.op("dve", lambda e, xb=xb, b=b, d=d: e.scalar_tensor_tensor(
                        out=xb, in0=bank(b), scalar=modsT[:, layer, jgate * 16 + d:jgate * 16 + d + 1], in1=xb,
                        op0=ALU.mult, op1=ALU.add),
                        reads=[("ps", b), ("xres", (gi - 1) % 4), ("modsT", layer, jgate)],
                        writes=[("xres", (gi - 1) % 4)])
                    sch.op("sp", lambda e, xb=xb, d=d, tt=tt: e.dma_start(
                        out=xT_d[d, :, tt * 512:(tt + 1) * 512], in_=xb),
                        reads=[("xres", (gi - 1) % 4)], writes=[("xT", d, tt)], dma=True)

        outproj_phase(e_wout, 0, 2, 0, 4, lambda k, sub: AT[:, k, sub * 512:(sub + 1) * 512],
                      lambda sub: [("AT", k, 4 * sub + t) for k in range(16) for t in range(4)])
        sch.barrier()
        if stop == "D":
            dump("xT")
            return nc

        def ffn_phase(layer):
            norm_phase(layer, 1)
            HT = T(OFF_AT, BF16, NF, 1024)
            sg = [T(OFF_HXC + i * 2 * KB, F32, 512) for i in range(2)]
            for half in range(2):
                gi = 0
                for f in range(NF):
                    g = wslot()
                    wg_ = load_w(g, 0, w_gate[layer][:, f * 128:(f + 1) * 128], 16, 128, 0)
                    wu_ = load_w(g, 4 * KB, w_up[layer][:, f * 128:(f + 1) * 128], 16, 128, 1)
                    for sub in range(2):
                        lo = half * 1024 + sub * 512
                        b = gi % 4
                        b2 = (gi + 1) % 4
                        gi += 2
                        gemm(bank(b), ("ps", b), wg_, ("W", g, 0), 16, lambda k, lo=lo: hx_rhs(k, lo, 512),
                             hx_keys(lo, 512))
                        gemm(bank(b2), ("ps", b2), wu_, ("W", g, 1), 16, lambda k, lo=lo: hx_rhs(k, lo, 512),
                             hx_keys(lo, 512))
                        sgt = sg[(gi // 2) % 2]
                        sch.op("act", lambda e, b=b, sgt=sgt: e.activation(out=sgt, in_=bank(b), func=AF.Silu),
                               reads=[("ps", b)], writes=[("sg", (gi // 2) % 2)])
                        sch.op("dve", lambda e, b2=b2, sgt=sgt, f=f, sub=sub: e.tensor_tensor(
                            out=HT[:, f, sub * 512:(sub + 1) * 512], in0=bank(b2), in1=sgt, op=ALU.mult),
                            reads=[("ps", b2), ("sg", (gi // 2) % 2)], writes=[("HT", f, sub)])
                    if layer == 0:
                        ada_step_ffn()
                gi = 0
                for d in range(16):
                    g = wslot()
                    wd_ = load_w(g, 0, w_down[layer][:, d * 128:(d + 1) * 128], NF, 128, 0)
                    for sub in range(2):
                        tt = half * 2 + sub
                        b = 4 + gi % 2
                        xi = gi % 4
                        xb = XRES[xi]
                        gi += 1
                        sch.op("sp", lambda e, xb=xb, d=d, tt=tt: e.dma_start(
                            out=xb, in_=xT_d[d, :, tt * 512:(tt + 1) * 512]),
                            reads=[("xT", d, tt)], writes=[("xres", xi)], dma=True)
                        gemm(bank(b), ("ps", b), wd_, ("W", g, 0), NF,
                             lambda k, sub=sub: HT[:, k, sub * 512:(sub + 1) * 512],
                             [("HT", f, sub) for f in range(NF)])
                        sch.op("dve", lambda e, xb=xb, b=b, d=d: e.scalar_tensor_tensor(
                            out=xb, in0=bank(b), scalar=modsT[:, layer, 5 * 16 + d:5 * 16 + d + 1], in1=xb,
                            op0=ALU.mult, op1=ALU.add),
                            reads=[("ps", b), ("xres", xi), ("modsT", layer, 5)], writes=[("xres", xi)])
                        sch.op("sp", lambda e, xb=xb, d=d, tt=tt: e.dma_start(
                            out=xT_d[d, :, tt * 512:(tt + 1) * 512], in_=xb),
                            reads=[("xres", xi)], writes=[("xT", d, tt)], dma=True)
                    if layer == 0:
                        ada_step_ffn()
            if layer == 0:
                while ada_step_ffn():
                    pass
            sch.barrier()

        ffn_phase(0)
        if stop == "F0":
            dump("xT")
            return nc

        norm_phase(1, 0)
        scr.reset()
        utmp = [scr.alloc(F32, 512) for _ in range(2)]
        mtmp = [scr.alloc(F32, 512) for _ in range(2)]
        Bt = scr.alloc(F32, 16, 128)
        bsbc = scr.alloc(F32, 16, 128)
        junk = scr.alloc(BF16, 256)
        sgwT = scr.alloc(BF16, 16, 128)
        sch.op("pool", lambda e: e.dma_start(out=sgwT, in_=sgwT_d), writes=["sgwT"], dma=True)
        sch.op("sp", lambda e: e.dma_start(out=bsbc.rearrange("p a b -> p (a b)"),
                                           in_=sgb_d.rearrange("a b -> (a b)").partition_broadcast(128)),
               writes=["bsbc"], dma=True)
        for q4 in range(4):
            def rmm(e, q4=q4):
                return e.matmul(bank(q4), lhsT=ones, rhs=sgwT[:, q4 * 4:(q4 + 1) * 4, :].rearrange("p a b -> p (a b)"),
                                start=True, stop=True)
            sch.op("pe", rmm, reads=["ones", "sgwT"], writes=[("ps", q4)])
            for gg in range(4):
                gidx = q4 * 4 + gg
                sch.op("dve", lambda e, q4=q4, gg=gg, gidx=gidx: e.scalar_tensor_tensor(
                    out=Bt[:, gidx, :], in0=bank(q4)[:, gg * 128:(gg + 1) * 128], scalar=vecs[:, 6, gidx:gidx + 1],
                    in1=bsbc[:, gidx, :], op0=ALU.mult, op1=ALU.add),
                    reads=[("ps", q4), "vecs", "bsbc"], writes=[("Bt", gidx)])
        VH = T(OFF_AT, BF16, 8, 2048)
        ATh = T(OFF_AT + 32 * KB, BF16, 16, 1024)
        for half in range(2):
            gi = 0
            for cg in range(8):
                g = wslot()
                wv_ = load_w(g, 0, o_win[:, 2048 + cg * 256:2048 + (cg + 1) * 256], 16, 256, 0)
                for blk in range(8):
                    tok = half * 1024 + blk * 128
                    b = gi % 4
                    gi += 1

                    def vmm(e, tok=tok, b=b, wv_=wv_):
                        for k in range(16):
                            ins = e.matmul(bank(b)[:, 0:256], lhsT=hx_rhs(k, tok, 128), rhs=wv_[:, k, :],
                                           start=(k == 0), stop=(k == 15))
                        return ins
                    sch.op("pe", vmm, reads=wkeys[id(wv_)] + hx_keys(tok, 128), writes=[("ps", b)])
                    sch.op("act", lambda e, b=b, blk=blk, cg=cg: e.activation(
                        out=VH[:, blk, cg * 256:(cg + 1) * 256], in_=bank(b)[:, 0:256], func=AF.Gelu_apprx_tanh,
                        accum_out=st1[:, blk, cg:cg + 1]),
                        reads=[("ps", b)], writes=[("VH", blk, cg), ("st1", blk, cg)])
                    sch.op("act", lambda e, blk=blk, cg=cg: e.activation(
                        out=junk, in_=VH[:, blk, cg * 256:(cg + 1) * 256], func=AF.Square,
                        accum_out=st2[:, blk, cg:cg + 1]),
                        reads=[("VH", blk, cg)], writes=["junk", ("st2", blk, cg)])
            allst = [("st1", b_, c_) for b_ in range(8) for c_ in range(8)] + \
                    [("st2", b_, c_) for b_ in range(8) for c_ in range(8)]
            sch.op("dve", lambda e: e.tensor_reduce(out=lnmu, in_=st1, axis=mybir.AxisListType.X, op=ALU.add),
                   reads=allst, writes=["lnmu"])
            sch.op("dve", lambda e: e.tensor_reduce(out=lnex, in_=st2, axis=mybir.AxisListType.X, op=ALU.add),
                   reads=allst, writes=["lnex"])
            sch.op("dve", lambda e: e.tensor_scalar(out=lnmu, in0=lnmu, scalar1=1.0 / 2048, scalar2=None,
                                                    op0=ALU.mult), reads=["lnmu"], writes=["lnmu"])
            sch.op("dve", lambda e: e.tensor_tensor(out=lnnm, in0=lnmu, in1=lnmu, op=ALU.mult),
                   reads=["lnmu"], writes=["lnnm"])
            sch.op("dve", lambda e: e.scalar_tensor_tensor(out=lnrs, in0=lnex, scalar=1.0 / 2048, in1=lnnm,
                                                           op0=ALU.mult, op1=ALU.subtract),
                   reads=["lnex", "lnnm"], writes=["lnrs"])
            sch.op("act", lambda e: e.activation(out=lnrs, in_=lnrs, func=AF.Sqrt, scale=1.0, bias=epsT),
                   reads=["lnrs", "eps"], writes=["lnrs"])
            sch.op("dve", lambda e: e.reciprocal(out=lnrs, in_=lnrs), reads=["lnrs"], writes=["lnrs"])
            sch.op("dve", lambda e: e.scalar_tensor_tensor(out=lnnm, in0=lnmu, scalar=-1.0, in1=lnrs,
                                                           op0=ALU.mult, op1=ALU.mult),
                   reads=["lnmu", "lnrs"], writes=["lnnm"])
            for blk in range(8):
                sch.op("dve", lambda e, blk=blk: e.tensor_scalar(
                    out=VH[:, blk, :], in0=VH[:, blk, :], scalar1=lnrs[:, blk:blk + 1], scalar2=lnnm[:, blk:blk + 1],
                    op0=ALU.mult, op1=ALU.add),
                    reads=[("VH", blk, c_) for c_ in range(8)] + ["lnrs", "lnnm"],
                    writes=[("VH", blk, c_) for c_ in range(8)])
            gi = 0
            for gch in range(16):
                g = wslot()
                wu_ = load_w(g, 0, o_win[:, gch * 128:(gch + 1) * 128], 16, 128, 0)
                for sub in range(2):
                    lo = half * 1024 + sub * 512
                    b = gi % 2
                    bm = 2 + gi % 2
                    gi += 1
                    gemm(bank(b), ("ps", b), wu_, ("W", g, 0), 16, lambda k, lo=lo: hx_rhs(k, lo, 512),
                         hx_keys(lo, 512))
                    ut = utmp[gi % 2]
                    sch.op("act", lambda e, b=b, ut=ut: e.activation(out=ut, in_=bank(b), func=AF.Gelu_apprx_tanh),
                           reads=[("ps", b)], writes=[("utmp", gi % 2)])

                    def smm(e, sub=sub, gch=gch, bm=bm):
                        for n4 in range(4):
                            blk = sub * 4 + n4
                            ins = e.matmul(bank(bm)[:, n4 * 128:(n4 + 1) * 128],
                                           lhsT=VH[:, blk, gch * 128:(gch + 1) * 128], rhs=sgwT[:, gch, :],
                                           start=True, stop=True)
                        return ins
                    sch.op("pe", smm, reads=["sgwT"] + [("VH", sub * 4 + n4, gch // 2) for n4 in range(4)],
                           writes=[("ps", bm)])
                    mt = mtmp[gi % 2]
                    sch.op("dve", lambda e, bm=bm, mt=mt, gch=gch: e.scalar_tensor_tensor(
                        out=mt.rearrange("p (a b) -> p a b", a=4),
                        in0=bank(bm).rearrange("p (a b) -> p a b", a=4), scalar=vecs[:, 5, gch:gch + 1],
                        in1=Bt[:, gch:gch + 1, :].broadcast_to([128, 4, 128]), op0=ALU.mult, op1=ALU.add),
                        reads=[("ps", bm), "vecs", ("Bt", gch)], writes=[("mtmp", gi % 2)])
                    sch.op("dve", lambda e, mt=mt, ut=ut, gch=gch, sub=sub: e.tensor_tensor(
                        out=ATh[:, gch, sub * 512:(sub + 1) * 512], in0=mt, in1=ut, op=ALU.mult),
                        reads=[("mtmp", gi % 2), ("utmp", gi % 2)], writes=[("ATh", gch, sub)])
            if stop == "G" and half == 0:
                for k in range(16):
                    for sub in range(2):
                        xb = XRES[(k * 2 + sub) % 4]
                        sch.op("dve", lambda e, xb=xb, k=k, sub=sub: e.tensor_copy(
                            out=xb, in_=ATh[:, k, sub * 512:(sub + 1) * 512]),
                            reads=[("ATh", k, sub)], writes=[("xres", (k * 2 + sub) % 4)])
                        sch.op("sp", lambda e, xb=xb, k=k, sub=sub: e.dma_start(
                            out=dbg_d[k, :, sub * 512:(sub + 1) * 512], in_=xb),
                            reads=[("xres", (k * 2 + sub) % 4)], writes=[("dbg", k, sub)], dma=True)
                sch.barrier()
                sch.op("sp", lambda e: None)
                sch.emit()
                return nc
            outproj_phase(o_wout, 1, 2, half * 1024, 2, lambda k, sub: ATh[:, k, sub * 512:(sub + 1) * 512],
                          lambda sub: [("ATh", k, sub) for k in range(16)])
            sch.barrier()
        if stop == "H":
            dump("xT")
            return nc

        ffn_phase(1)

        scr.reset()
        nbF = norm_bufs()
        otile = [scr.alloc(F32, 2048) for _ in range(2)]
        XTf = T(OFF_AT, F32, 16, 512)
        FIN = T(OFF_AT + 32 * KB, F32, 16, 512)
        aco_keys[0] = ["vecs"]
        oi = 0
        for tt in range(4):
            for k in range(16):
                sch.op("sp", lambda e, k=k, tt=tt: e.dma_start(out=XTf[:, k, :], in_=xT_d[k, :, tt * 512:(tt + 1) * 512]),
                       reads=[("xT", k, tt)], writes=[("XT", 0, k)], dma=True)
            norm_tile(XTf, [("XT", 0, k) for k in range(16)], 512, vecs[:, 4, :], None, None, None, nbF,
                      final=True, fin_out=FIN)
            for t in range(4):
                ot = otile[oi % 2]
                for q4 in range(4):
                    b = (oi * 4 + q4) % 4

                    def trb(e, t=t, q4=q4, b=b):
                        for kk in range(4):
                            k = q4 * 4 + kk
                            ins = e.transpose(bank(b)[:, kk * 128:(kk + 1) * 128], FIN[:, k, t * 128:(t + 1) * 128],
                                              ident)
                        return ins
                    sch.op("pe", trb, reads=["ident"] + [("fin", q4 * 4 + kk) for kk in range(4)],
                           writes=[("ps", b)])
                    sch.op("act" if q4 % 2 == 0 else "dve",
                           (lambda e, ot=ot, q4=q4, b=b: e.copy(out=ot[:, q4 * 512:(q4 + 1) * 512], in_=bank(b)))
                           if q4 % 2 == 0 else
                           (lambda e, ot=ot, q4=q4, b=b: e.tensor_copy(out=ot[:, q4 * 512:(q4 + 1) * 512], in_=bank(b))),
                           reads=[("ps", b)], writes=[("ot", oi % 2, q4)])
                r0 = tt * 512 + t * 128
                sch.op("sp", lambda e, ot=ot, r0=r0: e.dma_start(out=out_d[r0:r0 + 128, :], in_=ot),
                       reads=[("ot", oi % 2, q4) for q4 in range(4)], writes=[("out", r0)], dma=True)
                oi += 1
        sch.barrier()
        sch.op("sp", lambda e: None)
        sch.emit()
    return nc


def _vec_pc(v):
    return np.ascontiguousarray(np.asarray(v, np.float32).reshape(16, 128).T)


def _bias_tables(rpb):
    tiles = [(6, 6 + d) for d in range(-2, 3)]
    for m in (0, 1):
        tiles += [(m, p) for p in range(4)]
    for m in (14, 15):
        tiles += [(m, p) for p in range(12, 16)]
    kk = np.arange(128)
    ki, kc = kk // 64, kk % 64
    qi, qc = kk // 64, kk % 64
    bias = np.zeros((128, 8, 21, 128), np.float32)
    mask = np.zeros((128, 21, 128), np.float32)
    for t, (m, p) in enumerate(tiles):
        kr = (2 * p + ki)[:, None]
        r = (2 * m + qi)[None, :]
        rs = np.clip(r - 4, 0, 24)
        okr = (kr >= rs) & (kr < rs + 8)
        cs = np.clip(qc - 8, 0, 48)[None, :]
        okc = (kc[:, None] >= cs) & (kc[:, None] < cs + 16)
        dr = np.clip(kr - r + 7, 0, 14)
        dc = np.clip(kc[:, None] - qc[None, :], -15, 15) + 15
        bias[:, :, t, :] = rpb[:, dr, dc].transpose(1, 0, 2)
        mask[:, t, :] = (okr & okc)
    return bias, mask


def prep_inputs(inputs):
    f = lambda a: np.ascontiguousarray(np.asarray(a, np.float32))
    x = f(inputs["x"])
    c = f(inputs["c"])
    ctx = f(inputs["ctx"])
    c_ctx = f(inputs["c_ctx"])
    ada_b = f(inputs["ada_b"])
    adabT = np.ascontiguousarray(ada_b.reshape(2, 96, 128).transpose(2, 0, 1))
    vecs = np.stack([_vec_pc(inputs["norm_mix_g"][0]), _vec_pc(inputs["norm_mix_g"][1]),
                     _vec_pc(inputs["norm_ffn_g"][0]), _vec_pc(inputs["norm_ffn_g"][1]),
                     _vec_pc(inputs["final_norm_g"]), _vec_pc(inputs["sg_ln_g"][0]),
                     _vec_pc(inputs["sg_ln_b"][0])], axis=1)
    convw = np.ascontiguousarray(f(inputs["conv_w"])[0].reshape(3, 8, 128).transpose(2, 1, 0))
    bias, mask = _bias_tables(f(inputs["rpb"])[0])
    sgwT = np.ascontiguousarray(f(inputs["sg_w"])[0].transpose(2, 0, 1))
    shared = {
        "ada_w": f(inputs["ada_w"]), "adabT": adabT, "vecs": np.ascontiguousarray(vecs), "convw": convw,
        "ffn_w_gate": f(inputs["ffn_w_gate"]), "ffn_w_up": f(inputs["ffn_w_up"]),
        "ffn_w_down": f(inputs["ffn_w_down"]), "even_w_in": f(inputs["even_w_in"])[0],
        "even_w_out": f(inputs["even_w_out"])[0], "biastab": bias, "masktab": mask,
        "odd_w_in": f(inputs["odd_w_in"])[0], "odd_w_out": f(inputs["odd_w_out"])[0],
        "sgwT": sgwT, "sgb": f(inputs["sg_b"])[0], "ident": np.eye(128, dtype=np.float32),
    }
    in_maps = []
    for b in range(x.shape[0]):
        cT = np.stack([_vec_pc(c[b]), _vec_pc(c_ctx)], axis=2)
        m = dict(shared)
        m["x"] = x[b]
        m["ctx"] = ctx[b]
        m["cT"] = np.ascontiguousarray(cT)
        in_maps.append(m)
    return in_maps


_NC_CACHE = {}


def kernel(**inputs):
    in_maps = prep_inputs(inputs)
    if "nc" not in _NC_CACHE:
        _NC_CACHE["nc"] = build()
    nc = _NC_CACHE["nc"]
    res = run_bass_kernel_spmd(nc, in_maps, core_ids=list(range(8)))
    return np.stack([r["out"] for r in res.results], axis=0).astype(np.float32)
```
